# Optimizing a Trainium2 kernel written in Bass

```python
import jax, jax.numpy as jnp
from jax import lax
import numpy as np

D_MODEL = 2048
BATCH = 4
SEQ = 2048
DEPTH = 1
DEC_BATCH = 128
DEC_SEQ = 8
PAST_LEN = 16384
PAGE_SIZE = 128

D_MIX = D_MODEL
D_A = D_MIX // 2
A_HEADS = 8
A_EXPAND = 128
A_VDIM = D_A // A_HEADS
FORGET_DIM = A_HEADS * A_EXPAND
C_CONV = D_MIX - D_A
CONV_W = 31
CHUNK = 64
D_FF = 5632
MEM_LEN = 256
XA_HEADS = 4
XA_HDIM = 128
XA_DIM = XA_HEADS * XA_HDIM
D_IN = 2 * FORGET_DIM + 2 * D_A + 2 * C_CONV
EPS = 1e-6
FFN_RES = 0.5

kernel_name = 'hymba_hgrn2_conformer_macaron_step'


def rmsnorm(x, g):
    xf = x.astype(jnp.float32)
    y = xf * lax.rsqrt(jnp.mean(xf * xf, axis=-1, keepdims=True) + EPS)
    return (y * g.astype(jnp.float32)).astype(x.dtype)


def swiglu_ffn(x, wg, wu, wd):
    return (jax.nn.silu(x @ wg) * (x @ wu)) @ wd


def hgrn2_recurrence(q, k, logf, v, s0):
    N, L, H, E = q.shape
    DV = v.shape[-1]
    C = CHUNK if L % CHUNK == 0 else L
    nc = L // C

    def to_chunks(a):
        return a.reshape(N, nc, C, H, a.shape[-1]).transpose(1, 0, 3, 2, 4)

    causal = jnp.tril(jnp.ones((C, C), dtype=bool))[None, None, :, :, None]

    def step(S, inp):
        qc, kc, fc, vc = inp
        b = jnp.cumsum(fc, axis=2)
        decay = jnp.exp(jnp.where(causal, b[:, :, :, None, :] - b[:, :, None, :, :], -jnp.inf))
        A = jnp.einsum('nhte,nhtse,nhse->nhts', qc, decay, kc)
        o = jnp.einsum('nhts,nhsd->nhtd', A, vc) + jnp.einsum('nhte,nhed->nhtd', qc * jnp.exp(b), S)
        b_end = b[:, :, -1, :]
        k_to_end = kc * jnp.exp(b_end[:, :, None, :] - b)
        S = jnp.exp(b_end)[..., None] * S + jnp.einsum('nhse,nhsd->nhed', k_to_end, vc)
        return S, o

    S, o = lax.scan(step, s0, (to_chunks(q), to_chunks(k), to_chunks(logf), to_chunks(v)))
    o = o.transpose(1, 0, 3, 2, 4).reshape(N, L, H, DV)
    return o, S


def mixer(h, s0, buf0, lb, w_in, gnorm, conv_w, conv_b, ln_g, ln_b, w_out):
    N, L, _ = h.shape
    f32 = jnp.float32
    z = h @ w_in
    idx = [FORGET_DIM, 2 * FORGET_DIM, 2 * FORGET_DIM + D_A, 2 * FORGET_DIM + 2 * D_A,
           2 * FORGET_DIM + 2 * D_A + C_CONV]
    zq, zf, zi, zg, za, zb = jnp.split(z, idx, axis=-1)
    q = jax.nn.silu(zq.astype(f32)).reshape(N, L, A_HEADS, A_EXPAND)
    f = lb + (1.0 - lb) * jax.nn.sigmoid(zf.astype(f32))
    logf = jnp.log(f).reshape(N, L, A_HEADS, A_EXPAND)
    k = (1.0 - f).reshape(N, L, A_HEADS, A_EXPAND)
    v = zi.astype(f32).reshape(N, L, A_HEADS, A_VDIM)
    o, s_new = hgrn2_recurrence(q, k, logf, v, s0.astype(f32))
    o = rmsnorm(o, gnorm) * jax.nn.silu(zg.astype(f32).reshape(N, L, A_HEADS, A_VDIM))
    o_a = o.reshape(N, L, D_A)
    u = za.astype(f32) * jax.nn.sigmoid(zb.astype(f32))
    ucat = jnp.concatenate([buf0.astype(f32), u], axis=1)
    buf_new = ucat[:, ucat.shape[1] - (CONV_W - 1):]
    dw = lax.conv_general_dilated(ucat, conv_w.astype(f32)[:, None, :], (1,), 'VALID',
                                  dimension_numbers=('NWC', 'WIO', 'NWC'),
                                  feature_group_count=C_CONV) + conv_b.astype(f32)
    mu = jnp.mean(dw, axis=-1, keepdims=True)
    var = jnp.mean(jnp.square(dw - mu), axis=-1, keepdims=True)
    ln = (dw - mu) * lax.rsqrt(var + EPS) * ln_g.astype(f32) + ln_b.astype(f32)
    o_b = jax.nn.silu(ln)
    out = jnp.concatenate([o_a, o_b], axis=-1).astype(h.dtype) @ w_out
    return out, s_new.astype(s0.dtype), buf_new.astype(buf0.dtype)


def mem_kv(mem, g, wk, wv):
    N, M, _ = mem.shape
    mn = rmsnorm(mem, g)
    return (mn @ wk).reshape(N, M, XA_HEADS, XA_HDIM), (mn @ wv).reshape(N, M, XA_HEADS, XA_HDIM)


def cross_attn(h, mk, mv, wq, wo):
    N, L, _ = h.shape
    q = (h @ wq).reshape(N, L, XA_HEADS, XA_HDIM)
    s = jnp.einsum('nlhd,nmhd->nhlm', q.astype(jnp.float32), mk.astype(jnp.float32)) * (XA_HDIM ** -0.5)
    p = jax.nn.softmax(s, axis=-1)
    o = jnp.einsum('nhlm,nmhd->nlhd', p, mv.astype(jnp.float32)).reshape(N, L, XA_DIM)
    return o.astype(h.dtype) @ wo


def block(x, mk, mv, s0, buf0, lb, p):
    (n_f1, f1_g, f1_u, f1_d, n_mix, w_in, gnorm, conv_w, conv_b, ln_g, ln_b, w_out,
     n_xa, xa_q, xa_o, n_f2, f2_g, f2_u, f2_d) = p
    x = x + FFN_RES * swiglu_ffn(rmsnorm(x, n_f1), f1_g, f1_u, f1_d)
    m, s_new, buf_new = mixer(rmsnorm(x, n_mix), s0, buf0, lb, w_in, gnorm, conv_w, conv_b, ln_g, ln_b, w_out)
    x = x + m
    x = x + cross_attn(rmsnorm(x, n_xa), mk, mv, xa_q, xa_o)
    x = x + FFN_RES * swiglu_ffn(rmsnorm(x, n_f2), f2_g, f2_u, f2_d)
    return x, s_new, buf_new


def setup_inputs(seed: int = 0) -> dict:
    key = jax.random.key(seed)
    ks = iter(jax.random.split(key, 48))

    def nrm(shape, scale):
        return scale * jax.random.normal(next(ks), shape, jnp.float32)

    def gain(shape):
        return 1.0 + nrm(shape, 0.05)

    return {
        'x_prompt': nrm((BATCH, SEQ, D_MODEL), 1.0),
        'x_sample': nrm((DEC_BATCH, DEC_SEQ, D_MODEL), 1.0),
        'mem_prompt': nrm((BATCH, MEM_LEN, D_MODEL), 1.0),
        'state_hgrn': nrm((DEPTH, DEC_BATCH, A_HEADS, A_EXPAND, A_VDIM), 0.1),
        'state_conv': nrm((DEPTH, DEC_BATCH, CONV_W - 1, C_CONV), 0.5),
        'cache_mem_k': nrm((DEPTH, DEC_BATCH, MEM_LEN, XA_HEADS, XA_HDIM), 1.0),
        'cache_mem_v': nrm((DEPTH, DEC_BATCH, MEM_LEN, XA_HEADS, XA_HDIM), 1.0),
        'norm_ffn1': gain((DEPTH, D_MODEL)),
        'ffn1_w_gate': nrm((DEPTH, D_MODEL, D_FF), D_MODEL ** -0.5),
        'ffn1_w_up': nrm((DEPTH, D_MODEL, D_FF), D_MODEL ** -0.5),
        'ffn1_w_down': nrm((DEPTH, D_FF, D_MODEL), D_FF ** -0.5),
        'norm_mix': gain((DEPTH, D_MODEL)),
        'w_in': nrm((DEPTH, D_MODEL, D_IN), D_MODEL ** -0.5),
        'hgrn_lb': nrm((DEPTH + 1, FORGET_DIM), 0.5),
        'hgrn_gnorm': gain((DEPTH, A_VDIM)),
        'conv_w': nrm((DEPTH, CONV_W, C_CONV), CONV_W ** -0.5),
        'conv_b': nrm((DEPTH, C_CONV), 0.02),
        'conv_ln_g': gain((DEPTH, C_CONV)),
        'conv_ln_b': nrm((DEPTH, C_CONV), 0.02),
        'w_out': nrm((DEPTH, D_MIX, D_MODEL), D_MIX ** -0.5),
        'norm_xattn': gain((DEPTH, D_MODEL)),
        'norm_mem': gain((DEPTH, D_MODEL)),
        'xattn_wq': nrm((DEPTH, D_MODEL, XA_DIM), D_MODEL ** -0.5),
        'xattn_wk': nrm((DEPTH, D_MODEL, XA_DIM), D_MODEL ** -0.5),
        'xattn_wv': nrm((DEPTH, D_MODEL, XA_DIM), D_MODEL ** -0.5),
        'xattn_wo': nrm((DEPTH, XA_DIM, D_MODEL), XA_DIM ** -0.5),
        'norm_ffn2': gain((DEPTH, D_MODEL)),
        'ffn2_w_gate': nrm((DEPTH, D_MODEL, D_FF), D_MODEL ** -0.5),
        'ffn2_w_up': nrm((DEPTH, D_MODEL, D_FF), D_MODEL ** -0.5),
        'ffn2_w_down': nrm((DEPTH, D_FF, D_MODEL), D_FF ** -0.5),
        'norm_final': gain((D_MODEL,)),
    }


def reference(x_prompt, x_sample, mem_prompt, state_hgrn, state_conv, cache_mem_k, cache_mem_v,
              norm_ffn1, ffn1_w_gate, ffn1_w_up, ffn1_w_down,
              norm_mix, w_in, hgrn_lb, hgrn_gnorm, conv_w, conv_b, conv_ln_g, conv_ln_b, w_out,
              norm_xattn, norm_mem, xattn_wq, xattn_wk, xattn_wv, xattn_wo,
              norm_ffn2, ffn2_w_gate, ffn2_w_up, ffn2_w_down, norm_final):
    lb_all = jnp.cumsum(jax.nn.softmax(hgrn_lb.astype(jnp.float32), axis=0), axis=0)
    B = x_prompt.shape[0]
    xp, xs = x_prompt, x_sample
    sh_p, sc_p, mk_p, mv_p, sh_s, sc_s = [], [], [], [], [], []
    for l in range(DEPTH):
        p = (norm_ffn1[l], ffn1_w_gate[l], ffn1_w_up[l], ffn1_w_down[l],
             norm_mix[l], w_in[l], hgrn_gnorm[l], conv_w[l], conv_b[l], conv_ln_g[l], conv_ln_b[l], w_out[l],
             norm_xattn[l], xattn_wq[l], xattn_wo[l],
             norm_ffn2[l], ffn2_w_gate[l], ffn2_w_up[l], ffn2_w_down[l])
        lb = lb_all[l]
        mk, mv = mem_kv(mem_prompt, norm_mem[l], xattn_wk[l], xattn_wv[l])
        s0 = jnp.zeros((B, A_HEADS, A_EXPAND, A_VDIM), state_hgrn.dtype)
        b0 = jnp.zeros((B, CONV_W - 1, C_CONV), state_conv.dtype)
        xp, s_new, b_new = block(xp, mk, mv, s0, b0, lb, p)
        sh_p.append(s_new); sc_p.append(b_new); mk_p.append(mk.astype(cache_mem_k.dtype)); mv_p.append(mv.astype(cache_mem_v.dtype))
        xs, s_new, b_new = block(xs, cache_mem_k[l], cache_mem_v[l], state_hgrn[l], state_conv[l], lb, p)
        sh_s.append(s_new); sc_s.append(b_new)
    y_prompt = rmsnorm(xp, norm_final)
    y_sample = rmsnorm(xs, norm_final)
    return (y_prompt, y_sample, jnp.stack(sh_p), jnp.stack(sc_p), jnp.stack(mk_p), jnp.stack(mv_p),
            jnp.stack(sh_s), jnp.stack(sc_s))
```

```python
import contextlib
import concourse.bass as bass
import concourse.mybir as mybir

F32 = mybir.dt.float32
BF16 = mybir.dt.bfloat16
AF = mybir.ActivationFunctionType
ALU = mybir.AluOpType
AX = mybir.AxisListType


class Tile:
    __slots__ = ("name", "w", "r", "dsem", "dcnt")

    def __init__(self, name):
        self.name = name
        self.w = None
        self.r = {}
        self.dsem = None
        self.dcnt = 0


class Kern:
    ENG = ("pe", "act", "dve", "pool", "sp")

    def __init__(self, nc):
        self.nc = nc
        self.es = contextlib.ExitStack()
        self.eng = {"pe": nc.tensor, "act": nc.scalar, "dve": nc.vector,
                    "pool": nc.gpsimd, "sp": nc.sync}
        self.sem = {e: self.es.enter_context(nc.semaphore("s_" + e)) for e in self.ENG}
        self.cnt = {e: 0 for e in self.ENG}
        self.pending = {e: False for e in self.ENG}
        self.waited = {e: {} for e in self.ENG}
        self.dma_tiles = []
        self.n_ins = 0
        self.n_wait = 0

    def sbuf(self, name, shape, dtype):
        return self.es.enter_context(self.nc.sbuf_tensor(name, list(shape), dtype))

    def psum(self, name, shape, dtype):
        return self.es.enter_context(self.nc.psum_tensor(name, list(shape), dtype))

    def tile(self, name):
        return Tile(name)

    def _dsem(self, t):
        if t.dsem is None:
            t.dsem = self.es.enter_context(self.nc.semaphore("d_" + t.name))
            self.dma_tiles.append(t)
        return t.dsem

    def _wait(self, e, tok):
        key, sem, cnt, src = tok
        if self.waited[e].get(key, 0) >= cnt:
            return
        self.eng[e].wait_ge(sem, cnt)
        self.waited[e][key] = cnt
        self.n_wait += 1

    def _deps(self, e, reads, writes):
        for t in reads:
            if t.w is not None:
                self._wait(e, t.w)
        for t in writes:
            if t.w is not None and t.w[3] != e:
                self._wait(e, t.w)
            for key, tok in t.r.items():
                if tok[3] != e:
                    self._wait(e, tok)

    def op(self, e, fn, reads=(), writes=(), inc=True):
        self._deps(e, reads, writes)
        ins = fn(self.eng[e])
        self.n_ins += 1
        if inc:
            ins.then_inc(self.sem[e], 1)
            self.cnt[e] += 1
            c = self.cnt[e]
        else:
            c = self.cnt[e] + 1
        tok = (e, self.sem[e], c, e)
        for t in reads:
            t.r[e] = tok
        for t in writes:
            t.w = tok
            t.r = {}
        return ins

    def dma(self, q, out, in_, reads=(), writes=(), sem_tile=None, cont=False):
        if not cont:
            self._deps(q, reads, writes)
        st = sem_tile or (writes[0] if writes else reads[0])
        sem = self._dsem(st)
        ins = self.eng[q].dma_start(out=out, in_=in_)
        ins.then_inc(sem, 16)
        st.dcnt += 16
        self.n_ins += 1
        tok = ("d_" + st.name, sem, st.dcnt, None)
        for t in reads:
            t.r[tok[0]] = tok
        for t in writes:
            t.w = tok
            t.r = {}
        return ins

    def barrier(self):
        for e in self.ENG:
            for e2 in self.ENG:
                if e2 != e and self.cnt[e2]:
                    self._wait(e, (e2, self.sem[e2], self.cnt[e2], e2))
            for t in self.dma_tiles:
                if t.dcnt and not t.name.startswith("wb"):
                    self._wait(e, ("d_" + t.name, t.dsem, t.dcnt, None))

    def finish(self):
        for t in self.dma_tiles:
            if t.dcnt:
                self._wait("sp", ("d_" + t.name, t.dsem, t.dcnt, None))
        for e in self.ENG:
            if e != "sp" and self.cnt[e]:
                self._wait("sp", (e, self.sem[e], self.cnt[e], e))

    def close(self):
        self.es.close()


import os
import numpy as np
import concourse.bass as bass
import concourse.mybir as mybir

NT = 1152
TG = 384
NTG = 3
D = 2048
KC = 16
DFF = 5632
NPAN = 11
EPS = 1e-6

C_GF1, C_GMIX, C_GXA, C_GMEM, C_GF2, C_GFIN = 0, 16, 32, 48, 64, 80
C_GN = 96
C_CW = 97
C_CB = 97 + 248
C_LG = C_CB + 8
C_LB = C_LG + 8
C_FLAG = C_LB + 8
NCST = C_FLAG + 1
M_ID, M_TP16, M_TR16, M_TP8, M_TR8 = 0, 128, 256, 384, 512
M_TN16, M_TN8 = 640, 768
M_BI16 = 896
M_BI8 = 904
M_ONES = 920
M_RM16 = 1048
M_RM8 = 1052
NMSK = 1060


def build(stop_after="all", dbg_spec=None):
    nc = bass.Bass("TRN2", target_bir_lowering=False)
    dt = lambda n, s, k="ExternalInput": nc.dram_tensor(n, list(s), F32, kind=k).ap()
    x_d = dt("x", [NT, D]); mem_d = dt("mem", [256, D]); xpre_d = dt("xpre", [NT, D])
    pre_d = nc.dram_tensor("pre_scratch", [128, 1264], F32, kind="Internal").ap()
    sh_d = dt("sh", [16, 8, 128, 128]); sc_d = dt("sc", [16, 30, 1024])
    ck_d = dt("ck", [16, 256, 512]); cv_d = dt("cv", [16, 256, 512])
    f1g = dt("f1g", [D, DFF]); f1u = dt("f1u", [D, DFF]); f1d = dt("f1d", [DFF, D])
    win = dt("win", [D, 6144]); wout = dt("wout", [D, D])
    wq = dt("wq", [D, 512]); wk = dt("wk", [D, 512]); wv = dt("wv", [D, 512]); wo = dt("wo", [512, D])
    f2g = dt("f2g", [D, DFF]); f2u = dt("f2u", [D, DFF]); f2d = dt("f2d", [DFF, D])
    cst_d = dt("cst", [128, NCST]); msk_d = dt("msk", [128, NMSK]); lb_d = dt("lb", [2, 1024])
    y_d = dt("y", [NT, D], "ExternalOutput")
    shp_d = dt("shp", [8, 128, 128], "ExternalOutput")
    scp_d = dt("scp", [30, 1024], "ExternalOutput")
    mk_d = dt("mko", [256, 512], "ExternalOutput"); mv_d = dt("mvo", [256, 512], "ExternalOutput")
    shs_d = dt("shs", [16, 8, 128, 128], "ExternalOutput")
    scs_d = dt("scs", [16, 30, 1024], "ExternalOutput")
    dbg_d = None
    if dbg_spec:
        dbg_d = dt("dbg", [128, dbg_spec], "ExternalOutput")

    K = Kern(nc)
    lp = nc.allow_low_precision("bf16 matmul operands, fp32 accumulate")
    lp.__enter__()
    order = ["load", "ffn1", "mixer", "xattn", "ffn2", "final"]
    enabled = lambda ph: order.index(ph) <= order.index(stop_after) if stop_after in order else True

    xT = K.sbuf("xT", [128, KC, NT], F32)
    t_xT = [[K.tile(f"xT{k}_{g}") for g in range(NTG)] for k in range(KC)]
    NS = 4
    wb = [K.sbuf(f"wb{i}", [128, 8192], BF16) for i in range(NS)]
    t_wb = [K.tile(f"wb{i}") for i in range(NS)]
    cst = K.sbuf("cst_sb", [128, NCST], F32); t_cst = K.tile("cst")
    msk = K.sbuf("msk_sb", [128, NMSK], F32); t_msk = K.tile("msk")
    onesb = K.sbuf("onesb", [128, 128], BF16); t_onesb = K.tile("onesb")
    identb = K.sbuf("identb", [128, 128], BF16); t_identb = K.tile("identb")
    ARENA = 62976
    arena = K.sbuf("arena", [128, ARENA // 4], F32)
    P = [K.psum(f"P{i}", [128, 512], F32) for i in range(8)]
    tP = [K.tile(f"P{i}") for i in range(8)]

    ident = msk[:, M_ID:M_ID + 128]
    onesf = msk[:, M_ONES:M_ONES + 128]

    def xs(k, g):
        return xT[:, k, g * TG:(g + 1) * TG]


    def run_interleaved(gens, width=2):
        gens = list(gens)
        active = []
        while gens or active:
            while len(active) < width and gens:
                active.append(gens.pop(0))
            for g_ in list(active):
                try:
                    next(g_)
                except StopIteration:
                    active.remove(g_)

    class Carver:
        def __init__(self):
            self.off = 0

        def take(self, nbytes):
            o = self.off
            self.off += (nbytes + 3) // 4 * 4
            assert self.off <= ARENA, self.off
            return o

        def f32(self, n):
            o = self.take(n * 4)
            return arena[:, o // 4:o // 4 + n]

        def bf16(self, n):
            o = self.take(n * 2)
            return arena[:, o // 4:o // 4 + (n + 1) // 2].bitcast(BF16)[:, 0:n]

    K.dma("sp", cst[:], cst_d, writes=[t_cst])
    K.dma("sp", msk[:], msk_d, writes=[t_msk])
    K.op("dve", lambda e: e.tensor_copy(onesb[:], onesf), reads=[t_msk], writes=[t_onesb])
    K.op("dve", lambda e: e.tensor_copy(identb[:], ident), reads=[t_msk], writes=[t_identb])

    NCACHE = 20
    wcache_d = nc.dram_tensor("wcache", [NCACHE, 128, 8192], BF16, kind="Internal").ap()
    t_wcache = [K.tile(f"wcache{i}") for i in range(NCACHE)]

    class WS:
        def __init__(self):
            self.plan = []
            self.issued = 0
            self.cache_idx = {}
            self.cache_ready = set()

        def add_col(self, w, c0, key=None):
            self.plan.append(("col", w, c0, key)); return len(self.plan) - 1

        def add_row(self, w, r0, key=None):
            self.plan.append(("row", w, r0, key)); return len(self.plan) - 1

        def add_cached(self, key):
            self.plan.append(("flat", None, None, key)); return len(self.plan) - 1

        def _issue(self, i):
            kind, w, o, key = self.plan[i]
            b = i % NS
            assert kind != "flat" or key in self.cache_ready
            if key is not None and key in self.cache_ready:
                ci = self.cache_idx[key]
                for q in range(4):
                    K.dma("pool", wb[b][:, q * 2048:(q + 1) * 2048], wcache_d[ci][:, q * 2048:(q + 1) * 2048],
                          reads=[t_wcache[ci]], writes=[t_wb[b]], cont=(q > 0))
                return
            if kind == "col":
                src = w.rearrange("(kc p) c -> p kc c", p=128)
                dst = wb[b][:].rearrange("p (kc c) -> p kc c", c=512)
                for q in range(4):
                    K.dma("pool", dst[:, q * 4:(q + 1) * 4, :], src[:, q * 4:(q + 1) * 4, o:o + 512], writes=[t_wb[b]], cont=(q > 0))
            else:
                src = w[o:o + 512, :].rearrange("(j p) c -> p j c", p=128)
                dst = wb[b][:].rearrange("p (j c) -> p j c", c=2048)
                for q in range(4):
                    K.dma("pool", dst[:, q:q + 1, :], src[:, q:q + 1, :], writes=[t_wb[b]], cont=(q > 0))
            if key is not None:
                ci = self.cache_idx.setdefault(key, len(self.cache_idx))
                assert ci < NCACHE
                K.dma("sp", wcache_d[ci], wb[b][:], reads=[t_wb[b]], writes=[t_wcache[ci]])
                self.cache_ready.add(key)

        def get(self, i, upto=None):
            upto = i + 2 if upto is None else upto
            while self.issued < min(len(self.plan), upto + 1):
                self._issue(self.issued); self.issued += 1
            b = i % NS
            kind = self.plan[i][0]
            if kind == "col":
                return wb[b][:].rearrange("p (kc c) -> p kc c", c=512), t_wb[b]
            if kind == "flat":
                return wb[b][:], t_wb[b]
            return wb[b][:].rearrange("p (j c) -> p j c", c=2048), t_wb[b]

    W = WS()
    groups = [(0, 1), (2, 3), (4, 5), (6, 7), (8, 9), (10,)]

    def plan_ffn(g_d, u_d, d_d):
        pl = []
        for grp in groups:
            gu = [(W.add_col(g_d, p * 512), W.add_col(u_d, p * 512)) for p in grp]
            dn = [W.add_row(d_d, p * 512) for p in grp]
            pl.append((gu, dn))
        return pl

    plan_f1_pre = plan_ffn(f1g, f1u, f1d)
    plan_mix_pre = []
    for pp_ in range(2):
        extra = [W.add_col(win, p * 512, key=("win", p)) for p in (8, 10, 9, 11)] if pp_ == 1 else []
        plan_mix_pre.append((extra, [W.add_col(win, p * 512, key=("win", p)) for p in (2, 3, 4, 5)]))
    plan_f1 = plan_ffn(f1g, f1u, f1d)
    plan_mix = []
    PASSES = [(0, 2, False), (256, 2, False), (512, 2, False), (768, 2, False), (1024, 1, True)]
    WIN_ORDER = [8, 10, 9, 11, 0, 1, 6, 7, 2, 3, 4, 5]
    for (tok0_, ntile_, sample_) in PASSES:
        seq_ = [8, 10] + ([] if sample_ else ["dg0", "dg1"]) + [9, 11] + ([] if sample_ else ["dg2", "dg3"]) + [0, 1, 6, 7, 2, 3, 4, 5]
        ids_ = {}
        for p in seq_:
            ids_[p] = W.add_cached(("dg", int(p[2]))) if isinstance(p, str) else W.add_col(win, p * 512, key=("win", p))
        plan_mix.append(([ids_[p] for p in WIN_ORDER], [W.add_col(wout, p * 512, key=("wout", p)) for p in range(4)],
                         [ids_.get(f"dg{q}") for q in range(4)]))
    plan_xa = (W.add_col(wk, 0), W.add_col(wv, 0), W.add_col(wq, 0), W.add_row(wo, 0))
    plan_f2 = plan_ffn(f2g, f2u, f2d)

    dbg_off = [0]

    def dump(ap, reads, n):
        if dbg_d is None:
            return
        o = dbg_off[0]
        K.dma("sp", dbg_d[:, o:o + n], ap, reads=reads)
        dbg_off[0] += n

    lbt = K.sbuf("lbt", [128, 1024], F32); t_lbt = K.tile("lbt")

    def setup_lb():
        cv0 = Carver()
        tmp = cv0.f32(2048); t_tmp = K.tile("lbtmp")
        K.dma("sp", tmp, lb_d.rearrange("a b -> (a b)").partition_broadcast(128), writes=[t_tmp])
        K.op("dve", lambda e: e.tensor_tensor(tmp[:, 0:1024], tmp[:, 0:1024], tmp[:, 1024:2048], ALU.subtract), reads=[t_tmp], writes=[t_tmp])
        K.op("act", lambda e: e.activation(lbt[:], tmp[:, 0:1024], AF.Sigmoid), reads=[t_tmp], writes=[t_lbt])

    setup_lb()

    def setup_dg():
        K.barrier()
        cv0 = Carver()
        stg_ = [cv0.bf16(8192), cv0.bf16(8192)]
        t_stg_ = [K.tile("dgstg0"), K.tile("dgstg1")]
        for q in range(4):
            b = q % 2
            for cc in range(2):
                c = 2 * q + cc
                for k_ in range(31):
                    off = (cc * 31 + k_) * 128
                    K.op("dve", lambda e: e.tensor_scalar(stg_[b][:, off:off + 128], identb[:], cst[:, C_CW + c * 31 + k_:C_CW + c * 31 + k_ + 1], None, ALU.mult),
                         reads=[t_identb, t_cst], writes=[t_stg_[b]])
            ci = W.cache_idx.setdefault(("dg", q), len(W.cache_idx))
            K.dma("sp", wcache_d[ci], stg_[b], reads=[t_stg_[b]], writes=[t_wcache[ci]])
            W.cache_ready.add(("dg", q))

    setup_dg()

    def load_x(src_d, tag):
        K.barrier()
        cv = Carver()
        xst = [cv.f32(2048) for _ in range(4)]
        t_xst = [K.tile(f"xst{tag}{i}") for i in range(4)]
        for t in range(9):
            s = t % 4
            K.dma("sp", xst[s], src_d[t * 128:(t + 1) * 128, :], writes=[t_xst[s]])
            g, off = divmod(t * 128, TG)
            for q in range(4):
                bank = (t * 4 + q) % 4
                for j in range(4):
                    kc = q * 4 + j
                    K.op("pe", lambda e: e.transpose(P[bank][:, j * 128:(j + 1) * 128], xst[s][:, kc * 128:(kc + 1) * 128], ident),
                         reads=[t_xst[s], t_msk], writes=[tP[bank]], inc=(j == 3))
                eng = "act" if q % 2 == 0 else "dve"
                dst = xT[:, q * 4:(q + 1) * 4, t * 128:(t + 1) * 128]
                src = P[bank][:].rearrange("p (a b) -> p a b", b=128)
                if eng == "act":
                    K.op("act", lambda e: e.copy(dst, src), reads=[tP[bank]], writes=[t_xT[q * 4 + j][g] for j in range(4)])
                else:
                    K.op("dve", lambda e: e.tensor_copy(dst, src), reads=[tP[bank]], writes=[t_xT[q * 4 + j][g] for j in range(4)])

    def rmsnorm(gcol0, cvn, ntg=NTG, src=None, dst=None, t_src=None, t_dst=None, width=TG, tag=''):
        src = src or xs
        t_src = t_src or t_xT
        sq = [cvn.bf16(width) for _ in range(2)]
        t_sq = [K.tile(f"sq{i}_{gcol0}{tag}") for i in range(2)]
        rst = cvn.f32(width); t_rst = K.tile(f"rst_{gcol0}{tag}")
        for g in range(ntg):
            for kc in range(KC):
                b = kc % 2
                K.op("act", lambda e: e.activation(sq[b], src(kc, g), AF.Square), reads=[t_src[kc][g]], writes=[t_sq[b]])
                K.op("pe", lambda e: e.matmul(P[6][:, 0:width], onesb[:], sq[b], start=(kc == 0), stop=(kc == KC - 1)),
                     reads=[t_onesb, t_sq[b]], writes=[tP[6]], inc=True)
            K.op("dve", lambda e: e.tensor_scalar(rst, P[6][:, 0:width], 1.0 / D, EPS, ALU.mult, ALU.add), reads=[tP[6]], writes=[t_rst])
            K.op("act", lambda e: e.activation(rst, rst, AF.Sqrt), reads=[t_rst], writes=[t_rst])
            K.op("dve", lambda e: e.reciprocal(rst, rst), reads=[t_rst], writes=[t_rst])
            for kc in range(KC):
                eng = "dve"
                K.op(eng, lambda e: e.scalar_tensor_tensor(dst(kc, g), src(kc, g), cst[:, gcol0 + kc:gcol0 + kc + 1], rst, ALU.mult, ALU.mult),
                     reads=[t_src[kc][g], t_cst, t_rst], writes=[t_dst[kc][g]])

    def xt_tiles(kc, tok0, n):
        return [t_xT[kc][g] for g in range(tok0 // TG, (tok0 + n - 1) // TG + 1)]

    def ffn(plan, gcol0, tag, ntg=NTG, tgw=TG):
        K.barrier()
        cvf = Carver()
        hT = cvf.bf16(KC * NT).rearrange("p (k t) -> p k t", t=NT)
        t_hT = [[K.tile(f"hT{tag}{k}_{g}") for g in range(ntg)] for k in range(KC)]
        hs = lambda k, g: hT[:, k, g * tgw:(g + 1) * tgw]
        xsl = lambda m, g: xT[:, m, g * tgw:(g + 1) * tgw]
        xtl = lambda m, g: xt_tiles(m, g * tgw, tgw)
        if tgw == TG:
            rmsnorm(gcol0, cvf, dst=hs, t_dst=t_hT, tag=tag)
        else:
            t_dummy = [[K.tile(f"xsrc{tag}{k}_{g}") for g in range(ntg)] for k in range(KC)]
            rmsnorm(gcol0, cvf, ntg=ntg, src=xsl, dst=hs, t_src=t_dummy, t_dst=t_hT, width=tgw, tag=tag)
        sg = [cvf.f32(tgw) for g in range(ntg)]
        t_sg = [K.tile(f"sg{tag}_{g}") for g in range(ntg)]
        act = [[cvf.bf16(tgw) for g in range(ntg)] for _ in range(8)]
        t_act = [[K.tile(f"act{tag}{i}_{g}") for g in range(ntg)] for i in range(8)]
        dset = 0
        for gu, dn in plan:
            for pi, (ig, iu) in enumerate(gu):
                wg, twg = W.get(ig)
                wu, twu = W.get(iu)
                for j in range(4):
                    ch = pi * 4 + j
                    for kc in range(KC):
                        for g in range(ntg):
                            K.op("pe", lambda e: e.matmul(P[g][:, 0:tgw], wg[:, kc, j * 128:(j + 1) * 128], hs(kc, g), start=(kc == 0), stop=(kc == KC - 1)),
                                 reads=[twg, t_hT[kc][g]], writes=[tP[g]], inc=(kc == KC - 1))
                    for g in range(ntg):
                        K.op("act", lambda e: e.activation(sg[g], P[g][:, 0:tgw], AF.Silu), reads=[tP[g]], writes=[t_sg[g]])
                    for kc in range(KC):
                        for g in range(ntg):
                            K.op("pe", lambda e: e.matmul(P[3 + g][:, 0:tgw], wu[:, kc, j * 128:(j + 1) * 128], hs(kc, g), start=(kc == 0), stop=(kc == KC - 1)),
                                 reads=[twu, t_hT[kc][g]], writes=[tP[3 + g]], inc=(kc == KC - 1))
                    for g in range(ntg):
                        K.op("dve", lambda e: e.tensor_tensor(act[ch][g], sg[g], P[3 + g][:, 0:tgw], ALU.mult),
                             reads=[t_sg[g], tP[3 + g]], writes=[t_act[ch][g]])
            nk = 4 * len(dn)
            wds = [W.get(i, upto=i + 1 + (0 if (len(dn) == 2 and i == dn[0]) else 1)) for i in dn]
            for m in range(KC):
                base = 3 * (dset % 2); dset += 1
                for k in range(nk):
                    wd, twd = wds[k // 4]
                    for g in range(ntg):
                        K.op("pe", lambda e: e.matmul(P[base + g][:, 0:tgw], wd[:, k % 4, m * 128:(m + 1) * 128], act[k][g], start=(k == 0), stop=(k == nk - 1)),
                             reads=[twd, t_act[k][g]], writes=[tP[base + g]], inc=(k == nk - 1))
                for g in range(ntg):
                    K.op("dve", lambda e: e.scalar_tensor_tensor(xsl(m, g), P[base + g][:, 0:tgw], 0.5, xsl(m, g), ALU.mult, ALU.add),
                         reads=[tP[base + g]] + xtl(m, g), writes=xtl(m, g))

    t_pre = K.tile("pre_scratch")

    def mixer_pre():
        cvP = Carver()
        S0 = cvP.f32(1024); tS = [K.tile(f"Spre{h}") for h in range(8)]
        halo = cvP.f32(8 * 30).rearrange("p (c k) -> p c k", k=30); t_halo = K.tile("halopre")
        base_off = cvP.off
        K.barrier()
        K.op("dve", lambda e: e.memset(S0, 0.0), writes=tS)
        B, NB, G, NTK, ntile = 16, 8, 4, 512, 4
        TRm = msk[:, M_TR16:M_TR16 + 128]
        BIm = msk[:, M_BI16:M_BI16 + NB]
        for pi_ in range(2):
            tok0 = pi_ * 512
            extra_p, st_p = plan_mix_pre[pi_]
            K.barrier()
            cv = Carver(); cv.off = base_off
            tg_ = f"pre{pi_}"
            hTp = cv.bf16(KC * NTK).rearrange("p (k t) -> p k t", t=NTK)
            t_hTp = [[K.tile(f"hTp{tg_}_{kc}")] for kc in range(KC)]
            rmsnorm(C_GMIX, cv, ntg=1, src=lambda kc, g: xT[:, kc, tok0:tok0 + NTK], dst=lambda kc, g: hTp[:, kc, :],
                    t_src=[[xt_tiles(kc, tok0, NTK)[0]] for kc in range(KC)], t_dst=t_hTp, width=NTK, tag=tg_)

            def proj_fm(w, tw, j, bank):
                for kc in range(KC):
                    K.op("pe", lambda e: e.matmul(P[bank][:, 0:128], w[:, kc, j * 128:(j + 1) * 128], hTp[:, kc, NTK - 128:NTK], start=(kc == 0), stop=(kc == KC - 1)),
                         reads=[tw, t_hTp[kc][0]], writes=[tP[bank]], inc=(kc == KC - 1))

            def proj_tm(w, tw, tl, bank):
                for kc in range(KC):
                    K.op("pe", lambda e: e.matmul(P[bank][:, 0:512], hTp[:, kc, tl * 128:(tl + 1) * 128], w[:, kc, :], start=(kc == 0), stop=(kc == KC - 1)),
                         reads=[tw, t_hTp[kc][0]], writes=[tP[bank]], inc=(kc == KC - 1))

            if extra_p:
                zaT = cv.f32(4 * 128).rearrange("p (c t) -> p c t", t=128); t_zaT = K.tile("zaT" + tg_)
                sgt = cv.f32(128); t_sgt = K.tile("sgt" + tg_)
                for half in range(2):
                    wza, twza = W.get(extra_p[2 * half])
                    for j in range(4):
                        proj_fm(wza, twza, j, j % 2)
                        K.op("act", lambda e: e.copy(zaT[:, j, :], P[j % 2][:, 0:128]), reads=[tP[j % 2]], writes=[t_zaT])
                    wzb, twzb = W.get(extra_p[2 * half + 1])
                    for j in range(4):
                        c = half * 4 + j
                        proj_fm(wzb, twzb, j, j % 2)
                        K.op("act", lambda e: e.activation(sgt, P[j % 2][:, 0:128], AF.Sigmoid), reads=[tP[j % 2]], writes=[t_sgt])
                        K.op("dve", lambda e: e.tensor_tensor(halo[:, c, :], zaT[:, j, 98:128], sgt[:, 98:128], ALU.mult),
                             reads=[t_zaT, t_sgt], writes=[t_halo])
            Kh = cv.bf16(ntile * 1024).rearrange("p (l c) -> p l c", c=1024); t_Kh = K.tile("Kh" + tg_)
            Vv = cv.bf16(ntile * 1024).rearrange("p (l c) -> p l c", c=1024); t_V = K.tile("V" + tg_)
            Aa = cv.f32(512); t_A = K.tile("A" + tg_)
            Bf = cv.f32(512); t_B = K.tile("B" + tg_)
            Cc = cv.f32(512); t_C = K.tile("C" + tg_)
            dec = cv.f32(ntile * 8 * NB).rearrange("p (l c) -> p l c", c=8 * NB); t_dec = K.tile("dec" + tg_)
            khm_raw = cv.f32(G * 512); t_Khm = K.tile("Khm" + tg_)
            Khm = khm_raw.bitcast(BF16).rearrange("p (r c) -> p r c", c=1024)
            bufsets = [(Aa, Bf, Cc, [t_A], [t_B], [t_C]),
                       (khm_raw[:, 0:512], khm_raw[:, 512:1024], khm_raw[:, 1024:1536], [t_Khm], [t_Khm], [t_Khm])]

            def zf_unit(ui, half, tl):
                par = ui % 2
                bA, bC = (0, 2) if par == 0 else (1, 3)
                A_, B_, C_, tA_, tB_, tC_ = bufsets[par]
                w, tw = W.get(st_p[half])
                cols = slice(half * 512, half * 512 + 512)
                proj_tm(w, tw, tl, bA)
                yield
                K.op("act", lambda e: e.activation(A_, P[bA][:, 0:512], AF.Sigmoid), reads=[tP[bA]], writes=tA_)
                yield
                K.op("dve", lambda e: e.tensor_tensor(C_, A_, lbt[:, cols], ALU.mult), reads=tA_ + [t_lbt], writes=tC_)
                K.op("dve", lambda e: e.tensor_tensor(A_, A_, C_, ALU.subtract), reads=tA_ + tC_, writes=tA_)
                K.op("dve", lambda e: e.tensor_tensor(A_, A_, lbt[:, cols], ALU.add), reads=tA_ + [t_lbt], writes=tA_)
                yield
                K.op("act", lambda e: e.activation(B_, A_, AF.Ln), reads=tA_, writes=tB_)
                yield
                K.op("dve", lambda e: e.tensor_scalar(A_, A_, -1.0, 1.0, ALU.mult, ALU.add), reads=tA_, writes=tA_)
                K.op("pe", lambda e: e.matmul(P[bC][:, 0:512], TRm, B_, start=True, stop=True), reads=[t_msk] + tB_, writes=[tP[bC]])
                for j in range(4):
                    h = half * 4 + j
                    K.op("pe", lambda e: e.matmul(P[7][:, tl * 8 * NB + h * NB:tl * 8 * NB + (h + 1) * NB], B_[:, j * 128:(j + 1) * 128], BIm, start=True, stop=True),
                         reads=tB_ + [t_msk], writes=[tP[7]], inc=(j == 3))
                yield
                K.op("act", lambda e: e.activation(C_, P[bC][:, 0:512], AF.Exp), reads=[tP[bC]], writes=tC_)
                yield
                K.op("dve", lambda e: e.tensor_tensor(Kh[:, tl, cols], A_, C_, ALU.mult), reads=tA_ + tC_, writes=[t_Kh])
                yield

            run_interleaved([zf_unit(half * ntile + tl, half, tl) for half in range(2) for tl in range(ntile)])
            K.op("act", lambda e: e.activation(dec.rearrange("p l c -> p (l c)"), P[7][:, 0:ntile * 8 * NB], AF.Exp), reads=[tP[7]], writes=[t_dec])
            for half in range(2):
                w, tw = W.get(st_p[2 + half])
                for tl in range(ntile):
                    bank = (half * ntile + tl) % 2
                    proj_tm(w, tw, tl, bank)
                    K.op("act", lambda e: e.copy(Vv[:, tl, half * 512:half * 512 + 512], P[bank][:, 0:512]), reads=[tP[bank]], writes=[t_V])
            for tl in range(ntile):
                for r in range(G):
                    K.op("dve", lambda e: e.tensor_scalar(Khm[:, r, :], Kh[:, tl, :], msk[:, M_RM16 + r:M_RM16 + r + 1], None, ALU.mult),
                         reads=[t_Kh, t_msk], writes=[t_Khm])
                for blk in range(NB):
                    a, r = divmod(blk, G)
                    ub = 4 + 2 * (blk % 2)
                    for h in range(8):
                        K.op("pe", lambda e: e.matmul(P[ub + h // 4][:, (h % 4) * 128:(h % 4 + 1) * 128], Khm[64 * a:64 * a + 64, r, h * 128:(h + 1) * 128],
                                                      Vv[64 * a:64 * a + 64, tl, h * 128:(h + 1) * 128], start=True, stop=True),
                             reads=[t_Khm, t_V], writes=[tP[ub + h // 4]], inc=(h % 4 == 3))
                    for h in range(8):
                        dcol = dec[:, tl, h * NB + blk:h * NB + blk + 1]
                        K.op("dve", lambda e: e.scalar_tensor_tensor(S0[:, h * 128:(h + 1) * 128], S0[:, h * 128:(h + 1) * 128], dcol,
                                                                     P[ub + h // 4][:, (h % 4) * 128:(h % 4 + 1) * 128], ALU.mult, ALU.add),
                             reads=[tS[h], t_dec, tP[ub + h // 4]], writes=[tS[h]])
        K.op("dve", lambda e: e.tensor_scalar(S0, S0, cst[:, C_FLAG:C_FLAG + 1], None, ALU.mult), reads=tS + [t_cst], writes=tS)
        K.op("dve", lambda e: e.tensor_scalar(halo.rearrange("p c k -> p (c k)"), halo.rearrange("p c k -> p (c k)"), cst[:, C_FLAG:C_FLAG + 1], None, ALU.mult),
             reads=[t_halo, t_cst], writes=[t_halo])
        K.dma("sp", pre_d[:, 0:1024], S0, reads=tS, writes=[t_pre])
        K.dma("sp", pre_d[:, 1024:1264], halo.rearrange("p c k -> p (c k)"), reads=[t_halo], writes=[t_pre], cont=True)

    def mixer():
        cvP = Carver()
        S_ = [cvP.f32(1024), None, None, None]
        t_S = [[K.tile(f"S{i}_{h}") for h in range(8)] for i in range(4)]
        NSB = 4
        halo = cvP.f32(8 * 30).rearrange("p (c k) -> p c k", k=30); t_halo = K.tile("halo")
        base_off = cvP.off
        K.barrier()
        K.dma("sp", S_[0], pre_d[:, 0:1024], reads=[t_pre], writes=t_S[0])
        K.dma("sp", halo.rearrange("p c k -> p (c k)"), pre_d[:, 1024:1264], reads=[t_pre], writes=[t_halo])
        for pi_, (tok0, ntile, sample) in enumerate(PASSES):
            NTK = ntile * 128
            B = 8 if sample else 16
            NB = 128 // B
            G = 64 // B
            TPm = msk[:, (M_TP8 if sample else M_TP16):(M_TP8 if sample else M_TP16) + 128]
            TRm = msk[:, (M_TR8 if sample else M_TR16):(M_TR8 if sample else M_TR16) + 128]
            TNm = msk[:, (M_TN8 if sample else M_TN16):(M_TN8 if sample else M_TN16) + 128]
            BIm = msk[:, (M_BI8 if sample else M_BI16):(M_BI8 if sample else M_BI16) + NB]
            RM0 = M_RM8 if sample else M_RM16
            win_p, wout_p, dg_p = plan_mix[pi_]
            K.barrier()
            cv = Carver(); cv.off = base_off
            tg_ = f"m{pi_}"
            hTp = cv.bf16(KC * NTK).rearrange("p (k t) -> p k t", t=NTK)
            t_hTp = [[K.tile(f"hTp{tg_}_{kc}")] for kc in range(KC)]
            rmsnorm(C_GMIX, cv, ntg=1, src=lambda kc, g: xT[:, kc, tok0:tok0 + NTK], dst=lambda kc, g: hTp[:, kc, :],
                    t_src=[[xt_tiles(kc, tok0, NTK)[0]] for kc in range(KC)], t_dst=t_hTp, width=NTK, tag=tg_)
            hT_all = [t_hTp[kc][0] for kc in range(KC)]
            o_bT = cv.bf16(8 * NTK).rearrange("p (c t) -> p c t", t=NTK); t_obT = K.tile("obT" + tg_)
            if sample:
                S_[1] = cv.f32(1024)
            qs = cv.bf16(8 * NTK).rearrange("p (h t) -> p h t", t=NTK); t_qs = K.tile("qs" + tg_)
            o_aT = qs; t_oaT = t_qs
            sgz = cv.bf16(8 * NTK).rearrange("p (h t) -> p h t", t=NTK); t_sgz = K.tile("sgz" + tg_)
            mark = cv.off

            def proj_fm(w, tw, j, bank):
                for kc in range(KC):
                    K.op("pe", lambda e: e.matmul(P[bank][:, 0:NTK], w[:, kc, j * 128:(j + 1) * 128], hTp[:, kc, :], start=(kc == 0), stop=(kc == KC - 1)),
                         reads=[tw, t_hTp[kc][0]], writes=[tP[bank]], inc=(kc == KC - 1))

            def proj_tm(w, tw, tl, bank):
                for kc in range(KC):
                    K.op("pe", lambda e: e.matmul(P[bank][:, 0:512], hTp[:, kc, tl * 128:(tl + 1) * 128], w[:, kc, :], start=(kc == 0), stop=(kc == KC - 1)),
                         reads=[tw, t_hTp[kc][0]], writes=[tP[bank]], inc=(kc == KC - 1))

            zaT = cv.f32(4 * NTK).rearrange("p (c t) -> p c t", t=NTK); t_zaT = K.tile("zaT" + tg_)
            sgt = cv.f32(NTK); t_sgt = K.tile("sgt" + tg_)
            dw = cv.f32(8 * NTK).rearrange("p (c t) -> p c t", t=NTK); t_dw = [K.tile(f"dw{tg_}_{c}") for c in range(8)]
            t_u = [K.tile(f"u{tg_}_{c}") for c in range(8)]
            if not sample:
                ub = cv.bf16(8 * (32 + NTK)).rearrange("p (c t) -> p c t", t=32 + NTK)[:, :, 0:30 + NTK]
                utmp = [cv.f32(NTK), cv.f32(NTK)]; t_utmp = [K.tile("utmp0" + tg_), K.tile("utmp1" + tg_)]
                K.op("dve", lambda e: e.tensor_copy(ub[:, :, 0:30], halo), reads=[t_halo], writes=t_u)
            else:
                uT = cv.f32(8 * 16 * 38).rearrange("p (c j t) -> p c j t", j=16, t=38)
                mark_s = cv.off
                stg2_ = [cv.f32(1024), cv.f32(1024)]; t_stg2_ = [K.tile("cstg0"), K.tile("cstg1")]
                scv = sc_d.rearrange("j r c -> (j r) c")
                for q in range(4):
                    stg = stg2_[q % 2]; t_stg = t_stg2_[q % 2]
                    K.dma("sp", stg[0:120, :], scv[q * 120:(q + 1) * 120, :], writes=[t_stg])
                    for c in range(8):
                        bank = c % 2
                        K.op("pe", lambda e: e.transpose(P[bank][:, 0:120], stg[0:120, c * 128:(c + 1) * 128], ident[0:120, 0:120]),
                             reads=[t_stg, t_msk], writes=[tP[bank]])
                        K.op("act", lambda e: e.copy(uT[:, c, q * 4:(q + 1) * 4, 0:30], P[bank][:, 0:120].rearrange("p (j r) -> p j r", r=30)),
                             reads=[tP[bank]], writes=[t_u[c]])
                K.barrier()
                cv.off = mark_s
            dgcur = {}

            def conv_chunk(c):
                if not sample:
                    if c % 2 == 0:
                        dgcur["v"], dgcur["t"] = W.get(dg_p[c // 2])
                    dgv, tdg = dgcur["v"], dgcur["t"]
                    bank = 2 + c % 2
                    for k_ in range(31):
                        off = ((c % 2) * 31 + k_) * 128
                        K.op("pe", lambda e: e.matmul(P[bank][:, 0:NTK], dgv[:, off:off + 128], ub[:, c, k_:k_ + NTK], start=(k_ == 0), stop=(k_ == 30)),
                             reads=[tdg, t_u[c]], writes=[tP[bank]], inc=(k_ == 30))
                    K.op("act", lambda e: e.activation(dw[:, c, :], P[bank][:, 0:NTK], AF.Identity, bias=cst[:, C_CB + c:C_CB + c + 1]),
                         reads=[tP[bank], t_cst], writes=[t_dw[c]])
                    return

                def uwin(k):
                    return uT[:, c, k:k + NTK] if not sample else uT[:, c, :, k:k + 8]
                dwc = dw[:, c, :] if not sample else dw[:, c, :].rearrange("p (j t) -> p j t", t=8)
                wc = lambda k: cst[:, C_CW + c * 31 + k:C_CW + c * 31 + k + 1]
                K.op("dve", lambda e: e.tensor_scalar(dwc, uwin(0), wc(0), cst[:, C_CB + c:C_CB + c + 1], ALU.mult, ALU.add),
                     reads=[t_u[c], t_cst], writes=[t_dw[c]])
                for k in range(1, 31):
                    K.op("dve", lambda e: e.scalar_tensor_tensor(dwc, uwin(k), wc(k), dwc, ALU.mult, ALU.add),
                         reads=[t_u[c], t_cst, t_dw[c]], writes=[t_dw[c]])
            for half in range(2):
                wza, twza = W.get(win_p[2 * half])
                for j in range(4):
                    proj_fm(wza, twza, j, j % 2)
                    K.op("act", lambda e: e.copy(zaT[:, j, :], P[j % 2][:, 0:NTK]), reads=[tP[j % 2]], writes=[t_zaT])
                wzb, twzb = W.get(win_p[2 * half + 1])
                for j in range(4):
                    c = half * 4 + j
                    proj_fm(wzb, twzb, j, j % 2)
                    K.op("act", lambda e: e.activation(sgt, P[j % 2][:, 0:NTK], AF.Sigmoid), reads=[tP[j % 2]], writes=[t_sgt])
                    if not sample:
                        ut_ = utmp[c % 2]; tut_ = t_utmp[c % 2]
                        K.op("dve", lambda e: e.tensor_tensor(ut_, zaT[:, j, :], sgt, ALU.mult), reads=[t_zaT, t_sgt], writes=[tut_])
                        K.op("act", lambda e: e.copy(ub[:, c, 30:30 + NTK], ut_), reads=[tut_], writes=[t_u[c]])
                        K.op("dve", lambda e: e.tensor_copy(halo[:, c, :], ut_[:, NTK - 30:NTK]), reads=[tut_], writes=[t_halo])
                    else:
                        K.op("dve", lambda e: e.tensor_tensor(uT[:, c, :, 30:38], zaT[:, j, :].rearrange("p (j t) -> p j t", t=8),
                                                              sgt.rearrange("p (j t) -> p j t", t=8), ALU.mult),
                             reads=[t_zaT, t_sgt], writes=[t_u[c]])
                for c_ in range(half * 4, half * 4 + 4):
                    conv_chunk(c_)
            for half in range(2):
                w, tw = W.get(win_p[4 + half])
                for j in range(4):
                    proj_fm(w, tw, j, j % 2)
                    K.op("act", lambda e: e.activation(qs[:, half * 4 + j, :], P[j % 2][:, 0:NTK], AF.Silu), reads=[tP[j % 2]], writes=[t_qs])
            for half in range(2):
                w, tw = W.get(win_p[6 + half])
                for j in range(4):
                    proj_fm(w, tw, j, j % 2)
                    K.op("act", lambda e: e.activation(sgz[:, half * 4 + j, :], P[j % 2][:, 0:NTK], AF.Silu), reads=[tP[j % 2]], writes=[t_sgz])
            if not sample:
                if tok0 + NTK == 1024:
                    cso = cv.f32(1024); t_cso = K.tile("cso")
                    for c in range(8):
                        bank = c // 4
                        K.op("pe", lambda e: e.transpose(P[bank][0:30, (c % 4) * 128:(c % 4 + 1) * 128], halo[:, c, :], ident),
                             reads=[t_halo, t_msk], writes=[tP[bank]])
                    for bank in range(2):
                        K.op("act", lambda e: e.copy(cso[0:30, bank * 512:(bank + 1) * 512], P[bank][0:30, :]), reads=[tP[bank]], writes=[t_cso])
                    K.dma("sp", scp_d, cso[0:30, :], reads=[t_cso])
            else:
                cso = cv.f32(1024); t_cso = K.tile("csos")
                unew = cv.f32(1024).rearrange("p (c t) -> p c t", t=128); t_unew = K.tile("unew")
                K.op("dve", lambda e: e.tensor_copy(unew.rearrange("p c (j t) -> p c j t", t=8), uT[:, :, :, 30:38]), reads=t_u, writes=[t_unew])
                for c in range(8):
                    bank = c // 4
                    K.op("pe", lambda e: e.transpose(P[bank][:, (c % 4) * 128:(c % 4 + 1) * 128], unew[:, c, :], ident),
                         reads=[t_unew, t_msk], writes=[tP[bank]])
                for bank in range(2):
                    K.op("act", lambda e: e.copy(cso[:, bank * 512:(bank + 1) * 512], P[bank][:, :]), reads=[tP[bank]], writes=[t_cso])
                for j in range(16):
                    K.dma("sp", scs_d[j, 22:30, :], cso[8 * j:8 * j + 8, :], reads=[t_cso])
                t_cpy = K.tile("sccopy")
                K.dma("sp", scs_d[:, 0:22, :], sc_d[:, 8:30, :], writes=[t_cpy])
            sqd = cv.f32(NTK); t_sqd = K.tile("sqd" + tg_)
            mu = cv.f32(NTK); t_mu = K.tile("mu" + tg_)
            rs2 = cv.f32(NTK); t_rs2 = K.tile("rs2" + tg_)
            tt = cv.f32(NTK); t_tt = K.tile("tt" + tg_)
            for c in range(8):
                K.op("pe", lambda e: e.matmul(P[6][:, 0:NTK], onesf, dw[:, c, :], start=(c == 0), stop=(c == 7)),
                     reads=[t_msk, t_dw[c]], writes=[tP[6]])
                K.op("act", lambda e: e.activation(sqd, dw[:, c, :], AF.Square), reads=[t_dw[c]], writes=[t_sqd])
                K.op("pe", lambda e: e.matmul(P[7][:, 0:NTK], onesf, sqd, start=(c == 0), stop=(c == 7)),
                     reads=[t_msk, t_sqd], writes=[tP[7]])
            K.op("dve", lambda e: e.tensor_scalar(mu, P[6][:, 0:NTK], 1.0 / 1024, None, ALU.mult), reads=[tP[6]], writes=[t_mu])
            K.op("dve", lambda e: e.tensor_tensor(rs2, mu, mu, ALU.mult), reads=[t_mu], writes=[t_rs2])
            K.op("dve", lambda e: e.scalar_tensor_tensor(rs2, P[7][:, 0:NTK], 1.0 / 1024, rs2, ALU.mult, ALU.subtract), reads=[tP[7], t_rs2], writes=[t_rs2])
            K.op("act", lambda e: e.activation(rs2, rs2, AF.Sqrt, bias=EPS), reads=[t_rs2], writes=[t_rs2])
            K.op("dve", lambda e: e.reciprocal(rs2, rs2), reads=[t_rs2], writes=[t_rs2])
            for c in range(8):
                K.op("dve", lambda e: e.tensor_tensor(tt, dw[:, c, :], mu, ALU.subtract), reads=[t_dw[c], t_mu], writes=[t_tt])
                K.op("dve", lambda e: e.tensor_tensor(tt, tt, rs2, ALU.mult), reads=[t_tt, t_rs2], writes=[t_tt])
                K.op("act", lambda e: e.activation(o_bT[:, c, :], tt, AF.Silu, bias=cst[:, C_LB + c:C_LB + c + 1], scale=cst[:, C_LG + c:C_LG + c + 1]),
                     reads=[t_tt, t_cst], writes=[t_obT])

            K.barrier()
            cv.off = mark
            if sample:
                S_[2] = cv.f32(1024); S_[3] = cv.f32(1024)
            KtT = cv.bf16(8 * NTK).rearrange("p (h t) -> p h t", t=NTK); t_KtT = K.tile("KtT" + tg_)
            QtT = cv.bf16(8 * NTK).rearrange("p (h t) -> p h t", t=NTK); t_QtT = K.tile("QtT" + tg_)
            Kh = cv.bf16(ntile * 1024).rearrange("p (l c) -> p l c", c=1024); t_Kh = K.tile("Kh" + tg_)
            Vv = cv.bf16(ntile * 1024).rearrange("p (l c) -> p l c", c=1024); t_V = K.tile("V" + tg_)
            Aa = cv.f32(512); t_A = K.tile("A" + tg_)
            Bf = cv.f32(512); t_B = K.tile("B" + tg_)
            Cc = cv.f32(512); t_C = K.tile("C" + tg_)
            C2 = Cc; t_C2 = t_C

            dec = cv.f32(ntile * 8 * NB).rearrange("p (l c) -> p l c", c=8 * NB); t_dec = K.tile("dec" + tg_)
            kh_raw = cv.f32(1024); t_Khm = K.tile("Khm" + tg_)
            Khm = kh_raw.bitcast(BF16).rearrange("p (r c) -> p r c", c=1024)
            ATm_flat = cv.bf16(1024); ATm = ATm_flat.rearrange("p (h t) -> p h t", t=128); t_ATm = K.tile("ATm" + tg_); Ktm = ATm_flat[:, 0:512]; t_Ktm = t_ATm
            sbf_raw = cv.f32(512); Sbf = sbf_raw.bitcast(BF16); t_Sbf = [K.tile(f"Sbf{tg_}_{h}") for h in range(8)]
            osq = Sbf
            rsn = kh_raw; t_rsn = t_Khm
            P4b = P[4][:].bitcast(BF16)
            bufsets = [(Aa, Bf, Cc, Ktm, [t_A], [t_B], [t_C], [t_Ktm]),
                       (kh_raw[:, 0:512], kh_raw[:, 512:1024], sbf_raw, ATm_flat[:, 512:1024], [t_Khm], [t_Khm], t_Sbf, [t_ATm])]

            def zf_unit(ui, half, tl):
                par = ui % 2
                bA, bB, bC = (0, 2, 4) if par == 0 else (1, 3, 5)
                A_, B_, C_, Kt_, tA_, tB_, tC_, tK_ = bufsets[par]
                PAb = P[bA][:].bitcast(BF16)
                w, tw = W.get(win_p[8 + half])
                cols = slice(half * 512, half * 512 + 512)
                proj_tm(w, tw, tl, bA)
                yield
                K.op("act", lambda e: e.activation(A_, P[bA][:, 0:512], AF.Sigmoid), reads=[tP[bA]], writes=tA_)
                yield
                K.op("dve", lambda e: e.tensor_tensor(C_, A_, lbt[:, cols], ALU.mult), reads=tA_ + [t_lbt], writes=tC_)
                K.op("dve", lambda e: e.tensor_tensor(A_, A_, C_, ALU.subtract), reads=tA_ + tC_, writes=tA_)
                K.op("dve", lambda e: e.tensor_tensor(A_, A_, lbt[:, cols], ALU.add), reads=tA_ + [t_lbt], writes=tA_)
                yield
                K.op("act", lambda e: e.activation(B_, A_, AF.Ln), reads=tA_, writes=tB_)
                yield
                K.op("dve", lambda e: e.tensor_scalar(A_, A_, -1.0, 1.0, ALU.mult, ALU.add), reads=tA_, writes=tA_)
                K.op("pe", lambda e: e.matmul(P[bB][:, 0:512], TNm, B_, start=True, stop=True), reads=[t_msk] + tB_, writes=[tP[bB]])
                K.op("pe", lambda e: e.matmul(P[bC][:, 0:512], TRm, B_, start=True, stop=True), reads=[t_msk] + tB_, writes=[tP[bC]])
                yield
                K.op("act", lambda e: e.activation(C_, P[bB][:, 0:512], AF.Exp), reads=[tP[bB]], writes=tC_)
                yield
                K.op("dve", lambda e: e.tensor_tensor(Kt_, A_, C_, ALU.mult), reads=tA_ + tC_, writes=tK_)
                yield
                K.op("act", lambda e: e.activation(C_, P[bC][:, 0:512], AF.Exp), reads=[tP[bC]], writes=tC_)
                for j in range(4):
                    K.op("pe", lambda e: e.transpose(PAb[:, j * 128:(j + 1) * 128], Kt_[:, j * 128:(j + 1) * 128], identb[:]),
                         reads=tK_ + [t_identb], writes=[tP[bA]], inc=(j == 3))
                yield
                K.op("dve", lambda e: e.tensor_tensor(Kh[:, tl, cols], A_, C_, ALU.mult), reads=tA_ + tC_, writes=[t_Kh])
                K.op("act", lambda e: e.copy(KtT[:, half * 4:half * 4 + 4, tl * 128:(tl + 1) * 128], PAb[:, 0:512].rearrange("p (h t) -> p h t", t=128)),
                     reads=[tP[bA]], writes=[t_KtT])
                for j in range(4):
                    K.op("pe", lambda e: e.matmul(P[bB][:, j * 128:(j + 1) * 128], B_[:, j * 128:(j + 1) * 128], TPm, start=True, stop=True),
                         reads=tB_ + [t_msk], writes=[tP[bB]], inc=(j == 3))
                yield
                K.op("act", lambda e: e.activation(C_, P[bB][:, 0:512], AF.Exp), reads=[tP[bB]], writes=tC_)
                for j in range(4):
                    h = half * 4 + j
                    K.op("pe", lambda e: e.matmul(P[7][:, tl * 8 * NB + h * NB: tl * 8 * NB + (h + 1) * NB], B_[:, j * 128:(j + 1) * 128], BIm, start=True, stop=True),
                         reads=tB_ + [t_msk], writes=[tP[7]], inc=(j == 3))
                yield
                K.op("dve", lambda e: e.tensor_tensor(QtT[:, half * 4:half * 4 + 4, tl * 128:(tl + 1) * 128], qs[:, half * 4:half * 4 + 4, tl * 128:(tl + 1) * 128],
                                                      C_.rearrange("p (h t) -> p h t", t=128), ALU.mult), reads=[t_qs] + tC_, writes=[t_QtT])
                yield

            run_interleaved([zf_unit(half * ntile + tl, half, tl) for half in range(2) for tl in range(ntile)])
            K.op("act", lambda e: e.activation(dec.rearrange("p l c -> p (l c)"), P[7][:, 0:ntile * 8 * NB], AF.Exp), reads=[tP[7]], writes=[t_dec])
            for half in range(2):
                w, tw = W.get(win_p[10 + half])
                for tl in range(ntile):
                    bank = (half * ntile + tl) % 2
                    proj_tm(w, tw, tl, bank)
                    K.op("act", lambda e: e.copy(Vv[:, tl, half * 512:half * 512 + 512], P[bank][:, 0:512]), reads=[tP[bank]], writes=[t_V])
            for tl in range(ntile):
                tsl = slice(tl * 128, (tl + 1) * 128)
                for h in range(8):
                    K.op("pe", lambda e: e.matmul(P[h // 4][:, (h % 4) * 128:(h % 4 + 1) * 128], KtT[:, h, tsl], QtT[:, h, tsl], start=True, stop=True),
                         reads=[t_KtT, t_QtT], writes=[tP[h // 4]], inc=(h % 4 == 3))
                for h in range(8):
                    K.op("dve", lambda e: e.tensor_tensor(ATm[:, h, :], P[h // 4][:, (h % 4) * 128:(h % 4 + 1) * 128], TPm, ALU.mult),
                         reads=[tP[h // 4], t_msk], writes=[t_ATm])
                for h in range(8):
                    K.op("pe", lambda e: e.matmul(P[2 + h // 4][:, (h % 4) * 128:(h % 4 + 1) * 128], Vv[:, tl, h * 128:(h + 1) * 128], ATm[:, h, :], start=(h % 4 == 0), stop=True),
                         reads=[t_V, t_ATm], writes=[tP[2 + h // 4]], inc=(h % 4 == 3))
                def s_load(b_):
                    K.dma("sp", S_[b_ % NSB].rearrange("p (h d) -> p h d", d=128), sh_d[b_].rearrange("h e d -> e h d"), writes=t_S[b_ % NSB])
                if sample:
                    for b_ in range(NSB):
                        s_load(b_)
                for blk in range(NB):
                    kb_ = blk % 2
                    K.op("dve", lambda e: e.tensor_scalar(Khm[:, kb_, :], Kh[:, tl, :], msk[:, RM0 + blk % G:RM0 + blk % G + 1], None, ALU.mult),
                         reads=[t_Kh, t_msk], writes=[t_Khm])
                    ub = 4 + 2 * (blk % 2)
                    si = blk % NSB if sample else 0
                    Sc = S_[si]; tS = t_S[si]
                    a, r = divmod(blk, G)
                    for h in range(8):
                        K.op("act", lambda e: e.copy(Sbf[:, h * 128:(h + 1) * 128], Sc[:, h * 128:(h + 1) * 128]), reads=[tS[h]], writes=[t_Sbf[h]])
                    for h in range(8):
                        c0 = (h % 4) * 128 + blk * B
                        K.op("pe", lambda e: e.matmul(P[2 + h // 4][:, c0:c0 + B], Sbf[:, h * 128:(h + 1) * 128], QtT[:, h, tl * 128 + blk * B: tl * 128 + (blk + 1) * B],
                                                      start=False, stop=True), reads=[t_Sbf[h], t_QtT], writes=[tP[2 + h // 4]], inc=(h % 4 == 3))
                    for h in range(8):
                        K.op("pe", lambda e: e.matmul(P[ub + h // 4][:, (h % 4) * 128:(h % 4 + 1) * 128], Khm[64 * a:64 * a + 64, kb_, h * 128:(h + 1) * 128],
                                                      Vv[64 * a:64 * a + 64, tl, h * 128:(h + 1) * 128], start=True, stop=True),
                             reads=[t_Khm, t_V], writes=[tP[ub + h // 4]], inc=(h % 4 == 3))
                    for h in range(8):
                        dcol = dec[:, tl, h * NB + blk:h * NB + blk + 1]
                        K.op("dve", lambda e: e.scalar_tensor_tensor(Sc[:, h * 128:(h + 1) * 128], Sc[:, h * 128:(h + 1) * 128], dcol,
                                                                     P[ub + h // 4][:, (h % 4) * 128:(h % 4 + 1) * 128], ALU.mult, ALU.add),
                             reads=[tS[h], t_dec, tP[ub + h // 4]], writes=[tS[h]])
                    if sample:
                        K.dma("sp", shs_d[blk].rearrange("h e d -> e h d"), Sc.rearrange("p (h d) -> p h d", d=128), reads=tS)
                        if blk + NSB < NB:
                            s_load(blk + NSB)
                for hb in range(2):
                    K.op("act", lambda e: e.activation(osq[:, hb * 512:(hb + 1) * 512], P[2 + hb][:, :], AF.Square), reads=[tP[2 + hb]], writes=t_Sbf[hb * 4:hb * 4 + 4])
                for h in range(8):
                    K.op("pe", lambda e: e.matmul(P[6 + h // 4][:, (h % 4) * 128:(h % 4 + 1) * 128], onesb[:], osq[:, h * 128:(h + 1) * 128], start=True, stop=True),
                         reads=[t_onesb, t_Sbf[h]], writes=[tP[6 + h // 4]], inc=(h % 4 == 3))
                for hb in range(2):
                    K.op("dve", lambda e: e.tensor_scalar(rsn[:, hb * 512:(hb + 1) * 512], P[6 + hb][:, :], 1.0 / 128, EPS, ALU.mult, ALU.add), reads=[tP[6 + hb]], writes=[t_rsn])
                K.op("act", lambda e: e.activation(rsn, rsn, AF.Sqrt), reads=[t_rsn], writes=[t_rsn])
                K.op("dve", lambda e: e.reciprocal(rsn, rsn), reads=[t_rsn], writes=[t_rsn])
                for hb in range(2):
                    K.op("dve", lambda e: e.scalar_tensor_tensor(rsn[:, hb * 512:(hb + 1) * 512], P[2 + hb][:, :], cst[:, C_GN:C_GN + 1], rsn[:, hb * 512:(hb + 1) * 512], ALU.mult, ALU.mult),
                         reads=[tP[2 + hb], t_cst, t_rsn], writes=[t_rsn])
                    K.op("dve", lambda e: e.tensor_tensor(o_aT[:, hb * 4:hb * 4 + 4, tsl], rsn[:, hb * 512:(hb + 1) * 512].rearrange("p (h t) -> p h t", t=128),
                                                          sgz[:, hb * 4:hb * 4 + 4, tsl], ALU.mult), reads=[t_rsn, t_sgz], writes=[t_oaT])
            if (not sample) and tok0 + NTK == 1024:
                K.dma("sp", shp_d.rearrange("h e d -> e h d"), S_[0].rearrange("p (h d) -> p h d", d=128), reads=t_S[0])
            if dbg_d is not None and stop_after == "mixer" and pi_ == 0:
                dtmp = cv.f32(1024); t_dtmp = K.tile("dtmpm")
                K.op("dve", lambda e: e.tensor_copy(dtmp, o_aT.rearrange("p c t -> p (c t)")), reads=[t_oaT], writes=[t_dtmp])
                dump(dtmp, [t_dtmp], 1024)
                K.op("dve", lambda e: e.tensor_copy(dtmp, o_bT.rearrange("p c t -> p (c t)")), reads=[t_obT], writes=[t_dtmp])
                dump(dtmp, [t_dtmp], 1024)
            for pp in range(4):
                w, tw = W.get(wout_p[pp])
                for j in range(4):
                    m = pp * 4 + j
                    bank = m % 2
                    for k in range(KC):
                        rhs = o_aT[:, k, :] if k < 8 else o_bT[:, k - 8, :]
                        K.op("pe", lambda e: e.matmul(P[bank][:, 0:NTK], w[:, k, j * 128:(j + 1) * 128], rhs, start=(k == 0), stop=(k == KC - 1)),
                             reads=[tw, t_oaT if k < 8 else t_obT], writes=[tP[bank]], inc=(k == KC - 1))
                    xv = xT[:, m, tok0:tok0 + NTK]
                    K.op("dve", lambda e: e.tensor_tensor(xv, xv, P[bank][:, 0:NTK], ALU.add), reads=[tP[bank]] + xt_tiles(m, tok0, NTK), writes=xt_tiles(m, tok0, NTK))

    if not os.environ.get("SKIP_PRE"):
        load_x(xpre_d, "p")
        ffn(plan_f1_pre, C_GF1, "p", ntg=2, tgw=512)
        mixer_pre()
    load_x(x_d, "m")
    if enabled("ffn1") and not os.environ.get("SKIP_FFN1"):
        ffn(plan_f1, C_GF1, "a")

    if dbg_d is not None and stop_after in ("load", "ffn1"):
        for g in range(NTG):
            dump(xT[:, 0, g * TG:(g + 1) * TG], [t_xT[0][g]], TG)
            dump(xT[:, 15, g * TG:(g + 1) * TG], [t_xT[15][g]], TG)

    if enabled("mixer") and not os.environ.get("SKIP_MIX"):
        mixer()
    if dbg_d is not None and stop_after == "mixer":
        for g in range(NTG):
            dump(xT[:, 0, g * TG:(g + 1) * TG], [t_xT[0][g]], TG)
            dump(xT[:, 15, g * TG:(g + 1) * TG], [t_xT[15][g]], TG)

    def xattn():
        K.barrier()
        cvx = Carver()
        mkT = cvx.bf16(4 * 256).rearrange("p (h m) -> p h m", m=256); t_mkT = K.tile("mkT")
        mvb = cvx.bf16(2 * 512).rearrange("p (c d) -> p c d", d=512); t_mvb = K.tile("mvb")
        qT = cvx.bf16(4 * NT).rearrange("p (h t) -> p h t", t=NT); t_qT = [K.tile(f"qT{g}") for g in range(NTG)]
        oxT = cvx.bf16(4 * NT).rearrange("p (h t) -> p h t", t=NT); t_oxT = [K.tile(f"oxT{g}") for g in range(NTG)]
        base = cvx.off
        wk_i, wv_i, wq_i, wo_i = plan_xa
        mst0 = cvx.f32(2048); mst = [mst0, mst0]; t_mst0 = K.tile("mst0"); t_mst = [t_mst0, t_mst0]
        memT = cvx.f32(KC * 256).rearrange("p (k t) -> p k t", t=256); t_memT = [[K.tile(f"memT{kc}")] for kc in range(KC)]
        mnT = cvx.bf16(KC * 256).rearrange("p (k t) -> p k t", t=256); t_mnT = [[K.tile(f"mnT{kc}")] for kc in range(KC)]
        ostg = [cvx.f32(512) for _ in range(2)]; t_ostg = [K.tile(f"ostg{i}") for i in range(2)]
        for t in range(2):
            K.dma("sp", mst[t], mem_d[t * 128:(t + 1) * 128, :], writes=[t_mst[t]])
            for q in range(4):
                bank = q
                for j in range(4):
                    kc = q * 4 + j
                    K.op("pe", lambda e: e.transpose(P[bank][:, j * 128:(j + 1) * 128], mst[t][:, kc * 128:(kc + 1) * 128], ident),
                         reads=[t_mst[t], t_msk], writes=[tP[bank]], inc=(j == 3))
                K.op("act", lambda e: e.copy(memT[:, q * 4:(q + 1) * 4, t * 128:(t + 1) * 128], P[bank][:].rearrange("p (a b) -> p a b", b=128)),
                     reads=[tP[bank]], writes=[t_memT[q * 4 + j][0] for j in range(4)])
        if os.environ.get('XA_STOP') == 'a1':
            return
        rmsnorm(C_GMEM, cvx, ntg=1, src=lambda kc, g: memT[:, kc, :], dst=lambda kc, g: mnT[:, kc, :], t_src=t_memT, t_dst=t_mnT, width=256, tag="mem")
        if os.environ.get('XA_STOP') == 'a2':
            return
        wkp, twk = W.get(wk_i)
        for h in range(4):
            for kc in range(KC):
                K.op("pe", lambda e: e.matmul(P[h % 2][:, 0:256], wkp[:, kc, h * 128:(h + 1) * 128], mnT[:, kc, :], start=(kc == 0), stop=(kc == KC - 1)),
                     reads=[twk, t_mnT[kc][0]], writes=[tP[h % 2]], inc=(kc == KC - 1))
            K.op("act", lambda e: e.copy(mkT[:, h, :], P[h % 2][:, 0:256]), reads=[tP[h % 2]], writes=[t_mkT])
        if os.environ.get('XA_STOP') == 'a3':
            return
        for t in range(2):
            for kc in range(KC):
                K.op("pe", lambda e: e.matmul(P[2 + t][:, 0:512], mnT[:, kc, t * 128:(t + 1) * 128], wkp[:, kc, :], start=(kc == 0), stop=(kc == KC - 1)),
                     reads=[twk, t_mnT[kc][0]], writes=[tP[2 + t]], inc=(kc == KC - 1))
            K.op("act", lambda e: e.copy(ostg[t], P[2 + t][:, 0:512]), reads=[tP[2 + t]], writes=[t_ostg[t]])
            K.dma("sp", mk_d[t * 128:(t + 1) * 128, :], ostg[t], reads=[t_ostg[t]])
        if os.environ.get('XA_STOP') == 'a4':
            return
        wvp, twv = W.get(wv_i)
        for t in range(2):
            for kc in range(KC):
                K.op("pe", lambda e: e.matmul(P[4 + t][:, 0:512], mnT[:, kc, t * 128:(t + 1) * 128], wvp[:, kc, :], start=(kc == 0), stop=(kc == KC - 1)),
                     reads=[twv, t_mnT[kc][0]], writes=[tP[4 + t]], inc=(kc == KC - 1))
            K.op("act", lambda e: e.copy(ostg[t], P[4 + t][:, 0:512]), reads=[tP[4 + t]], writes=[t_ostg[t]])
            K.op("dve", lambda e: e.tensor_copy(mvb[:, t, :], ostg[t]), reads=[t_ostg[t]], writes=[t_mvb])
            K.dma("sp", mv_d[t * 128:(t + 1) * 128, :], ostg[t], reads=[t_ostg[t]])
        if os.environ.get('XA_STOP') == 'a':
            return
        K.barrier()
        cvx.off = base
        hT = cvx.bf16(KC * NT).rearrange("p (k t) -> p k t", t=NT)
        t_hT = [[K.tile(f"hTx{k}_{g}") for g in range(NTG)] for k in range(KC)]
        hs = lambda k, g: hT[:, k, g * TG:(g + 1) * TG]
        rmsnorm(C_GXA, cvx, dst=hs, t_dst=t_hT, tag="xa")
        wqp, twq = W.get(wq_i)
        for h in range(4):
            bs = 3 * (h % 2)
            for kc in range(KC):
                for g in range(NTG):
                    K.op("pe", lambda e: e.matmul(P[bs + g][:, 0:TG], wqp[:, kc, h * 128:(h + 1) * 128], hs(kc, g), start=(kc == 0), stop=(kc == KC - 1)),
                         reads=[twq, t_hT[kc][g]], writes=[tP[bs + g]], inc=(kc == KC - 1))
            for g in range(NTG):
                K.op("act", lambda e: e.mul(qT[:, h, g * TG:(g + 1) * TG], P[bs + g][:, 0:TG], 128.0 ** -0.5), reads=[tP[bs + g]], writes=[t_qT[g]])
        if os.environ.get('XA_STOP') == 'b':
            return
        K.barrier()
        cvx.off = base
        pf = cvx.f32(1024).rearrange("p (h m) -> p h m", m=256); t_pf = K.tile("pf")
        pn = cvx.bf16(1024); t_pn = K.tile("pn")
        pT = cvx.bf16(1024).rearrange("p (a t) -> p a t", t=128); t_pT = K.tile("pT")
        mx = cvx.f32(4); t_mx = K.tile("mx")
        rsum = cvx.f32(4); t_rsum = K.tile("rsum")
        kst2 = [cvx.f32(1024).rearrange("p (c d) -> p c d", d=512) for _ in range(4)]; t_kst2 = [K.tile(f"kst{i}") for i in range(4)]
        KjT = cvx.bf16(1024).rearrange("p (h m) -> p h m", m=256); t_KjT = K.tile("KjT")
        qm = [cvx.bf16(512).rearrange("p (h t) -> p h t", t=128) for _ in range(2)]; t_qm = [K.tile(f"qm{i}") for i in range(2)]
        Vjb = cvx.bf16(1024).rearrange("p (c d) -> p c d", d=512); t_Vjb = K.tile("Vjb")
        P4b = P[4][:].bitcast(BF16)
        for t in range(9):
            sample = (t == 8) and not os.environ.get('XA_NOSAMPLE')
            g = (t * 128) // TG
            tsl = slice(t * 128, (t + 1) * 128)
            if not sample:
                for h in range(4):
                    K.op("pe", lambda e: e.matmul(P[h // 2][:, (h % 2) * 256:(h % 2 + 1) * 256], qT[:, h, tsl], mkT[:, h, :], start=True, stop=True),
                         reads=[t_qT[g], t_mkT], writes=[tP[h // 2]])
            else:
                for j in range(16):
                    kst = kst2[j % 4]; t_kst = t_kst2[j % 4]
                    K.dma("sp", kst, ck_d[j].rearrange("(c p) d -> p c d", p=128), writes=[t_kst])
                    for h in range(4):
                        for mc in range(2):
                            K.op("pe", lambda e: e.transpose(P[2 + h // 2][:, ((h % 2) * 2 + mc) * 128:((h % 2) * 2 + mc + 1) * 128], kst[:, mc, h * 128:(h + 1) * 128], ident),
                                 reads=[t_kst, t_msk], writes=[tP[2 + h // 2]])
                    for hb in range(2):
                        K.op("act", lambda e: e.copy(KjT[:, hb * 2:hb * 2 + 2, :], P[2 + hb][:].rearrange("p (h m) -> p h m", m=256)), reads=[tP[2 + hb]], writes=[t_KjT])
                    qb = j % 2
                    K.op("dve", lambda e: e.memset(qm[qb], 0.0), writes=[t_qm[qb]])
                    K.op("dve", lambda e: e.tensor_copy(qm[qb][:, :, 8 * j:8 * j + 8], qT[:, :, 1024 + 8 * j:1024 + 8 * j + 8]), reads=[t_qT[2]], writes=[t_qm[qb]])
                    for h in range(4):
                        K.op("pe", lambda e: e.matmul(P[h // 2][:, (h % 2) * 256:(h % 2 + 1) * 256], qm[qb][:, h, :], KjT[:, h, :], start=(j == 0 and h % 2 == 0), stop=(j == 15)),
                             reads=[t_qm[qb], t_KjT], writes=[tP[h // 2]])
            for hb in range(2):
                K.op("dve", lambda e: e.reduce_max(mx[:, hb * 2:hb * 2 + 2], P[hb][:].rearrange("p (h m) -> p h m", m=256), AX.X), reads=[tP[hb]], writes=[t_mx])
            K.op("dve", lambda e: e.tensor_scalar(mx, mx, -1.0, None, ALU.mult), reads=[t_mx], writes=[t_mx])
            for h in range(4):
                K.op("act", lambda e: e.activation(pf[:, h, :], P[h // 2][:, (h % 2) * 256:(h % 2 + 1) * 256], AF.Exp, bias=mx[:, h:h + 1], scale=1.0, accum_out=rsum[:, h:h + 1]),
                     reads=[tP[h // 2], t_mx], writes=[t_pf, t_rsum])
            K.op("dve", lambda e: e.reciprocal(rsum, rsum), reads=[t_rsum], writes=[t_rsum])
            for h in range(4):
                K.op("dve", lambda e: e.tensor_scalar(pn[:, h * 256:(h + 1) * 256], pf[:, h, :], rsum[:, h:h + 1], None, ALU.mult), reads=[t_pf, t_rsum], writes=[t_pn])
            for a in range(8):
                K.op("pe", lambda e: e.transpose(P4b[:, a * 128:(a + 1) * 128], pn[:, a * 128:(a + 1) * 128], identb[:]), reads=[t_pn, t_identb], writes=[tP[4]])
            K.op("act", lambda e: e.copy(pT, P4b[:, 0:1024].rearrange("p (a t) -> p a t", t=128)), reads=[tP[4]], writes=[t_pT])
            if not sample:
                for h in range(4):
                    for mc in range(2):
                        K.op("pe", lambda e: e.matmul(P[5][:, h * 128:(h + 1) * 128], mvb[:, mc, h * 128:(h + 1) * 128], pT[:, h * 2 + mc, :], start=(h == 0 and mc == 0), stop=True),
                             reads=[t_mvb, t_pT], writes=[tP[5]])
            else:
                for j in range(16):
                    kst = kst2[j % 4]; t_kst = t_kst2[j % 4]
                    K.dma("sp", kst, cv_d[j].rearrange("(c p) d -> p c d", p=128), writes=[t_kst])
                    K.op("dve", lambda e: e.tensor_copy(Vjb, kst), reads=[t_kst], writes=[t_Vjb])
                    for h in range(4):
                        for mc in range(2):
                            K.op("pe", lambda e: e.matmul(P[5][:, h * 128 + 8 * j:h * 128 + 8 * j + 8], Vjb[:, mc, h * 128:(h + 1) * 128], pT[:, h * 2 + mc, 8 * j:8 * j + 8],
                                                          start=(j == 0 and h == 0 and mc == 0), stop=True), reads=[t_Vjb, t_pT], writes=[tP[5]])
            K.op("act", lambda e: e.copy(oxT[:, :, tsl], P[5][:].rearrange("p (h t) -> p h t", t=128)), reads=[tP[5]], writes=[t_oxT[g]])
        if os.environ.get('XA_STOP') == 'c':
            return
        wop, two = W.get(wo_i)
        for m in range(KC):
            bs = 3 * (m % 2)
            for k in range(4):
                for g in range(NTG):
                    K.op("pe", lambda e: e.matmul(P[bs + g][:, 0:TG], wop[:, k, m * 128:(m + 1) * 128], oxT[:, k, g * TG:(g + 1) * TG], start=(k == 0), stop=(k == 3)),
                         reads=[two, t_oxT[g]], writes=[tP[bs + g]], inc=(k == 3))
            for g in range(NTG):
                K.op("dve", lambda e: e.tensor_tensor(xs(m, g), xs(m, g), P[bs + g][:, 0:TG], ALU.add), reads=[tP[bs + g], t_xT[m][g]], writes=[t_xT[m][g]])

    if enabled("xattn"):
        xattn()
    if dbg_d is not None and stop_after == "xattn":
        for g in range(NTG):
            dump(xT[:, 0, g * TG:(g + 1) * TG], [t_xT[0][g]], TG)
            dump(xT[:, 15, g * TG:(g + 1) * TG], [t_xT[15][g]], TG)

    if enabled("ffn2"):
        ffn(plan_f2, C_GF2, "b")
    if enabled("final"):
        K.barrier()
        cvz = Carver()
        rmsnorm(C_GFIN, cvz, dst=xs, t_dst=t_xT, tag="fin")
        ystg = [cvz.f32(2048) for _ in range(2)]
        t_ystg = [K.tile(f"ystg{i}") for i in range(2)]
        for t in range(9):
            s = t % 2
            g = (t * 128) // TG
            for q in range(4):
                bank = q
                for j in range(4):
                    kc = q * 4 + j
                    K.op("pe", lambda e: e.transpose(P[bank][:, j * 128:(j + 1) * 128], xT[:, kc, t * 128:(t + 1) * 128], ident),
                         reads=[t_xT[kc][g], t_msk], writes=[tP[bank]], inc=(j == 3))
                if q % 2 == 0:
                    K.op("act", lambda e: e.copy(ystg[s][:, q * 512:(q + 1) * 512], P[bank][:]), reads=[tP[bank]], writes=[t_ystg[s]])
                else:
                    K.op("dve", lambda e: e.tensor_copy(ystg[s][:, q * 512:(q + 1) * 512], P[bank][:]), reads=[tP[bank]], writes=[t_ystg[s]])
            K.dma("sp", y_d[t * 128:(t + 1) * 128, :], ystg[s], reads=[t_ystg[s]])

    K.finish()
    lp.__exit__(None, None, None)
    K.close()
    print("instructions", K.n_ins, "waits", K.n_wait, "panels", len(W.plan))
    return nc


def make_masks():
    m = np.zeros((128, NMSK), np.float32)
    idx = np.arange(128)
    m[:, M_ID:M_ID + 128] = np.eye(128, dtype=np.float32)
    for B, otp, otr, otn in ((16, M_TP16, M_TR16, M_TN16), (8, M_TP8, M_TR8, M_TN8)):
        same = (idx[:, None] // B) == (idx[None, :] // B)
        tp = (same & (idx[:, None] <= idx[None, :])).astype(np.float32)
        tr = (same & (idx[:, None] > idx[None, :])).astype(np.float32)
        m[:, otp:otp + 128] = tp
        m[:, otr:otr + 128] = tr
        m[:, otn:otn + 128] = -tp
    m[:, M_BI16:M_BI16 + 8] = (idx[:, None] // 16 == np.arange(8)[None, :])
    m[:, M_BI8:M_BI8 + 16] = (idx[:, None] // 8 == np.arange(16)[None, :])
    m[:, M_ONES:M_ONES + 128] = 1.0
    m[:, M_RM16:M_RM16 + 4] = ((idx[:, None] % 64) // 16 == np.arange(4)[None, :])
    m[:, M_RM8:M_RM8 + 8] = ((idx[:, None] % 64) // 8 == np.arange(8)[None, :])
    return m


def fm(v):
    return np.ascontiguousarray(v.reshape(16, 128).T)


def prep_core(inp, c, masks):
    seq, half = c // 2, c % 2
    d = {}
    xp = inp["x_prompt"][seq, half * 1024:(half + 1) * 1024]
    xsm = inp["x_sample"][c * 16:(c + 1) * 16].reshape(128, D)
    d["x"] = np.ascontiguousarray(np.concatenate([xp, xsm], 0))
    d["mem"] = np.ascontiguousarray(inp["mem_prompt"][seq])
    d["xpre"] = np.ascontiguousarray(np.concatenate([inp["x_prompt"][seq, 0:1024], np.zeros((128, D), np.float32)], 0))
    d["sh"] = np.ascontiguousarray(inp["state_hgrn"][0, c * 16:(c + 1) * 16])
    d["sc"] = np.ascontiguousarray(inp["state_conv"][0, c * 16:(c + 1) * 16])
    d["ck"] = np.ascontiguousarray(inp["cache_mem_k"][0, c * 16:(c + 1) * 16].reshape(16, 256, 512))
    d["cv"] = np.ascontiguousarray(inp["cache_mem_v"][0, c * 16:(c + 1) * 16].reshape(16, 256, 512))
    for k, n in (("f1g", "ffn1_w_gate"), ("f1u", "ffn1_w_up"), ("f1d", "ffn1_w_down"), ("win", "w_in"), ("wout", "w_out"),
                 ("wq", "xattn_wq"), ("wk", "xattn_wk"), ("wv", "xattn_wv"), ("wo", "xattn_wo"),
                 ("f2g", "ffn2_w_gate"), ("f2u", "ffn2_w_up"), ("f2d", "ffn2_w_down")):
        d[k] = inp[n][0]
    cst = np.zeros((128, NCST), np.float32)
    cst[:, C_GF1:C_GF1 + 16] = fm(inp["norm_ffn1"][0])
    cst[:, C_GMIX:C_GMIX + 16] = fm(inp["norm_mix"][0])
    cst[:, C_GXA:C_GXA + 16] = fm(inp["norm_xattn"][0])
    cst[:, C_GMEM:C_GMEM + 16] = fm(inp["norm_mem"][0])
    cst[:, C_GF2:C_GF2 + 16] = fm(inp["norm_ffn2"][0])
    cst[:, C_GFIN:C_GFIN + 16] = fm(inp["norm_final"])
    cst[:, C_GN] = inp["hgrn_gnorm"][0]
    cw = inp["conv_w"][0]
    cst[:, C_CW:C_CW + 248] = cw.reshape(31, 8, 128).transpose(2, 1, 0).reshape(128, 248)
    cst[:, C_CB:C_CB + 8] = inp["conv_b"][0].reshape(8, 128).T
    cst[:, C_LG:C_LG + 8] = inp["conv_ln_g"][0].reshape(8, 128).T
    cst[:, C_LB:C_LB + 8] = inp["conv_ln_b"][0].reshape(8, 128).T
    cst[:, C_FLAG] = float(half)
    d["cst"] = cst
    d["msk"] = masks
    d["lb"] = np.ascontiguousarray(inp["hgrn_lb"])
    return d


from concourse.bass_utils import run_bass_kernel_spmd

_NC = None


def kernel(**inp):
    global _NC
    inp = {k: np.asarray(v) for k, v in inp.items()}
    masks = make_masks()
    in_maps = [prep_core(inp, c, masks) for c in range(8)]
    nc = build(stop_after="all")
    res = run_bass_kernel_spmd(nc, in_maps, core_ids=list(range(8)))
    R_ = res.results
    y_prompt = np.zeros((4, 2048, 2048), np.float32)
    y_sample = np.zeros((128, 8, 2048), np.float32)
    shp = np.zeros((1, 4, 8, 128, 128), np.float32)
    scp = np.zeros((1, 4, 30, 1024), np.float32)
    mk = np.zeros((1, 4, 256, 4, 128), np.float32)
    mv = np.zeros((1, 4, 256, 4, 128), np.float32)
    shs = np.zeros((1, 128, 8, 128, 128), np.float32)
    scs = np.zeros((1, 128, 30, 1024), np.float32)
    for c in range(8):
        r = R_[c]
        seq, half = c // 2, c % 2
        y_prompt[seq, half * 1024:(half + 1) * 1024] = r["y"][:1024]
        y_sample[c * 16:(c + 1) * 16] = r["y"][1024:].reshape(16, 8, 2048)
        shs[0, c * 16:(c + 1) * 16] = r["shs"]
        scs[0, c * 16:(c + 1) * 16] = r["scs"]
        if half == 1:
            shp[0, seq] = r["shp"]
            scp[0, seq] = r["scp"]
        else:
            mk[0, seq] = r["mko"].reshape(256, 4, 128)
            mv[0, seq] = r["mvo"].reshape(256, 4, 128)
    return (y_prompt, y_sample, shp, scp, mk, mv, shs, scs)
```

```python
import contextlib
import concourse.bass as bass
import concourse.mybir as mybir

F32 = mybir.dt.float32
BF16 = mybir.dt.bfloat16
AF = mybir.ActivationFunctionType
ALU = mybir.AluOpType
AX = mybir.AxisListType


class Tile:
    __slots__ = ("name", "w", "r", "dsem", "dcnt")

    def __init__(self, name):
        self.name = name
        self.w = None
        self.r = {}
        self.dsem = None
        self.dcnt = 0


class Kern:
    ENG = ("pe", "act", "dve", "pool", "sp")

    def __init__(self, nc):
        self.nc = nc
        self.es = contextlib.ExitStack()
        self.eng = {"pe": nc.tensor, "act": nc.scalar, "dve": nc.vector,
                    "pool": nc.gpsimd, "sp": nc.sync}
        self.sem = {e: self.es.enter_context(nc.semaphore("s_" + e)) for e in self.ENG}
        self.cnt = {e: 0 for e in self.ENG}
        self.pending = {e: False for e in self.ENG}
        self.waited = {e: {} for e in self.ENG}
        self.dma_tiles = []
        self.n_ins = 0
        self.n_wait = 0

    def sbuf(self, name, shape, dtype):
        return self.es.enter_context(self.nc.sbuf_tensor(name, list(shape), dtype))

    def psum(self, name, shape, dtype):
        return self.es.enter_context(self.nc.psum_tensor(name, list(shape), dtype))

    def tile(self, name):
        return Tile(name)

    def _dsem(self, t):
        if t.dsem is None:
            t.dsem = self.es.enter_context(self.nc.semaphore("d_" + t.name))
            self.dma_tiles.append(t)
        return t.dsem

    def _wait(self, e, tok):
        key, sem, cnt, src = tok
        if self.waited[e].get(key, 0) >= cnt:
            return
        self.eng[e].wait_ge(sem, cnt)
        self.waited[e][key] = cnt
        self.n_wait += 1

    def _deps(self, e, reads, writes):
        for t in reads:
            if t.w is not None:
                self._wait(e, t.w)
        for t in writes:
            if t.w is not None and t.w[3] != e:
                self._wait(e, t.w)
            for key, tok in t.r.items():
                if tok[3] != e:
                    self._wait(e, tok)

    def op(self, e, fn, reads=(), writes=(), inc=True):
        self._deps(e, reads, writes)
        ins = fn(self.eng[e])
        self.n_ins += 1
        if inc:
            ins.then_inc(self.sem[e], 1)
            self.cnt[e] += 1
            c = self.cnt[e]
        else:
            c = self.cnt[e] + 1
        tok = (e, self.sem[e], c, e)
        for t in reads:
            t.r[e] = tok
        for t in writes:
            t.w = tok
            t.r = {}
        return ins

    def dma(self, q, out, in_, reads=(), writes=(), sem_tile=None, cont=False):
        if not cont:
            self._deps(q, reads, writes)
        st = sem_tile or (writes[0] if writes else reads[0])
        sem = self._dsem(st)
        ins = self.eng[q].dma_start(out=out, in_=in_)
        ins.then_inc(sem, 16)
        st.dcnt += 16
        self.n_ins += 1
        tok = ("d_" + st.name, sem, st.dcnt, None)
        for t in reads:
            t.r[tok[0]] = tok
        for t in writes:
            t.w = tok
            t.r = {}
        return ins

    def barrier(self):
        for e in self.ENG:
            for e2 in self.ENG:
                if e2 != e and self.cnt[e2]:
                    self._wait(e, (e2, self.sem[e2], self.cnt[e2], e2))
            for t in self.dma_tiles:
                if t.dcnt and not t.name.startswith("wb"):
                    self._wait(e, ("d_" + t.name, t.dsem, t.dcnt, None))

    def finish(self):
        for t in self.dma_tiles:
            if t.dcnt:
                self._wait("sp", ("d_" + t.name, t.dsem, t.dcnt, None))
        for e in self.ENG:
            if e != "sp" and self.cnt[e]:
                self._wait("sp", (e, self.sem[e], self.cnt[e], e))

    def close(self):
        self.es.close()


import os
import numpy as np
import concourse.bass as bass
import concourse.mybir as mybir

NT = 1152
TG = 384
NTG = 3
D = 2048
KC = 16
DFF = 5632
NPAN = 11
EPS = 1e-6

C_GF1, C_GMIX, C_GXA, C_GMEM, C_GF2, C_GFIN = 0, 16, 32, 48, 64, 80
C_GN = 96
C_CW = 97
C_CB = 97 + 248
C_LG = C_CB + 8
C_LB = C_LG + 8
C_FLAG = C_LB + 8
NCST = C_FLAG + 1
M_ID, M_TP16, M_TR16, M_TP8, M_TR8 = 0, 128, 256, 384, 512
M_TN16, M_TN8 = 640, 768
M_BI16 = 896
M_BI8 = 904
M_ONES = 920
M_RM16 = 1048
M_RM8 = 1052
NMSK = 1060


def build(stop_after="all", dbg_spec=None):
    nc = bass.Bass("TRN2", target_bir_lowering=False)
    dt = lambda n, s, k="ExternalInput": nc.dram_tensor(n, list(s), F32, kind=k).ap()
    x_d = dt("x", [NT, D]); mem_d = dt("mem", [256, D]); xpre_d = dt("xpre", [NT, D])
    pre_d = nc.dram_tensor("pre_scratch", [128, 1264], F32, kind="Internal").ap()
    sh_d = dt("sh", [16, 8, 128, 128]); sc_d = dt("sc", [16, 30, 1024])
    ck_d = dt("ck", [16, 256, 512]); cv_d = dt("cv", [16, 256, 512])
    f1g = dt("f1g", [D, DFF]); f1u = dt("f1u", [D, DFF]); f1d = dt("f1d", [DFF, D])
    win = dt("win", [D, 6144]); wout = dt("wout", [D, D])
    wq = dt("wq", [D, 512]); wk = dt("wk", [D, 512]); wv = dt("wv", [D, 512]); wo = dt("wo", [512, D])
    f2g = dt("f2g", [D, DFF]); f2u = dt("f2u", [D, DFF]); f2d = dt("f2d", [DFF, D])
    cst_d = dt("cst", [128, NCST]); msk_d = dt("msk", [128, NMSK]); lb_d = dt("lb", [2, 1024])
    y_d = dt("y", [NT, D], "ExternalOutput")
    shp_d = dt("shp", [8, 128, 128], "ExternalOutput")
    scp_d = dt("scp", [30, 1024], "ExternalOutput")
    mk_d = dt("mko", [256, 512], "ExternalOutput"); mv_d = dt("mvo", [256, 512], "ExternalOutput")
    shs_d = dt("shs", [16, 8, 128, 128], "ExternalOutput")
    scs_d = dt("scs", [16, 30, 1024], "ExternalOutput")
    dbg_d = None
    if dbg_spec:
        dbg_d = dt("dbg", [128, dbg_spec], "ExternalOutput")

    K = Kern(nc)
    lp = nc.allow_low_precision("bf16 matmul operands, fp32 accumulate")
    lp.__enter__()
    order = ["load", "ffn1", "mixer", "xattn", "ffn2", "final"]
    enabled = lambda ph: order.index(ph) <= order.index(stop_after) if stop_after in order else True

    xT = K.sbuf("xT", [128, KC, NT], F32)
    t_xT = [[K.tile(f"xT{k}_{g}") for g in range(NTG)] for k in range(KC)]
    NS = 4
    wb = [K.sbuf(f"wb{i}", [128, 8192], BF16) for i in range(NS)]
    t_wb = [K.tile(f"wb{i}") for i in range(NS)]
    cst = K.sbuf("cst_sb", [128, NCST], F32); t_cst = K.tile("cst")
    msk = K.sbuf("msk_sb", [128, NMSK], F32); t_msk = K.tile("msk")
    onesb = K.sbuf("onesb", [128, 128], BF16); t_onesb = K.tile("onesb")
    identb = K.sbuf("identb", [128, 128], BF16); t_identb = K.tile("identb")
    ARENA = 62976
    arena = K.sbuf("arena", [128, ARENA // 4], F32)
    P = [K.psum(f"P{i}", [128, 512], F32) for i in range(8)]
    tP = [K.tile(f"P{i}") for i in range(8)]

    ident = msk[:, M_ID:M_ID + 128]
    onesf = msk[:, M_ONES:M_ONES + 128]

    def xs(k, g):
        return xT[:, k, g * TG:(g + 1) * TG]


    def run_interleaved(gens, width=2):
        gens = list(gens)
        active = []
        while gens or active:
            while len(active) < width and gens:
                active.append(gens.pop(0))
            for g_ in list(active):
                try:
                    next(g_)
                except StopIteration:
                    active.remove(g_)

    class Carver:
        def __init__(self):
            self.off = 0

        def take(self, nbytes):
            o = self.off
            self.off += (nbytes + 3) // 4 * 4
            assert self.off <= ARENA, self.off
            return o

        def f32(self, n):
            o = self.take(n * 4)
            return arena[:, o // 4:o // 4 + n]

        def bf16(self, n):
            o = self.take(n * 2)
            return arena[:, o // 4:o // 4 + (n + 1) // 2].bitcast(BF16)[:, 0:n]

    K.dma("sp", cst[:], cst_d, writes=[t_cst])
    K.dma("sp", msk[:], msk_d, writes=[t_msk])
    K.op("dve", lambda e: e.tensor_copy(onesb[:], onesf), reads=[t_msk], writes=[t_onesb])
    K.op("dve", lambda e: e.tensor_copy(identb[:], ident), reads=[t_msk], writes=[t_identb])

    NCACHE = 20
    wcache_d = nc.dram_tensor("wcache", [NCACHE, 128, 8192], BF16, kind="Internal").ap()
    t_wcache = [K.tile(f"wcache{i}") for i in range(NCACHE)]

    class WS:
        def __init__(self):
            self.plan = []
            self.issued = 0
            self.cache_idx = {}
            self.cache_ready = set()

        def add_col(self, w, c0, key=None):
            self.plan.append(("col", w, c0, key)); return len(self.plan) - 1

        def add_row(self, w, r0, key=None):
            self.plan.append(("row", w, r0, key)); return len(self.plan) - 1

        def add_cached(self, key):
            self.plan.append(("flat", None, None, key)); return len(self.plan) - 1

        def _issue(self, i):
            kind, w, o, key = self.plan[i]
            b = i % NS
            assert kind != "flat" or key in self.cache_ready
            if key is not None and key in self.cache_ready:
                ci = self.cache_idx[key]
                for q in range(4):
                    K.dma("pool", wb[b][:, q * 2048:(q + 1) * 2048], wcache_d[ci][:, q * 2048:(q + 1) * 2048],
                          reads=[t_wcache[ci]], writes=[t_wb[b]], cont=(q > 0))
                return
            if kind == "col":
                src = w.rearrange("(kc p) c -> p kc c", p=128)
                dst = wb[b][:].rearrange("p (kc c) -> p kc c", c=512)
                for q in range(4):
                    K.dma("pool", dst[:, q * 4:(q + 1) * 4, :], src[:, q * 4:(q + 1) * 4, o:o + 512], writes=[t_wb[b]], cont=(q > 0))
            else:
                src = w[o:o + 512, :].rearrange("(j p) c -> p j c", p=128)
                dst = wb[b][:].rearrange("p (j c) -> p j c", c=2048)
                for q in range(4):
                    K.dma("pool", dst[:, q:q + 1, :], src[:, q:q + 1, :], writes=[t_wb[b]], cont=(q > 0))
            if key is not None:
                ci = self.cache_idx.setdefault(key, len(self.cache_idx))
                assert ci < NCACHE
                K.dma("sp", wcache_d[ci], wb[b][:], reads=[t_wb[b]], writes=[t_wcache[ci]])
                self.cache_ready.add(key)

        def get(self, i, upto=None):
            upto = i + 2 if upto is None else upto
            while self.issued < min(len(self.plan), upto + 1):
                self._issue(self.issued); self.issued += 1
            b = i % NS
            kind = self.plan[i][0]
            if kind == "col":
                return wb[b][:].rearrange("p (kc c) -> p kc c", c=512), t_wb[b]
            if kind == "flat":
                return wb[b][:], t_wb[b]
            return wb[b][:].rearrange("p (j c) -> p j c", c=2048), t_wb[b]

    W = WS()
    groups = [(0, 1), (2, 3), (4, 5), (6, 7), (8, 9), (10,)]

    def plan_ffn(g_d, u_d, d_d):
        pl = []
        for grp in groups:
            gu = [(W.add_col(g_d, p * 512), W.add_col(u_d, p * 512)) for p in grp]
            dn = [W.add_row(d_d, p * 512) for p in grp]
            pl.append((gu, dn))
        return pl

    plan_f1_pre = plan_ffn(f1g, f1u, f1d)
    plan_mix_pre = []
    for pp_ in range(4):
        extra = [W.add_col(win, p * 512, key=("win", p)) for p in (8, 10, 9, 11)] if pp_ == 3 else []
        plan_mix_pre.append((extra, [W.add_col(win, p * 512, key=("win", p)) for p in (2, 3, 4, 5)]))
    plan_f1 = plan_ffn(f1g, f1u, f1d)
    plan_mix = []
    PASSES = [(0, 2, False), (256, 2, False), (512, 2, False), (768, 2, False), (1024, 1, True)]
    WIN_ORDER = [8, 10, 9, 11, 0, 1, 6, 7, 2, 3, 4, 5]
    for (tok0_, ntile_, sample_) in PASSES:
        seq_ = [8, 10] + ([] if sample_ else ["dg0", "dg1"]) + [9, 11] + ([] if sample_ else ["dg2", "dg3"]) + [0, 1, 6, 7, 2, 3, 4, 5]
        ids_ = {}
        for p in seq_:
            ids_[p] = W.add_cached(("dg", int(p[2]))) if isinstance(p, str) else W.add_col(win, p * 512, key=("win", p))
        plan_mix.append(([ids_[p] for p in WIN_ORDER], [W.add_col(wout, p * 512, key=("wout", p)) for p in range(4)],
                         [ids_.get(f"dg{q}") for q in range(4)]))
    plan_xa = (W.add_col(wk, 0), W.add_col(wv, 0), W.add_col(wq, 0), W.add_row(wo, 0))
    plan_f2 = plan_ffn(f2g, f2u, f2d)

    dbg_off = [0]

    def dump(ap, reads, n):
        if dbg_d is None:
            return
        o = dbg_off[0]
        K.dma("sp", dbg_d[:, o:o + n], ap, reads=reads)
        dbg_off[0] += n

    lbt = K.sbuf("lbt", [128, 1024], F32); t_lbt = K.tile("lbt")

    def setup_lb():
        cv0 = Carver()
        tmp = cv0.f32(2048); t_tmp = K.tile("lbtmp")
        K.dma("sp", tmp, lb_d.rearrange("a b -> (a b)").partition_broadcast(128), writes=[t_tmp])
        K.op("dve", lambda e: e.tensor_tensor(tmp[:, 0:1024], tmp[:, 0:1024], tmp[:, 1024:2048], ALU.subtract), reads=[t_tmp], writes=[t_tmp])
        K.op("act", lambda e: e.activation(lbt[:], tmp[:, 0:1024], AF.Sigmoid), reads=[t_tmp], writes=[t_lbt])

    setup_lb()

    def setup_dg():
        K.barrier()
        cv0 = Carver()
        stg_ = [cv0.bf16(8192), cv0.bf16(8192)]
        t_stg_ = [K.tile("dgstg0"), K.tile("dgstg1")]
        for q in range(4):
            b = q % 2
            for cc in range(2):
                c = 2 * q + cc
                for k_ in range(31):
                    off = (cc * 31 + k_) * 128
                    K.op("dve", lambda e: e.tensor_scalar(stg_[b][:, off:off + 128], identb[:], cst[:, C_CW + c * 31 + k_:C_CW + c * 31 + k_ + 1], None, ALU.mult),
                         reads=[t_identb, t_cst], writes=[t_stg_[b]])
            ci = W.cache_idx.setdefault(("dg", q), len(W.cache_idx))
            K.dma("sp", wcache_d[ci], stg_[b], reads=[t_stg_[b]], writes=[t_wcache[ci]])
            W.cache_ready.add(("dg", q))

    setup_dg()

    def load_x(src_d, tag):
        K.barrier()
        cv = Carver()
        xst = [cv.f32(2048) for _ in range(4)]
        t_xst = [K.tile(f"xst{tag}{i}") for i in range(4)]
        for t in range(9):
            s = t % 4
            K.dma("sp", xst[s], src_d[t * 128:(t + 1) * 128, :], writes=[t_xst[s]])
            g, off = divmod(t * 128, TG)
            for q in range(4):
                bank = (t * 4 + q) % 4
                for j in range(4):
                    kc = q * 4 + j
                    K.op("pe", lambda e: e.transpose(P[bank][:, j * 128:(j + 1) * 128], xst[s][:, kc * 128:(kc + 1) * 128], ident),
                         reads=[t_xst[s], t_msk], writes=[tP[bank]], inc=(j == 3))
                eng = "act" if q % 2 == 0 else "dve"
                dst = xT[:, q * 4:(q + 1) * 4, t * 128:(t + 1) * 128]
                src = P[bank][:].rearrange("p (a b) -> p a b", b=128)
                if eng == "act":
                    K.op("act", lambda e: e.copy(dst, src), reads=[tP[bank]], writes=[t_xT[q * 4 + j][g] for j in range(4)])
                else:
                    K.op("dve", lambda e: e.tensor_copy(dst, src), reads=[tP[bank]], writes=[t_xT[q * 4 + j][g] for j in range(4)])

    def rmsnorm(gcol0, cvn, ntg=NTG, src=None, dst=None, t_src=None, t_dst=None, width=TG, tag=''):
        src = src or xs
        t_src = t_src or t_xT
        sq = [cvn.bf16(width) for _ in range(2)]
        t_sq = [K.tile(f"sq{i}_{gcol0}{tag}") for i in range(2)]
        rst = cvn.f32(width); t_rst = K.tile(f"rst_{gcol0}{tag}")
        for g in range(ntg):
            for kc in range(KC):
                b = kc % 2
                K.op("act", lambda e: e.activation(sq[b], src(kc, g), AF.Square), reads=[t_src[kc][g]], writes=[t_sq[b]])
                K.op("pe", lambda e: e.matmul(P[6][:, 0:width], onesb[:], sq[b], start=(kc == 0), stop=(kc == KC - 1)),
                     reads=[t_onesb, t_sq[b]], writes=[tP[6]], inc=True)
            K.op("dve", lambda e: e.tensor_scalar(rst, P[6][:, 0:width], 1.0 / D, EPS, ALU.mult, ALU.add), reads=[tP[6]], writes=[t_rst])
            K.op("act", lambda e: e.activation(rst, rst, AF.Sqrt), reads=[t_rst], writes=[t_rst])
            K.op("dve", lambda e: e.reciprocal(rst, rst), reads=[t_rst], writes=[t_rst])
            for kc in range(KC):
                eng = "dve"
                K.op(eng, lambda e: e.scalar_tensor_tensor(dst(kc, g), src(kc, g), cst[:, gcol0 + kc:gcol0 + kc + 1], rst, ALU.mult, ALU.mult),
                     reads=[t_src[kc][g], t_cst, t_rst], writes=[t_dst[kc][g]])

    def xt_tiles(kc, tok0, n):
        return [t_xT[kc][g] for g in range(tok0 // TG, (tok0 + n - 1) // TG + 1)]

    def ffn(plan, gcol0, tag, ntg=NTG, tgw=TG):
        K.barrier()
        cvf = Carver()
        hT = cvf.bf16(KC * NT).rearrange("p (k t) -> p k t", t=NT)
        t_hT = [[K.tile(f"hT{tag}{k}_{g}") for g in range(ntg)] for k in range(KC)]
        hs = lambda k, g: hT[:, k, g * tgw:(g + 1) * tgw]
        xsl = lambda m, g: xT[:, m, g * tgw:(g + 1) * tgw]
        xtl = lambda m, g: xt_tiles(m, g * tgw, tgw)
        if tgw == TG:
            rmsnorm(gcol0, cvf, dst=hs, t_dst=t_hT, tag=tag)
        else:
            t_dummy = [[K.tile(f"xsrc{tag}{k}_{g}") for g in range(ntg)] for k in range(KC)]
            rmsnorm(gcol0, cvf, ntg=ntg, src=xsl, dst=hs, t_src=t_dummy, t_dst=t_hT, width=tgw, tag=tag)
        sg = [cvf.f32(tgw) for g in range(ntg)]
        t_sg = [K.tile(f"sg{tag}_{g}") for g in range(ntg)]
        act = [[cvf.bf16(tgw) for g in range(ntg)] for _ in range(8)]
        t_act = [[K.tile(f"act{tag}{i}_{g}") for g in range(ntg)] for i in range(8)]
        dset = 0
        for gu, dn in plan:
            for pi, (ig, iu) in enumerate(gu):
                wg, twg = W.get(ig)
                wu, twu = W.get(iu)
                for j in range(4):
                    ch = pi * 4 + j
                    for kc in range(KC):
                        for g in range(ntg):
                            K.op("pe", lambda e: e.matmul(P[g][:, 0:tgw], wg[:, kc, j * 128:(j + 1) * 128], hs(kc, g), start=(kc == 0), stop=(kc == KC - 1)),
                                 reads=[twg, t_hT[kc][g]], writes=[tP[g]], inc=(kc == KC - 1))
                    for g in range(ntg):
                        K.op("act", lambda e: e.activation(sg[g], P[g][:, 0:tgw], AF.Silu), reads=[tP[g]], writes=[t_sg[g]])
                    for kc in range(KC):
                        for g in range(ntg):
                            K.op("pe", lambda e: e.matmul(P[3 + g][:, 0:tgw], wu[:, kc, j * 128:(j + 1) * 128], hs(kc, g), start=(kc == 0), stop=(kc == KC - 1)),
                                 reads=[twu, t_hT[kc][g]], writes=[tP[3 + g]], inc=(kc == KC - 1))
                    for g in range(ntg):
                        K.op("dve", lambda e: e.tensor_tensor(act[ch][g], sg[g], P[3 + g][:, 0:tgw], ALU.mult),
                             reads=[t_sg[g], tP[3 + g]], writes=[t_act[ch][g]])
            nk = 4 * len(dn)
            wds = [W.get(i, upto=i + 1 + (0 if (len(dn) == 2 and i == dn[0]) else 1)) for i in dn]
            for m in range(KC):
                base = 3 * (dset % 2); dset += 1
                for k in range(nk):
                    wd, twd = wds[k // 4]
                    for g in range(ntg):
                        K.op("pe", lambda e: e.matmul(P[base + g][:, 0:tgw], wd[:, k % 4, m * 128:(m + 1) * 128], act[k][g], start=(k == 0), stop=(k == nk - 1)),
                             reads=[twd, t_act[k][g]], writes=[tP[base + g]], inc=(k == nk - 1))
                for g in range(ntg):
                    K.op("dve", lambda e: e.scalar_tensor_tensor(xsl(m, g), P[base + g][:, 0:tgw], 0.5, xsl(m, g), ALU.mult, ALU.add),
                         reads=[tP[base + g]] + xtl(m, g), writes=xtl(m, g))

    t_pre = K.tile("pre_scratch")

    def mixer_pre():
        cvP = Carver()
        S0 = cvP.f32(1024); tS = [K.tile(f"Spre{h}") for h in range(8)]
        halo = cvP.f32(8 * 30).rearrange("p (c k) -> p c k", k=30); t_halo = K.tile("halopre")
        base_off = cvP.off
        K.barrier()
        K.op("dve", lambda e: e.memset(S0, 0.0), writes=tS)
        B, NB, G, NTK, ntile = 16, 8, 4, 256, 2
        TRm = msk[:, M_TR16:M_TR16 + 128]
        BIm = msk[:, M_BI16:M_BI16 + NB]
        for pi_ in range(4):
            tok0 = pi_ * 256
            extra_p, st_p = plan_mix_pre[pi_]
            K.barrier()
            cv = Carver(); cv.off = base_off
            tg_ = f"pre{pi_}"
            hTp = cv.bf16(KC * NTK).rearrange("p (k t) -> p k t", t=NTK)
            t_hTp = [[K.tile(f"hTp{tg_}_{kc}")] for kc in range(KC)]
            rmsnorm(C_GMIX, cv, ntg=1, src=lambda kc, g: xT[:, kc, tok0:tok0 + NTK], dst=lambda kc, g: hTp[:, kc, :],
                    t_src=[[xt_tiles(kc, tok0, NTK)[0]] for kc in range(KC)], t_dst=t_hTp, width=NTK, tag=tg_)

            def proj_fm(w, tw, j, bank):
                for kc in range(KC):
                    K.op("pe", lambda e: e.matmul(P[bank][:, 0:NTK], w[:, kc, j * 128:(j + 1) * 128], hTp[:, kc, :], start=(kc == 0), stop=(kc == KC - 1)),
                         reads=[tw, t_hTp[kc][0]], writes=[tP[bank]], inc=(kc == KC - 1))

            def proj_tm(w, tw, tl, bank):
                for kc in range(KC):
                    K.op("pe", lambda e: e.matmul(P[bank][:, 0:512], hTp[:, kc, tl * 128:(tl + 1) * 128], w[:, kc, :], start=(kc == 0), stop=(kc == KC - 1)),
                         reads=[tw, t_hTp[kc][0]], writes=[tP[bank]], inc=(kc == KC - 1))

            if extra_p:
                zaT = cv.f32(4 * NTK).rearrange("p (c t) -> p c t", t=NTK); t_zaT = K.tile("zaT" + tg_)
                sgt = cv.f32(NTK); t_sgt = K.tile("sgt" + tg_)
                for half in range(2):
                    wza, twza = W.get(extra_p[2 * half])
                    for j in range(4):
                        proj_fm(wza, twza, j, j % 2)
                        K.op("act", lambda e: e.copy(zaT[:, j, :], P[j % 2][:, 0:NTK]), reads=[tP[j % 2]], writes=[t_zaT])
                    wzb, twzb = W.get(extra_p[2 * half + 1])
                    for j in range(4):
                        c = half * 4 + j
                        proj_fm(wzb, twzb, j, j % 2)
                        K.op("act", lambda e: e.activation(sgt, P[j % 2][:, 0:NTK], AF.Sigmoid), reads=[tP[j % 2]], writes=[t_sgt])
                        K.op("dve", lambda e: e.tensor_tensor(halo[:, c, :], zaT[:, j, NTK - 30:NTK], sgt[:, NTK - 30:NTK], ALU.mult),
                             reads=[t_zaT, t_sgt], writes=[t_halo])
            Kh = cv.bf16(ntile * 1024).rearrange("p (l c) -> p l c", c=1024); t_Kh = K.tile("Kh" + tg_)
            Vv = cv.bf16(ntile * 1024).rearrange("p (l c) -> p l c", c=1024); t_V = K.tile("V" + tg_)
            Aa = cv.f32(512); t_A = K.tile("A" + tg_)
            Bf = cv.f32(512); t_B = K.tile("B" + tg_)
            Cc = cv.f32(512); t_C = K.tile("C" + tg_)
            dec = cv.f32(ntile * 8 * NB).rearrange("p (l c) -> p l c", c=8 * NB); t_dec = K.tile("dec" + tg_)
            khm_raw = cv.f32(G * 512); t_Khm = K.tile("Khm" + tg_)
            Khm = khm_raw.bitcast(BF16).rearrange("p (r c) -> p r c", c=1024)
            bufsets = [(Aa, Bf, Cc, [t_A], [t_B], [t_C]),
                       (khm_raw[:, 0:512], khm_raw[:, 512:1024], khm_raw[:, 1024:1536], [t_Khm], [t_Khm], [t_Khm])]

            def zf_unit(ui, half, tl):
                par = ui % 2
                bA, bC = (0, 2) if par == 0 else (1, 3)
                A_, B_, C_, tA_, tB_, tC_ = bufsets[par]
                w, tw = W.get(st_p[half])
                cols = slice(half * 512, half * 512 + 512)
                proj_tm(w, tw, tl, bA)
                yield
                K.op("act", lambda e: e.activation(A_, P[bA][:, 0:512], AF.Sigmoid), reads=[tP[bA]], writes=tA_)
                yield
                K.op("dve", lambda e: e.tensor_tensor(C_, A_, lbt[:, cols], ALU.mult), reads=tA_ + [t_lbt], writes=tC_)
                K.op("dve", lambda e: e.tensor_tensor(A_, A_, C_, ALU.subtract), reads=tA_ + tC_, writes=tA_)
                K.op("dve", lambda e: e.tensor_tensor(A_, A_, lbt[:, cols], ALU.add), reads=tA_ + [t_lbt], writes=tA_)
                yield
                K.op("act", lambda e: e.activation(B_, A_, AF.Ln), reads=tA_, writes=tB_)
                yield
                K.op("dve", lambda e: e.tensor_scalar(A_, A_, -1.0, 1.0, ALU.mult, ALU.add), reads=tA_, writes=tA_)
                K.op("pe", lambda e: e.matmul(P[bC][:, 0:512], TRm, B_, start=True, stop=True), reads=[t_msk] + tB_, writes=[tP[bC]])
                for j in range(4):
                    h = half * 4 + j
                    K.op("pe", lambda e: e.matmul(P[7][:, tl * 8 * NB + h * NB:tl * 8 * NB + (h + 1) * NB], B_[:, j * 128:(j + 1) * 128], BIm, start=True, stop=True),
                         reads=tB_ + [t_msk], writes=[tP[7]], inc=(j == 3))
                yield
                K.op("act", lambda e: e.activation(C_, P[bC][:, 0:512], AF.Exp), reads=[tP[bC]], writes=tC_)
                yield
                K.op("dve", lambda e: e.tensor_tensor(Kh[:, tl, cols], A_, C_, ALU.mult), reads=tA_ + tC_, writes=[t_Kh])
                yield

            run_interleaved([zf_unit(half * ntile + tl, half, tl) for half in range(2) for tl in range(ntile)])
            K.op("act", lambda e: e.activation(dec.rearrange("p l c -> p (l c)"), P[7][:, 0:ntile * 8 * NB], AF.Exp), reads=[tP[7]], writes=[t_dec])
            for half in range(2):
                w, tw = W.get(st_p[2 + half])
                for tl in range(ntile):
                    bank = (half * ntile + tl) % 2
                    proj_tm(w, tw, tl, bank)
                    K.op("act", lambda e: e.copy(Vv[:, tl, half * 512:half * 512 + 512], P[bank][:, 0:512]), reads=[tP[bank]], writes=[t_V])
            for tl in range(ntile):
                for r in range(G):
                    K.op("dve", lambda e: e.tensor_scalar(Khm[:, r, :], Kh[:, tl, :], msk[:, M_RM16 + r:M_RM16 + r + 1], None, ALU.mult),
                         reads=[t_Kh, t_msk], writes=[t_Khm])
                for blk in range(NB):
                    a, r = divmod(blk, G)
                    ub = 4 + 2 * (blk % 2)
                    for h in range(8):
                        K.op("pe", lambda e: e.matmul(P[ub + h // 4][:, (h % 4) * 128:(h % 4 + 1) * 128], Khm[64 * a:64 * a + 64, r, h * 128:(h + 1) * 128],
                                                      Vv[64 * a:64 * a + 64, tl, h * 128:(h + 1) * 128], start=True, stop=True),
                             reads=[t_Khm, t_V], writes=[tP[ub + h // 4]], inc=(h % 4 == 3))
                    for h in range(8):
                        dcol = dec[:, tl, h * NB + blk:h * NB + blk + 1]
                        K.op("dve", lambda e: e.scalar_tensor_tensor(S0[:, h * 128:(h + 1) * 128], S0[:, h * 128:(h + 1) * 128], dcol,
                                                                     P[ub + h // 4][:, (h % 4) * 128:(h % 4 + 1) * 128], ALU.mult, ALU.add),
                             reads=[tS[h], t_dec, tP[ub + h // 4]], writes=[tS[h]])
        K.op("dve", lambda e: e.tensor_scalar(S0, S0, cst[:, C_FLAG:C_FLAG + 1], None, ALU.mult), reads=tS + [t_cst], writes=tS)
        K.op("dve", lambda e: e.tensor_scalar(halo.rearrange("p c k -> p (c k)"), halo.rearrange("p c k -> p (c k)"), cst[:, C_FLAG:C_FLAG + 1], None, ALU.mult),
             reads=[t_halo, t_cst], writes=[t_halo])
        K.dma("sp", pre_d[:, 0:1024], S0, reads=tS, writes=[t_pre])
        K.dma("sp", pre_d[:, 1024:1264], halo.rearrange("p c k -> p (c k)"), reads=[t_halo], writes=[t_pre], cont=True)

    def mixer():
        cvP = Carver()
        S_ = [cvP.f32(1024), None, None, None]
        t_S = [[K.tile(f"S{i}_{h}") for h in range(8)] for i in range(4)]
        NSB = 4
        halo = cvP.f32(8 * 30).rearrange("p (c k) -> p c k", k=30); t_halo = K.tile("halo")
        base_off = cvP.off
        K.barrier()
        K.dma("sp", S_[0], pre_d[:, 0:1024], reads=[t_pre], writes=t_S[0])
        K.dma("sp", halo.rearrange("p c k -> p (c k)"), pre_d[:, 1024:1264], reads=[t_pre], writes=[t_halo])
        for pi_, (tok0, ntile, sample) in enumerate(PASSES):
            NTK = ntile * 128
            B = 8 if sample else 16
            NB = 128 // B
            G = 64 // B
            TPm = msk[:, (M_TP8 if sample else M_TP16):(M_TP8 if sample else M_TP16) + 128]
            TRm = msk[:, (M_TR8 if sample else M_TR16):(M_TR8 if sample else M_TR16) + 128]
            TNm = msk[:, (M_TN8 if sample else M_TN16):(M_TN8 if sample else M_TN16) + 128]
            BIm = msk[:, (M_BI8 if sample else M_BI16):(M_BI8 if sample else M_BI16) + NB]
            RM0 = M_RM8 if sample else M_RM16
            win_p, wout_p, dg_p = plan_mix[pi_]
            K.barrier()
            cv = Carver(); cv.off = base_off
            tg_ = f"m{pi_}"
            hTp = cv.bf16(KC * NTK).rearrange("p (k t) -> p k t", t=NTK)
            t_hTp = [[K.tile(f"hTp{tg_}_{kc}")] for kc in range(KC)]
            rmsnorm(C_GMIX, cv, ntg=1, src=lambda kc, g: xT[:, kc, tok0:tok0 + NTK], dst=lambda kc, g: hTp[:, kc, :],
                    t_src=[[xt_tiles(kc, tok0, NTK)[0]] for kc in range(KC)], t_dst=t_hTp, width=NTK, tag=tg_)
            hT_all = [t_hTp[kc][0] for kc in range(KC)]
            o_bT = cv.bf16(8 * NTK).rearrange("p (c t) -> p c t", t=NTK); t_obT = K.tile("obT" + tg_)
            if sample:
                S_[1] = cv.f32(1024)
            qs = cv.bf16(8 * NTK).rearrange("p (h t) -> p h t", t=NTK); t_qs = K.tile("qs" + tg_)
            o_aT = qs; t_oaT = t_qs
            sgz = cv.bf16(8 * NTK).rearrange("p (h t) -> p h t", t=NTK); t_sgz = K.tile("sgz" + tg_)
            mark = cv.off

            def proj_fm(w, tw, j, bank):
                for kc in range(KC):
                    K.op("pe", lambda e: e.matmul(P[bank][:, 0:NTK], w[:, kc, j * 128:(j + 1) * 128], hTp[:, kc, :], start=(kc == 0), stop=(kc == KC - 1)),
                         reads=[tw, t_hTp[kc][0]], writes=[tP[bank]], inc=(kc == KC - 1))

            def proj_tm(w, tw, tl, bank):
                for kc in range(KC):
                    K.op("pe", lambda e: e.matmul(P[bank][:, 0:512], hTp[:, kc, tl * 128:(tl + 1) * 128], w[:, kc, :], start=(kc == 0), stop=(kc == KC - 1)),
                         reads=[tw, t_hTp[kc][0]], writes=[tP[bank]], inc=(kc == KC - 1))

            zaT = cv.f32(4 * NTK).rearrange("p (c t) -> p c t", t=NTK); t_zaT = K.tile("zaT" + tg_)
            sgt = cv.f32(NTK); t_sgt = K.tile("sgt" + tg_)
            dw = cv.f32(8 * NTK).rearrange("p (c t) -> p c t", t=NTK); t_dw = [K.tile(f"dw{tg_}_{c}") for c in range(8)]
            t_u = [K.tile(f"u{tg_}_{c}") for c in range(8)]
            if not sample:
                ub = cv.bf16(8 * (32 + NTK)).rearrange("p (c t) -> p c t", t=32 + NTK)[:, :, 0:30 + NTK]
                utmp = [cv.f32(NTK), cv.f32(NTK)]; t_utmp = [K.tile("utmp0" + tg_), K.tile("utmp1" + tg_)]
                K.op("dve", lambda e: e.tensor_copy(ub[:, :, 0:30], halo), reads=[t_halo], writes=t_u)
            else:
                uT = cv.f32(8 * 16 * 38).rearrange("p (c j t) -> p c j t", j=16, t=38)
                mark_s = cv.off
                stg2_ = [cv.f32(1024), cv.f32(1024)]; t_stg2_ = [K.tile("cstg0"), K.tile("cstg1")]
                scv = sc_d.rearrange("j r c -> (j r) c")
                for q in range(4):
                    stg = stg2_[q % 2]; t_stg = t_stg2_[q % 2]
                    K.dma("sp", stg[0:120, :], scv[q * 120:(q + 1) * 120, :], writes=[t_stg])
                    for c in range(8):
                        bank = c % 2
                        K.op("pe", lambda e: e.transpose(P[bank][:, 0:120], stg[0:120, c * 128:(c + 1) * 128], ident[0:120, 0:120]),
                             reads=[t_stg, t_msk], writes=[tP[bank]])
                        K.op("act", lambda e: e.copy(uT[:, c, q * 4:(q + 1) * 4, 0:30], P[bank][:, 0:120].rearrange("p (j r) -> p j r", r=30)),
                             reads=[tP[bank]], writes=[t_u[c]])
                K.barrier()
                cv.off = mark_s
            dgcur = {}

            def conv_chunk(c):
                if not sample:
                    if c % 2 == 0:
                        dgcur["v"], dgcur["t"] = W.get(dg_p[c // 2])
                    dgv, tdg = dgcur["v"], dgcur["t"]
                    bank = 2 + c % 2
                    for k_ in range(31):
                        off = ((c % 2) * 31 + k_) * 128
                        K.op("pe", lambda e: e.matmul(P[bank][:, 0:NTK], dgv[:, off:off + 128], ub[:, c, k_:k_ + NTK], start=(k_ == 0), stop=(k_ == 30)),
                             reads=[tdg, t_u[c]], writes=[tP[bank]], inc=(k_ == 30))
                    K.op("act", lambda e: e.activation(dw[:, c, :], P[bank][:, 0:NTK], AF.Identity, bias=cst[:, C_CB + c:C_CB + c + 1]),
                         reads=[tP[bank], t_cst], writes=[t_dw[c]])
                    return

                def uwin(k):
                    return uT[:, c, k:k + NTK] if not sample else uT[:, c, :, k:k + 8]
                dwc = dw[:, c, :] if not sample else dw[:, c, :].rearrange("p (j t) -> p j t", t=8)
                wc = lambda k: cst[:, C_CW + c * 31 + k:C_CW + c * 31 + k + 1]
                K.op("dve", lambda e: e.tensor_scalar(dwc, uwin(0), wc(0), cst[:, C_CB + c:C_CB + c + 1], ALU.mult, ALU.add),
                     reads=[t_u[c], t_cst], writes=[t_dw[c]])
                for k in range(1, 31):
                    K.op("dve", lambda e: e.scalar_tensor_tensor(dwc, uwin(k), wc(k), dwc, ALU.mult, ALU.add),
                         reads=[t_u[c], t_cst, t_dw[c]], writes=[t_dw[c]])
            for half in range(2):
                wza, twza = W.get(win_p[2 * half])
                for j in range(4):
                    proj_fm(wza, twza, j, j % 2)
                    K.op("act", lambda e: e.copy(zaT[:, j, :], P[j % 2][:, 0:NTK]), reads=[tP[j % 2]], writes=[t_zaT])
                wzb, twzb = W.get(win_p[2 * half + 1])
                for j in range(4):
                    c = half * 4 + j
                    proj_fm(wzb, twzb, j, j % 2)
                    K.op("act", lambda e: e.activation(sgt, P[j % 2][:, 0:NTK], AF.Sigmoid), reads=[tP[j % 2]], writes=[t_sgt])
                    if not sample:
                        ut_ = utmp[c % 2]; tut_ = t_utmp[c % 2]
                        K.op("dve", lambda e: e.tensor_tensor(ut_, zaT[:, j, :], sgt, ALU.mult), reads=[t_zaT, t_sgt], writes=[tut_])
                        K.op("act", lambda e: e.copy(ub[:, c, 30:30 + NTK], ut_), reads=[tut_], writes=[t_u[c]])
                        K.op("dve", lambda e: e.tensor_copy(halo[:, c, :], ut_[:, NTK - 30:NTK]), reads=[tut_], writes=[t_halo])
                    else:
                        K.op("dve", lambda e: e.tensor_tensor(uT[:, c, :, 30:38], zaT[:, j, :].rearrange("p (j t) -> p j t", t=8),
                                                              sgt.rearrange("p (j t) -> p j t", t=8), ALU.mult),
                             reads=[t_zaT, t_sgt], writes=[t_u[c]])
                for c_ in range(half * 4, half * 4 + 4):
                    conv_chunk(c_)
            for half in range(2):
                w, tw = W.get(win_p[4 + half])
                for j in range(4):
                    proj_fm(w, tw, j, j % 2)
                    K.op("act", lambda e: e.activation(qs[:, half * 4 + j, :], P[j % 2][:, 0:NTK], AF.Silu), reads=[tP[j % 2]], writes=[t_qs])
            for half in range(2):
                w, tw = W.get(win_p[6 + half])
                for j in range(4):
                    proj_fm(w, tw, j, j % 2)
                    K.op("act", lambda e: e.activation(sgz[:, half * 4 + j, :], P[j % 2][:, 0:NTK], AF.Silu), reads=[tP[j % 2]], writes=[t_sgz])
            if not sample:
                if tok0 + NTK == 1024:
                    cso = cv.f32(1024); t_cso = K.tile("cso")
                    for c in range(8):
                        bank = c // 4
                        K.op("pe", lambda e: e.transpose(P[bank][0:30, (c % 4) * 128:(c % 4 + 1) * 128], halo[:, c, :], ident),
                             reads=[t_halo, t_msk], writes=[tP[bank]])
                    for bank in range(2):
                        K.op("act", lambda e: e.copy(cso[0:30, bank * 512:(bank + 1) * 512], P[bank][0:30, :]), reads=[tP[bank]], writes=[t_cso])
                    K.dma("sp", scp_d, cso[0:30, :], reads=[t_cso])
            else:
                cso = cv.f32(1024); t_cso = K.tile("csos")
                unew = cv.f32(1024).rearrange("p (c t) -> p c t", t=128); t_unew = K.tile("unew")
                K.op("dve", lambda e: e.tensor_copy(unew.rearrange("p c (j t) -> p c j t", t=8), uT[:, :, :, 30:38]), reads=t_u, writes=[t_unew])
                for c in range(8):
                    bank = c // 4
                    K.op("pe", lambda e: e.transpose(P[bank][:, (c % 4) * 128:(c % 4 + 1) * 128], unew[:, c, :], ident),
                         reads=[t_unew, t_msk], writes=[tP[bank]])
                for bank in range(2):
                    K.op("act", lambda e: e.copy(cso[:, bank * 512:(bank + 1) * 512], P[bank][:, :]), reads=[tP[bank]], writes=[t_cso])
                for j in range(16):
                    K.dma("sp", scs_d[j, 22:30, :], cso[8 * j:8 * j + 8, :], reads=[t_cso])
                t_cpy = K.tile("sccopy")
                K.dma("sp", scs_d[:, 0:22, :], sc_d[:, 8:30, :], writes=[t_cpy])
            sqd = cv.f32(NTK); t_sqd = K.tile("sqd" + tg_)
            mu = cv.f32(NTK); t_mu = K.tile("mu" + tg_)
            rs2 = cv.f32(NTK); t_rs2 = K.tile("rs2" + tg_)
            tt = cv.f32(NTK); t_tt = K.tile("tt" + tg_)
            for c in range(8):
                K.op("pe", lambda e: e.matmul(P[6][:, 0:NTK], onesf, dw[:, c, :], start=(c == 0), stop=(c == 7)),
                     reads=[t_msk, t_dw[c]], writes=[tP[6]])
                K.op("act", lambda e: e.activation(sqd, dw[:, c, :], AF.Square), reads=[t_dw[c]], writes=[t_sqd])
                K.op("pe", lambda e: e.matmul(P[7][:, 0:NTK], onesf, sqd, start=(c == 0), stop=(c == 7)),
                     reads=[t_msk, t_sqd], writes=[tP[7]])
            K.op("dve", lambda e: e.tensor_scalar(mu, P[6][:, 0:NTK], 1.0 / 1024, None, ALU.mult), reads=[tP[6]], writes=[t_mu])
            K.op("dve", lambda e: e.tensor_tensor(rs2, mu, mu, ALU.mult), reads=[t_mu], writes=[t_rs2])
            K.op("dve", lambda e: e.scalar_tensor_tensor(rs2, P[7][:, 0:NTK], 1.0 / 1024, rs2, ALU.mult, ALU.subtract), reads=[tP[7], t_rs2], writes=[t_rs2])
            K.op("act", lambda e: e.activation(rs2, rs2, AF.Sqrt, bias=EPS), reads=[t_rs2], writes=[t_rs2])
            K.op("dve", lambda e: e.reciprocal(rs2, rs2), reads=[t_rs2], writes=[t_rs2])
            for c in range(8):
                K.op("dve", lambda e: e.tensor_tensor(tt, dw[:, c, :], mu, ALU.subtract), reads=[t_dw[c], t_mu], writes=[t_tt])
                K.op("dve", lambda e: e.tensor_tensor(tt, tt, rs2, ALU.mult), reads=[t_tt, t_rs2], writes=[t_tt])
                K.op("act", lambda e: e.activation(o_bT[:, c, :], tt, AF.Silu, bias=cst[:, C_LB + c:C_LB + c + 1], scale=cst[:, C_LG + c:C_LG + c + 1]),
                     reads=[t_tt, t_cst], writes=[t_obT])

            K.barrier()
            cv.off = mark
            if sample:
                S_[2] = cv.f32(1024); S_[3] = cv.f32(1024)
            KtT = cv.bf16(8 * NTK).rearrange("p (h t) -> p h t", t=NTK); t_KtT = K.tile("KtT" + tg_)
            QtT = cv.bf16(8 * NTK).rearrange("p (h t) -> p h t", t=NTK); t_QtT = K.tile("QtT" + tg_)
            Kh = cv.bf16(ntile * 1024).rearrange("p (l c) -> p l c", c=1024); t_Kh = K.tile("Kh" + tg_)
            Vv = cv.bf16(ntile * 1024).rearrange("p (l c) -> p l c", c=1024); t_V = K.tile("V" + tg_)
            Aa = cv.f32(512); t_A = K.tile("A" + tg_)
            Bf = cv.f32(512); t_B = K.tile("B" + tg_)
            Cc = cv.f32(512); t_C = K.tile("C" + tg_)
            C2 = Cc; t_C2 = t_C

            dec = cv.f32(ntile * 8 * NB).rearrange("p (l c) -> p l c", c=8 * NB); t_dec = K.tile("dec" + tg_)
            kh_raw = cv.f32(1024); t_Khm = K.tile("Khm" + tg_)
            Khm = kh_raw.bitcast(BF16).rearrange("p (r c) -> p r c", c=1024)
            ATm_flat = cv.bf16(1024); ATm = ATm_flat.rearrange("p (h t) -> p h t", t=128); t_ATm = K.tile("ATm" + tg_); Ktm = ATm_flat[:, 0:512]; t_Ktm = t_ATm
            sbf_raw = cv.f32(512); Sbf = sbf_raw.bitcast(BF16); t_Sbf = [K.tile(f"Sbf{tg_}_{h}") for h in range(8)]
            osq = Sbf
            rsn = kh_raw; t_rsn = t_Khm
            P4b = P[4][:].bitcast(BF16)
            bufsets = [(Aa, Bf, Cc, Ktm, [t_A], [t_B], [t_C], [t_Ktm]),
                       (kh_raw[:, 0:512], kh_raw[:, 512:1024], sbf_raw, ATm_flat[:, 512:1024], [t_Khm], [t_Khm], t_Sbf, [t_ATm])]

            def zf_unit(ui, half, tl):
                par = ui % 2
                bA, bB, bC = (0, 2, 4) if par == 0 else (1, 3, 5)
                A_, B_, C_, Kt_, tA_, tB_, tC_, tK_ = bufsets[par]
                PAb = P[bA][:].bitcast(BF16)
                w, tw = W.get(win_p[8 + half])
                cols = slice(half * 512, half * 512 + 512)
                proj_tm(w, tw, tl, bA)
                yield
                K.op("act", lambda e: e.activation(A_, P[bA][:, 0:512], AF.Sigmoid), reads=[tP[bA]], writes=tA_)
                yield
                K.op("dve", lambda e: e.tensor_tensor(C_, A_, lbt[:, cols], ALU.mult), reads=tA_ + [t_lbt], writes=tC_)
                K.op("dve", lambda e: e.tensor_tensor(A_, A_, C_, ALU.subtract), reads=tA_ + tC_, writes=tA_)
                K.op("dve", lambda e: e.tensor_tensor(A_, A_, lbt[:, cols], ALU.add), reads=tA_ + [t_lbt], writes=tA_)
                yield
                K.op("act", lambda e: e.activation(B_, A_, AF.Ln), reads=tA_, writes=tB_)
                yield
                K.op("dve", lambda e: e.tensor_scalar(A_, A_, -1.0, 1.0, ALU.mult, ALU.add), reads=tA_, writes=tA_)
                K.op("pe", lambda e: e.matmul(P[bB][:, 0:512], TNm, B_, start=True, stop=True), reads=[t_msk] + tB_, writes=[tP[bB]])
                K.op("pe", lambda e: e.matmul(P[bC][:, 0:512], TRm, B_, start=True, stop=True), reads=[t_msk] + tB_, writes=[tP[bC]])
                yield
                K.op("act", lambda e: e.activation(C_, P[bB][:, 0:512], AF.Exp), reads=[tP[bB]], writes=tC_)
                yield
                K.op("dve", lambda e: e.tensor_tensor(Kt_, A_, C_, ALU.mult), reads=tA_ + tC_, writes=tK_)
                yield
                K.op("act", lambda e: e.activation(C_, P[bC][:, 0:512], AF.Exp), reads=[tP[bC]], writes=tC_)
                for j in range(4):
                    K.op("pe", lambda e: e.transpose(PAb[:, j * 128:(j + 1) * 128], Kt_[:, j * 128:(j + 1) * 128], identb[:]),
                         reads=tK_ + [t_identb], writes=[tP[bA]], inc=(j == 3))
                yield
                K.op("dve", lambda e: e.tensor_tensor(Kh[:, tl, cols], A_, C_, ALU.mult), reads=tA_ + tC_, writes=[t_Kh])
                K.op("act", lambda e: e.copy(KtT[:, half * 4:half * 4 + 4, tl * 128:(tl + 1) * 128], PAb[:, 0:512].rearrange("p (h t) -> p h t", t=128)),
                     reads=[tP[bA]], writes=[t_KtT])
                for j in range(4):
                    K.op("pe", lambda e: e.matmul(P[bB][:, j * 128:(j + 1) * 128], B_[:, j * 128:(j + 1) * 128], TPm, start=True, stop=True),
                         reads=tB_ + [t_msk], writes=[tP[bB]], inc=(j == 3))
                yield
                K.op("act", lambda e: e.activation(C_, P[bB][:, 0:512], AF.Exp), reads=[tP[bB]], writes=tC_)
                for j in range(4):
                    h = half * 4 + j
                    K.op("pe", lambda e: e.matmul(P[7][:, tl * 8 * NB + h * NB: tl * 8 * NB + (h + 1) * NB], B_[:, j * 128:(j + 1) * 128], BIm, start=True, stop=True),
                         reads=tB_ + [t_msk], writes=[tP[7]], inc=(j == 3))
                yield
                K.op("dve", lambda e: e.tensor_tensor(QtT[:, half * 4:half * 4 + 4, tl * 128:(tl + 1) * 128], qs[:, half * 4:half * 4 + 4, tl * 128:(tl + 1) * 128],
                                                      C_.rearrange("p (h t) -> p h t", t=128), ALU.mult), reads=[t_qs] + tC_, writes=[t_QtT])
                yield

            run_interleaved([zf_unit(half * ntile + tl, half, tl) for half in range(2) for tl in range(ntile)])
            K.op("act", lambda e: e.activation(dec.rearrange("p l c -> p (l c)"), P[7][:, 0:ntile * 8 * NB], AF.Exp), reads=[tP[7]], writes=[t_dec])
            for half in range(2):
                w, tw = W.get(win_p[10 + half])
                for tl in range(ntile):
                    bank = (half * ntile + tl) % 2
                    proj_tm(w, tw, tl, bank)
                    K.op("act", lambda e: e.copy(Vv[:, tl, half * 512:half * 512 + 512], P[bank][:, 0:512]), reads=[tP[bank]], writes=[t_V])
            for tl in range(ntile):
                tsl = slice(tl * 128, (tl + 1) * 128)
                for h in range(8):
                    K.op("pe", lambda e: e.matmul(P[h // 4][:, (h % 4) * 128:(h % 4 + 1) * 128], KtT[:, h, tsl], QtT[:, h, tsl], start=True, stop=True),
                         reads=[t_KtT, t_QtT], writes=[tP[h // 4]], inc=(h % 4 == 3))
                for h in range(8):
                    K.op("dve", lambda e: e.tensor_tensor(ATm[:, h, :], P[h // 4][:, (h % 4) * 128:(h % 4 + 1) * 128], TPm, ALU.mult),
                         reads=[tP[h // 4], t_msk], writes=[t_ATm])
                for h in range(8):
                    K.op("pe", lambda e: e.matmul(P[2 + h // 4][:, (h % 4) * 128:(h % 4 + 1) * 128], Vv[:, tl, h * 128:(h + 1) * 128], ATm[:, h, :], start=(h % 4 == 0), stop=True),
                         reads=[t_V, t_ATm], writes=[tP[2 + h // 4]], inc=(h % 4 == 3))
                def s_load(b_):
                    K.dma("sp", S_[b_ % NSB].rearrange("p (h d) -> p h d", d=128), sh_d[b_].rearrange("h e d -> e h d"), writes=t_S[b_ % NSB])
                if sample:
                    for b_ in range(NSB):
                        s_load(b_)
                for blk in range(NB):
                    kb_ = blk % 2
                    K.op("dve", lambda e: e.tensor_scalar(Khm[:, kb_, :], Kh[:, tl, :], msk[:, RM0 + blk % G:RM0 + blk % G + 1], None, ALU.mult),
                         reads=[t_Kh, t_msk], writes=[t_Khm])
                    ub = 4 + 2 * (blk % 2)
                    si = blk % NSB if sample else 0
                    Sc = S_[si]; tS = t_S[si]
                    a, r = divmod(blk, G)
                    for h in range(8):
                        K.op("act", lambda e: e.copy(Sbf[:, h * 128:(h + 1) * 128], Sc[:, h * 128:(h + 1) * 128]), reads=[tS[h]], writes=[t_Sbf[h]])
                    for h in range(8):
                        c0 = (h % 4) * 128 + blk * B
                        K.op("pe", lambda e: e.matmul(P[2 + h // 4][:, c0:c0 + B], Sbf[:, h * 128:(h + 1) * 128], QtT[:, h, tl * 128 + blk * B: tl * 128 + (blk + 1) * B],
                                                      start=False, stop=True), reads=[t_Sbf[h], t_QtT], writes=[tP[2 + h // 4]], inc=(h % 4 == 3))
                    for h in range(8):
                        K.op("pe", lambda e: e.matmul(P[ub + h // 4][:, (h % 4) * 128:(h % 4 + 1) * 128], Khm[64 * a:64 * a + 64, kb_, h * 128:(h + 1) * 128],
                                                      Vv[64 * a:64 * a + 64, tl, h * 128:(h + 1) * 128], start=True, stop=True),
                             reads=[t_Khm, t_V], writes=[tP[ub + h // 4]], inc=(h % 4 == 3))
                    for h in range(8):
                        dcol = dec[:, tl, h * NB + blk:h * NB + blk + 1]
                        K.op("dve", lambda e: e.scalar_tensor_tensor(Sc[:, h * 128:(h + 1) * 128], Sc[:, h * 128:(h + 1) * 128], dcol,
                                                                     P[ub + h // 4][:, (h % 4) * 128:(h % 4 + 1) * 128], ALU.mult, ALU.add),
                             reads=[tS[h], t_dec, tP[ub + h // 4]], writes=[tS[h]])
                    if sample:
                        K.dma("sp", shs_d[blk].rearrange("h e d -> e h d"), Sc.rearrange("p (h d) -> p h d", d=128), reads=tS)
                        if blk + NSB < NB:
                            s_load(blk + NSB)
                for hb in range(2):
                    K.op("act", lambda e: e.activation(osq[:, hb * 512:(hb + 1) * 512], P[2 + hb][:, :], AF.Square), reads=[tP[2 + hb]], writes=t_Sbf[hb * 4:hb * 4 + 4])
                for h in range(8):
                    K.op("pe", lambda e: e.matmul(P[6 + h // 4][:, (h % 4) * 128:(h % 4 + 1) * 128], onesb[:], osq[:, h * 128:(h + 1) * 128], start=True, stop=True),
                         reads=[t_onesb, t_Sbf[h]], writes=[tP[6 + h // 4]], inc=(h % 4 == 3))
                for hb in range(2):
                    K.op("dve", lambda e: e.tensor_scalar(rsn[:, hb * 512:(hb + 1) * 512], P[6 + hb][:, :], 1.0 / 128, EPS, ALU.mult, ALU.add), reads=[tP[6 + hb]], writes=[t_rsn])
                K.op("act", lambda e: e.activation(rsn, rsn, AF.Sqrt), reads=[t_rsn], writes=[t_rsn])
                K.op("dve", lambda e: e.reciprocal(rsn, rsn), reads=[t_rsn], writes=[t_rsn])
                for hb in range(2):
                    K.op("dve", lambda e: e.scalar_tensor_tensor(rsn[:, hb * 512:(hb + 1) * 512], P[2 + hb][:, :], cst[:, C_GN:C_GN + 1], rsn[:, hb * 512:(hb + 1) * 512], ALU.mult, ALU.mult),
                         reads=[tP[2 + hb], t_cst, t_rsn], writes=[t_rsn])
                    K.op("dve", lambda e: e.tensor_tensor(o_aT[:, hb * 4:hb * 4 + 4, tsl], rsn[:, hb * 512:(hb + 1) * 512].rearrange("p (h t) -> p h t", t=128),
                                                          sgz[:, hb * 4:hb * 4 + 4, tsl], ALU.mult), reads=[t_rsn, t_sgz], writes=[t_oaT])
            if (not sample) and tok0 + NTK == 1024:
                K.dma("sp", shp_d.rearrange("h e d -> e h d"), S_[0].rearrange("p (h d) -> p h d", d=128), reads=t_S[0])
            if dbg_d is not None and stop_after == "mixer" and pi_ == 0:
                dtmp = cv.f32(1024); t_dtmp = K.tile("dtmpm")
                K.op("dve", lambda e: e.tensor_copy(dtmp, o_aT.rearrange("p c t -> p (c t)")), reads=[t_oaT], writes=[t_dtmp])
                dump(dtmp, [t_dtmp], 1024)
                K.op("dve", lambda e: e.tensor_copy(dtmp, o_bT.rearrange("p c t -> p (c t)")), reads=[t_obT], writes=[t_dtmp])
                dump(dtmp, [t_dtmp], 1024)
            for pp in range(4):
                w, tw = W.get(wout_p[pp])
                for j in range(4):
                    m = pp * 4 + j
                    bank = m % 2
                    for k in range(KC):
                        rhs = o_aT[:, k, :] if k < 8 else o_bT[:, k - 8, :]
                        K.op("pe", lambda e: e.matmul(P[bank][:, 0:NTK], w[:, k, j * 128:(j + 1) * 128], rhs, start=(k == 0), stop=(k == KC - 1)),
                             reads=[tw, t_oaT if k < 8 else t_obT], writes=[tP[bank]], inc=(k == KC - 1))
                    xv = xT[:, m, tok0:tok0 + NTK]
                    K.op("dve", lambda e: e.tensor_tensor(xv, xv, P[bank][:, 0:NTK], ALU.add), reads=[tP[bank]] + xt_tiles(m, tok0, NTK), writes=xt_tiles(m, tok0, NTK))

    if not os.environ.get("SKIP_PRE"):
        load_x(xpre_d, "p")
        ffn(plan_f1_pre, C_GF1, "p", ntg=2, tgw=512)
        mixer_pre()
    load_x(x_d, "m")
    if enabled("ffn1") and not os.environ.get("SKIP_FFN1"):
        ffn(plan_f1, C_GF1, "a")

    if dbg_d is not None and stop_after in ("load", "ffn1"):
        for g in range(NTG):
            dump(xT[:, 0, g * TG:(g + 1) * TG], [t_xT[0][g]], TG)
            dump(xT[:, 15, g * TG:(g + 1) * TG], [t_xT[15][g]], TG)

    if enabled("mixer") and not os.environ.get("SKIP_MIX"):
        mixer()
    if dbg_d is not None and stop_after == "mixer":
        for g in range(NTG):
            dump(xT[:, 0, g * TG:(g + 1) * TG], [t_xT[0][g]], TG)
            dump(xT[:, 15, g * TG:(g + 1) * TG], [t_xT[15][g]], TG)

    def xattn():
        K.barrier()
        cvx = Carver()
        mkT = cvx.bf16(4 * 256).rearrange("p (h m) -> p h m", m=256); t_mkT = K.tile("mkT")
        mvb = cvx.bf16(2 * 512).rearrange("p (c d) -> p c d", d=512); t_mvb = K.tile("mvb")
        qT = cvx.bf16(4 * NT).rearrange("p (h t) -> p h t", t=NT); t_qT = [K.tile(f"qT{g}") for g in range(NTG)]
        oxT = cvx.bf16(4 * NT).rearrange("p (h t) -> p h t", t=NT); t_oxT = [K.tile(f"oxT{g}") for g in range(NTG)]
        base = cvx.off
        wk_i, wv_i, wq_i, wo_i = plan_xa
        mst0 = cvx.f32(2048); mst = [mst0, mst0]; t_mst0 = K.tile("mst0"); t_mst = [t_mst0, t_mst0]
        memT = cvx.f32(KC * 256).rearrange("p (k t) -> p k t", t=256); t_memT = [[K.tile(f"memT{kc}")] for kc in range(KC)]
        mnT = cvx.bf16(KC * 256).rearrange("p (k t) -> p k t", t=256); t_mnT = [[K.tile(f"mnT{kc}")] for kc in range(KC)]
        ostg = [cvx.f32(512) for _ in range(2)]; t_ostg = [K.tile(f"ostg{i}") for i in range(2)]
        for t in range(2):
            K.dma("sp", mst[t], mem_d[t * 128:(t + 1) * 128, :], writes=[t_mst[t]])
            for q in range(4):
                bank = q
                for j in range(4):
                    kc = q * 4 + j
                    K.op("pe", lambda e: e.transpose(P[bank][:, j * 128:(j + 1) * 128], mst[t][:, kc * 128:(kc + 1) * 128], ident),
                         reads=[t_mst[t], t_msk], writes=[tP[bank]], inc=(j == 3))
                K.op("act", lambda e: e.copy(memT[:, q * 4:(q + 1) * 4, t * 128:(t + 1) * 128], P[bank][:].rearrange("p (a b) -> p a b", b=128)),
                     reads=[tP[bank]], writes=[t_memT[q * 4 + j][0] for j in range(4)])
        if os.environ.get('XA_STOP') == 'a1':
            return
        rmsnorm(C_GMEM, cvx, ntg=1, src=lambda kc, g: memT[:, kc, :], dst=lambda kc, g: mnT[:, kc, :], t_src=t_memT, t_dst=t_mnT, width=256, tag="mem")
        if os.environ.get('XA_STOP') == 'a2':
            return
        wkp, twk = W.get(wk_i)
        for h in range(4):
            for kc in range(KC):
                K.op("pe", lambda e: e.matmul(P[h % 2][:, 0:256], wkp[:, kc, h * 128:(h + 1) * 128], mnT[:, kc, :], start=(kc == 0), stop=(kc == KC - 1)),
                     reads=[twk, t_mnT[kc][0]], writes=[tP[h % 2]], inc=(kc == KC - 1))
            K.op("act", lambda e: e.copy(mkT[:, h, :], P[h % 2][:, 0:256]), reads=[tP[h % 2]], writes=[t_mkT])
        if os.environ.get('XA_STOP') == 'a3':
            return
        for t in range(2):
            for kc in range(KC):
                K.op("pe", lambda e: e.matmul(P[2 + t][:, 0:512], mnT[:, kc, t * 128:(t + 1) * 128], wkp[:, kc, :], start=(kc == 0), stop=(kc == KC - 1)),
                     reads=[twk, t_mnT[kc][0]], writes=[tP[2 + t]], inc=(kc == KC - 1))
            K.op("act", lambda e: e.copy(ostg[t], P[2 + t][:, 0:512]), reads=[tP[2 + t]], writes=[t_ostg[t]])
            K.dma("sp", mk_d[t * 128:(t + 1) * 128, :], ostg[t], reads=[t_ostg[t]])
        if os.environ.get('XA_STOP') == 'a4':
            return
        wvp, twv = W.get(wv_i)
        for t in range(2):
            for kc in range(KC):
                K.op("pe", lambda e: e.matmul(P[4 + t][:, 0:512], mnT[:, kc, t * 128:(t + 1) * 128], wvp[:, kc, :], start=(kc == 0), stop=(kc == KC - 1)),
                     reads=[twv, t_mnT[kc][0]], writes=[tP[4 + t]], inc=(kc == KC - 1))
            K.op("act", lambda e: e.copy(ostg[t], P[4 + t][:, 0:512]), reads=[tP[4 + t]], writes=[t_ostg[t]])
            K.op("dve", lambda e: e.tensor_copy(mvb[:, t, :], ostg[t]), reads=[t_ostg[t]], writes=[t_mvb])
            K.dma("sp", mv_d[t * 128:(t + 1) * 128, :], ostg[t], reads=[t_ostg[t]])
        if os.environ.get('XA_STOP') == 'a':
            return
        K.barrier()
        cvx.off = base
        hT = cvx.bf16(KC * NT).rearrange("p (k t) -> p k t", t=NT)
        t_hT = [[K.tile(f"hTx{k}_{g}") for g in range(NTG)] for k in range(KC)]
        hs = lambda k, g: hT[:, k, g * TG:(g + 1) * TG]
        rmsnorm(C_GXA, cvx, dst=hs, t_dst=t_hT, tag="xa")
        wqp, twq = W.get(wq_i)
        for h in range(4):
            bs = 3 * (h % 2)
            for kc in range(KC):
                for g in range(NTG):
                    K.op("pe", lambda e: e.matmul(P[bs + g][:, 0:TG], wqp[:, kc, h * 128:(h + 1) * 128], hs(kc, g), start=(kc == 0), stop=(kc == KC - 1)),
                         reads=[twq, t_hT[kc][g]], writes=[tP[bs + g]], inc=(kc == KC - 1))
            for g in range(NTG):
                K.op("act", lambda e: e.mul(qT[:, h, g * TG:(g + 1) * TG], P[bs + g][:, 0:TG], 128.0 ** -0.5), reads=[tP[bs + g]], writes=[t_qT[g]])
        if os.environ.get('XA_STOP') == 'b':
            return
        K.barrier()
        cvx.off = base
        xsets = []
        for i_ in range(2):
            xsets.append((cvx.f32(1024).rearrange("p (h m) -> p h m", m=256), cvx.bf16(1024), cvx.bf16(1024).rearrange("p (a t) -> p a t", t=128),
                          cvx.f32(4), cvx.f32(4), K.tile(f"pf{i_}"), K.tile(f"pn{i_}"), K.tile(f"pT{i_}"), K.tile(f"mx{i_}"), K.tile(f"rsum{i_}")))
        kst2 = [cvx.f32(1024).rearrange("p (c d) -> p c d", d=512) for _ in range(4)]; t_kst2 = [K.tile(f"kst{i}") for i in range(4)]
        KjT = cvx.bf16(1024).rearrange("p (h m) -> p h m", m=256); t_KjT = K.tile("KjT")
        qm = [cvx.bf16(512).rearrange("p (h t) -> p h t", t=128) for _ in range(2)]; t_qm = [K.tile(f"qm{i}") for i in range(2)]
        Vjb = cvx.bf16(1024).rearrange("p (c d) -> p c d", d=512); t_Vjb = K.tile("Vjb")
        def xa_tile(t):
            sample = (t == 8) and not os.environ.get('XA_NOSAMPLE')
            par = 0 if t == 8 else t % 2
            pf, pn, pT, mx, rsum, t_pf, t_pn, t_pT, t_mx, t_rsum = xsets[par]
            bs0, btr, bpv = (0, 4, 5) if par == 0 else (2, 6, 7)
            P4b = P[btr][:].bitcast(BF16)
            g = (t * 128) // TG
            tsl = slice(t * 128, (t + 1) * 128)
            if not sample:
                for h in range(4):
                    K.op("pe", lambda e: e.matmul(P[bs0 + h // 2][:, (h % 2) * 256:(h % 2 + 1) * 256], qT[:, h, tsl], mkT[:, h, :], start=True, stop=True),
                         reads=[t_qT[g], t_mkT], writes=[tP[bs0 + h // 2]], inc=(h % 2 == 1))
            else:
                for j in range(16):
                    kst = kst2[j % 4]; t_kst = t_kst2[j % 4]
                    K.dma("sp", kst, ck_d[j].rearrange("(c p) d -> p c d", p=128), writes=[t_kst])
                    for h in range(4):
                        for mc in range(2):
                            K.op("pe", lambda e: e.transpose(P[2 + h // 2][:, ((h % 2) * 2 + mc) * 128:((h % 2) * 2 + mc + 1) * 128], kst[:, mc, h * 128:(h + 1) * 128], ident),
                                 reads=[t_kst, t_msk], writes=[tP[2 + h // 2]])
                    for hb in range(2):
                        K.op("act", lambda e: e.copy(KjT[:, hb * 2:hb * 2 + 2, :], P[2 + hb][:].rearrange("p (h m) -> p h m", m=256)), reads=[tP[2 + hb]], writes=[t_KjT])
                    qb = j % 2
                    K.op("dve", lambda e: e.memset(qm[qb], 0.0), writes=[t_qm[qb]])
                    K.op("dve", lambda e: e.tensor_copy(qm[qb][:, :, 8 * j:8 * j + 8], qT[:, :, 1024 + 8 * j:1024 + 8 * j + 8]), reads=[t_qT[2]], writes=[t_qm[qb]])
                    for h in range(4):
                        K.op("pe", lambda e: e.matmul(P[h // 2][:, (h % 2) * 256:(h % 2 + 1) * 256], qm[qb][:, h, :], KjT[:, h, :], start=(j == 0 and h % 2 == 0), stop=(j == 15)),
                             reads=[t_qm[qb], t_KjT], writes=[tP[h // 2]])
            yield
            for hb in range(2):
                K.op("dve", lambda e: e.reduce_max(mx[:, hb * 2:hb * 2 + 2], P[bs0 + hb][:].rearrange("p (h m) -> p h m", m=256), AX.X), reads=[tP[bs0 + hb]], writes=[t_mx])
            K.op("dve", lambda e: e.tensor_scalar(mx, mx, -1.0, None, ALU.mult), reads=[t_mx], writes=[t_mx])
            yield
            for h in range(4):
                K.op("act", lambda e: e.activation(pf[:, h, :], P[bs0 + h // 2][:, (h % 2) * 256:(h % 2 + 1) * 256], AF.Exp, bias=mx[:, h:h + 1], scale=1.0, accum_out=rsum[:, h:h + 1]),
                     reads=[tP[bs0 + h // 2], t_mx], writes=[t_pf, t_rsum])
            yield
            K.op("dve", lambda e: e.reciprocal(rsum, rsum), reads=[t_rsum], writes=[t_rsum])
            for h in range(4):
                K.op("dve", lambda e: e.tensor_scalar(pn[:, h * 256:(h + 1) * 256], pf[:, h, :], rsum[:, h:h + 1], None, ALU.mult), reads=[t_pf, t_rsum], writes=[t_pn])
            yield
            for a in range(8):
                K.op("pe", lambda e: e.transpose(P4b[:, a * 128:(a + 1) * 128], pn[:, a * 128:(a + 1) * 128], identb[:]), reads=[t_pn, t_identb], writes=[tP[btr]], inc=(a == 7))
            K.op("act", lambda e: e.copy(pT, P4b[:, 0:1024].rearrange("p (a t) -> p a t", t=128)), reads=[tP[btr]], writes=[t_pT])
            yield
            if not sample:
                for h in range(4):
                    for mc in range(2):
                        K.op("pe", lambda e: e.matmul(P[bpv][:, h * 128:(h + 1) * 128], mvb[:, mc, h * 128:(h + 1) * 128], pT[:, h * 2 + mc, :], start=(h == 0 and mc == 0), stop=True),
                             reads=[t_mvb, t_pT], writes=[tP[bpv]], inc=(h == 3 and mc == 1))
            else:
                for j in range(16):
                    kst = kst2[j % 4]; t_kst = t_kst2[j % 4]
                    K.dma("sp", kst, cv_d[j].rearrange("(c p) d -> p c d", p=128), writes=[t_kst])
                    K.op("dve", lambda e: e.tensor_copy(Vjb, kst), reads=[t_kst], writes=[t_Vjb])
                    for h in range(4):
                        for mc in range(2):
                            K.op("pe", lambda e: e.matmul(P[5][:, h * 128 + 8 * j:h * 128 + 8 * j + 8], Vjb[:, mc, h * 128:(h + 1) * 128], pT[:, h * 2 + mc, 8 * j:8 * j + 8],
                                                          start=(j == 0 and h == 0 and mc == 0), stop=True), reads=[t_Vjb, t_pT], writes=[tP[5]])
            K.op("act", lambda e: e.copy(oxT[:, :, tsl], P[bpv][:].rearrange("p (h t) -> p h t", t=128)), reads=[tP[bpv]], writes=[t_oxT[g]])
            yield

        run_interleaved([xa_tile(t) for t in range(8)])
        for _ in xa_tile(8):
            pass
        if os.environ.get('XA_STOP') == 'c':
            return
        wop, two = W.get(wo_i)
        for m in range(KC):
            bs = 3 * (m % 2)
            for k in range(4):
                for g in range(NTG):
                    K.op("pe", lambda e: e.matmul(P[bs + g][:, 0:TG], wop[:, k, m * 128:(m + 1) * 128], oxT[:, k, g * TG:(g + 1) * TG], start=(k == 0), stop=(k == 3)),
                         reads=[two, t_oxT[g]], writes=[tP[bs + g]], inc=(k == 3))
            for g in range(NTG):
                K.op("dve", lambda e: e.tensor_tensor(xs(m, g), xs(m, g), P[bs + g][:, 0:TG], ALU.add), reads=[tP[bs + g], t_xT[m][g]], writes=[t_xT[m][g]])

    if enabled("xattn"):
        xattn()
    if dbg_d is not None and stop_after == "xattn":
        for g in range(NTG):
            dump(xT[:, 0, g * TG:(g + 1) * TG], [t_xT[0][g]], TG)
            dump(xT[:, 15, g * TG:(g + 1) * TG], [t_xT[15][g]], TG)

    if enabled("ffn2"):
        ffn(plan_f2, C_GF2, "b")
    if enabled("final"):
        K.barrier()
        cvz = Carver()
        rmsnorm(C_GFIN, cvz, dst=xs, t_dst=t_xT, tag="fin")
        ystg = [cvz.f32(2048) for _ in range(2)]
        t_ystg = [K.tile(f"ystg{i}") for i in range(2)]
        for t in range(9):
            s = t % 2
            g = (t * 128) // TG
            for q in range(4):
                bank = q
                for j in range(4):
                    kc = q * 4 + j
                    K.op("pe", lambda e: e.transpose(P[bank][:, j * 128:(j + 1) * 128], xT[:, kc, t * 128:(t + 1) * 128], ident),
                         reads=[t_xT[kc][g], t_msk], writes=[tP[bank]], inc=(j == 3))
                if q % 2 == 0:
                    K.op("act", lambda e: e.copy(ystg[s][:, q * 512:(q + 1) * 512], P[bank][:]), reads=[tP[bank]], writes=[t_ystg[s]])
                else:
                    K.op("dve", lambda e: e.tensor_copy(ystg[s][:, q * 512:(q + 1) * 512], P[bank][:]), reads=[tP[bank]], writes=[t_ystg[s]])
            K.dma("sp", y_d[t * 128:(t + 1) * 128, :], ystg[s], reads=[t_ystg[s]])

    K.finish()
    lp.__exit__(None, None, None)
    K.close()
    print("instructions", K.n_ins, "waits", K.n_wait, "panels", len(W.plan))
    return nc


def make_masks():
    m = np.zeros((128, NMSK), np.float32)
    idx = np.arange(128)
    m[:, M_ID:M_ID + 128] = np.eye(128, dtype=np.float32)
    for B, otp, otr, otn in ((16, M_TP16, M_TR16, M_TN16), (8, M_TP8, M_TR8, M_TN8)):
        same = (idx[:, None] // B) == (idx[None, :] // B)
        tp = (same & (idx[:, None] <= idx[None, :])).astype(np.float32)
        tr = (same & (idx[:, None] > idx[None, :])).astype(np.float32)
        m[:, otp:otp + 128] = tp
        m[:, otr:otr + 128] = tr
        m[:, otn:otn + 128] = -tp
    m[:, M_BI16:M_BI16 + 8] = (idx[:, None] // 16 == np.arange(8)[None, :])
    m[:, M_BI8:M_BI8 + 16] = (idx[:, None] // 8 == np.arange(16)[None, :])
    m[:, M_ONES:M_ONES + 128] = 1.0
    m[:, M_RM16:M_RM16 + 4] = ((idx[:, None] % 64) // 16 == np.arange(4)[None, :])
    m[:, M_RM8:M_RM8 + 8] = ((idx[:, None] % 64) // 8 == np.arange(8)[None, :])
    return m


def fm(v):
    return np.ascontiguousarray(v.reshape(16, 128).T)


def prep_core(inp, c, masks):
    seq, half = c // 2, c % 2
    d = {}
    xp = inp["x_prompt"][seq, half * 1024:(half + 1) * 1024]
    xsm = inp["x_sample"][c * 16:(c + 1) * 16].reshape(128, D)
    d["x"] = np.ascontiguousarray(np.concatenate([xp, xsm], 0))
    d["mem"] = np.ascontiguousarray(inp["mem_prompt"][seq])
    d["xpre"] = np.ascontiguousarray(np.concatenate([inp["x_prompt"][seq, 0:1024], np.zeros((128, D), np.float32)], 0))
    d["sh"] = np.ascontiguousarray(inp["state_hgrn"][0, c * 16:(c + 1) * 16])
    d["sc"] = np.ascontiguousarray(inp["state_conv"][0, c * 16:(c + 1) * 16])
    d["ck"] = np.ascontiguousarray(inp["cache_mem_k"][0, c * 16:(c + 1) * 16].reshape(16, 256, 512))
    d["cv"] = np.ascontiguousarray(inp["cache_mem_v"][0, c * 16:(c + 1) * 16].reshape(16, 256, 512))
    for k, n in (("f1g", "ffn1_w_gate"), ("f1u", "ffn1_w_up"), ("f1d", "ffn1_w_down"), ("win", "w_in"), ("wout", "w_out"),
                 ("wq", "xattn_wq"), ("wk", "xattn_wk"), ("wv", "xattn_wv"), ("wo", "xattn_wo"),
                 ("f2g", "ffn2_w_gate"), ("f2u", "ffn2_w_up"), ("f2d", "ffn2_w_down")):
        d[k] = inp[n][0]
    cst = np.zeros((128, NCST), np.float32)
    cst[:, C_GF1:C_GF1 + 16] = fm(inp["norm_ffn1"][0])
    cst[:, C_GMIX:C_GMIX + 16] = fm(inp["norm_mix"][0])
    cst[:, C_GXA:C_GXA + 16] = fm(inp["norm_xattn"][0])
    cst[:, C_GMEM:C_GMEM + 16] = fm(inp["norm_mem"][0])
    cst[:, C_GF2:C_GF2 + 16] = fm(inp["norm_ffn2"][0])
    cst[:, C_GFIN:C_GFIN + 16] = fm(inp["norm_final"])
    cst[:, C_GN] = inp["hgrn_gnorm"][0]
    cw = inp["conv_w"][0]
    cst[:, C_CW:C_CW + 248] = cw.reshape(31, 8, 128).transpose(2, 1, 0).reshape(128, 248)
    cst[:, C_CB:C_CB + 8] = inp["conv_b"][0].reshape(8, 128).T
    cst[:, C_LG:C_LG + 8] = inp["conv_ln_g"][0].reshape(8, 128).T
    cst[:, C_LB:C_LB + 8] = inp["conv_ln_b"][0].reshape(8, 128).T
    cst[:, C_FLAG] = float(half)
    d["cst"] = cst
    d["msk"] = masks
    d["lb"] = np.ascontiguousarray(inp["hgrn_lb"])
    return d


from concourse.bass_utils import run_bass_kernel_spmd

_NC = None


def kernel(**inp):
    global _NC
    inp = {k: np.asarray(v) for k, v in inp.items()}
    masks = make_masks()
    in_maps = [prep_core(inp, c, masks) for c in range(8)]
    nc = build(stop_after="all")
    res = run_bass_kernel_spmd(nc, in_maps, core_ids=list(range(8)))
    R_ = res.results
    y_prompt = np.zeros((4, 2048, 2048), np.float32)
    y_sample = np.zeros((128, 8, 2048), np.float32)
    shp = np.zeros((1, 4, 8, 128, 128), np.float32)
    scp = np.zeros((1, 4, 30, 1024), np.float32)
    mk = np.zeros((1, 4, 256, 4, 128), np.float32)
    mv = np.zeros((1, 4, 256, 4, 128), np.float32)
    shs = np.zeros((1, 128, 8, 128, 128), np.float32)
    scs = np.zeros((1, 128, 30, 1024), np.float32)
    for c in range(8):
        r = R_[c]
        seq, half = c // 2, c % 2
        y_prompt[seq, half * 1024:(half + 1) * 1024] = r["y"][:1024]
        y_sample[c * 16:(c + 1) * 16] = r["y"][1024:].reshape(16, 8, 2048)
        shs[0, c * 16:(c + 1) * 16] = r["shs"]
        scs[0, c * 16:(c + 1) * 16] = r["scs"]
        if half == 1:
            shp[0, seq] = r["shp"]
            scp[0, seq] = r["scp"]
        else:
            mk[0, seq] = r["mko"].reshape(256, 4, 128)
            mv[0, seq] = r["mvo"].reshape(256, 4, 128)
    return (y_prompt, y_sample, shp, scp, mk, mv, shs, scs)
```

```python
import contextlib
import concourse.bass as bass
import concourse.mybir as mybir

F32 = mybir.dt.float32
BF16 = mybir.dt.bfloat16
AF = mybir.ActivationFunctionType
ALU = mybir.AluOpType
AX = mybir.AxisListType


class Tile:
    __slots__ = ("name", "w", "r", "dsem", "dcnt")

    def __init__(self, name):
        self.name = name
        self.w = None
        self.r = {}
        self.dsem = None
        self.dcnt = 0


class Kern:
    ENG = ("pe", "act", "dve", "pool", "sp")

    def __init__(self, nc):
        self.nc = nc
        self.es = contextlib.ExitStack()
        self.eng = {"pe": nc.tensor, "act": nc.scalar, "dve": nc.vector,
                    "pool": nc.gpsimd, "sp": nc.sync}
        self.sem = {e: self.es.enter_context(nc.semaphore("s_" + e)) for e in self.ENG}
        self.cnt = {e: 0 for e in self.ENG}
        self.pending = {e: False for e in self.ENG}
        self.waited = {e: {} for e in self.ENG}
        self.dma_tiles = []
        self.n_ins = 0
        self.n_wait = 0

    def sbuf(self, name, shape, dtype):
        return self.es.enter_context(self.nc.sbuf_tensor(name, list(shape), dtype))

    def psum(self, name, shape, dtype):
        return self.es.enter_context(self.nc.psum_tensor(name, list(shape), dtype))

    def tile(self, name):
        return Tile(name)

    def _dsem(self, t):
        if t.dsem is None:
            t.dsem = self.es.enter_context(self.nc.semaphore("d_" + t.name))
            self.dma_tiles.append(t)
        return t.dsem

    def _wait(self, e, tok):
        key, sem, cnt, src = tok
        if self.waited[e].get(key, 0) >= cnt:
            return
        self.eng[e].wait_ge(sem, cnt)
        self.waited[e][key] = cnt
        self.n_wait += 1

    def _deps(self, e, reads, writes):
        for t in reads:
            if t.w is not None:
                self._wait(e, t.w)
        for t in writes:
            if t.w is not None and t.w[3] != e:
                self._wait(e, t.w)
            for key, tok in t.r.items():
                if tok[3] != e:
                    self._wait(e, tok)

    def op(self, e, fn, reads=(), writes=(), inc=True):
        self._deps(e, reads, writes)
        ins = fn(self.eng[e])
        self.n_ins += 1
        if inc:
            ins.then_inc(self.sem[e], 1)
            self.cnt[e] += 1
            c = self.cnt[e]
        else:
            c = self.cnt[e] + 1
        tok = (e, self.sem[e], c, e)
        for t in reads:
            t.r[e] = tok
        for t in writes:
            t.w = tok
            t.r = {}
        return ins

    def dma(self, q, out, in_, reads=(), writes=(), sem_tile=None, cont=False):
        if not cont:
            self._deps(q, reads, writes)
        st = sem_tile or (writes[0] if writes else reads[0])
        sem = self._dsem(st)
        ins = self.eng[q].dma_start(out=out, in_=in_)
        ins.then_inc(sem, 16)
        st.dcnt += 16
        self.n_ins += 1
        tok = ("d_" + st.name, sem, st.dcnt, None)
        for t in reads:
            t.r[tok[0]] = tok
        for t in writes:
            t.w = tok
            t.r = {}
        return ins

    def barrier(self):
        for e in self.ENG:
            for e2 in self.ENG:
                if e2 != e and self.cnt[e2]:
                    self._wait(e, (e2, self.sem[e2], self.cnt[e2], e2))
            for t in self.dma_tiles:
                if t.dcnt and not t.name.startswith("wb"):
                    self._wait(e, ("d_" + t.name, t.dsem, t.dcnt, None))

    def finish(self):
        for t in self.dma_tiles:
            if t.dcnt:
                self._wait("sp", ("d_" + t.name, t.dsem, t.dcnt, None))
        for e in self.ENG:
            if e != "sp" and self.cnt[e]:
                self._wait("sp", (e, self.sem[e], self.cnt[e], e))

    def close(self):
        self.es.close()


import os
import numpy as np
import concourse.bass as bass
import concourse.mybir as mybir

NT = 1152
TG = 384
NTG = 3
D = 2048
KC = 16
DFF = 5632
NPAN = 11
EPS = 1e-6

C_GF1, C_GMIX, C_GXA, C_GMEM, C_GF2, C_GFIN = 0, 16, 32, 48, 64, 80
C_GN = 96
C_CW = 97
C_CB = 97 + 248
C_LG = C_CB + 8
C_LB = C_LG + 8
C_FLAG = C_LB + 8
NCST = C_FLAG + 1
M_ID, M_TP16, M_TR16, M_TP8, M_TR8 = 0, 128, 256, 384, 512
M_TN16, M_TN8 = 640, 768
M_BI16 = 896
M_BI8 = 904
M_ONES = 920
M_RM16 = 1048
M_RM8 = 1052
NMSK = 1060


def build(stop_after="all", dbg_spec=None):
    nc = bass.Bass("TRN2", target_bir_lowering=False)
    dt = lambda n, s, k="ExternalInput": nc.dram_tensor(n, list(s), F32, kind=k).ap()
    x_d = dt("x", [NT, D]); mem_d = dt("mem", [256, D]); xpre_d = dt("xpre", [NT, D])
    pre_d = nc.dram_tensor("pre_scratch", [128, 1264], F32, kind="Internal").ap()
    sh_d = dt("sh", [16, 8, 128, 128]); sc_d = dt("sc", [16, 30, 1024])
    ck_d = dt("ck", [16, 256, 512]); cv_d = dt("cv", [16, 256, 512])
    f1g = dt("f1g", [D, DFF]); f1u = dt("f1u", [D, DFF]); f1d = dt("f1d", [DFF, D])
    win = dt("win", [D, 6144]); wout = dt("wout", [D, D])
    wq = dt("wq", [D, 512]); wk = dt("wk", [D, 512]); wv = dt("wv", [D, 512]); wo = dt("wo", [512, D])
    f2g = dt("f2g", [D, DFF]); f2u = dt("f2u", [D, DFF]); f2d = dt("f2d", [DFF, D])
    cst_d = dt("cst", [128, NCST]); msk_d = dt("msk", [128, NMSK]); lb_d = dt("lb", [2, 1024])
    y_d = dt("y", [NT, D], "ExternalOutput")
    shp_d = dt("shp", [8, 128, 128], "ExternalOutput")
    scp_d = dt("scp", [30, 1024], "ExternalOutput")
    mk_d = dt("mko", [256, 512], "ExternalOutput"); mv_d = dt("mvo", [256, 512], "ExternalOutput")
    shs_d = dt("shs", [16, 8, 128, 128], "ExternalOutput")
    scs_d = dt("scs", [16, 30, 1024], "ExternalOutput")
    dbg_d = None
    if dbg_spec:
        dbg_d = dt("dbg", [128, dbg_spec], "ExternalOutput")

    K = Kern(nc)
    lp = nc.allow_low_precision("bf16 matmul operands, fp32 accumulate")
    lp.__enter__()
    order = ["load", "ffn1", "mixer", "xattn", "ffn2", "final"]
    enabled = lambda ph: order.index(ph) <= order.index(stop_after) if stop_after in order else True

    xT = K.sbuf("xT", [128, KC, NT], F32)
    t_xT = [[K.tile(f"xT{k}_{g}") for g in range(NTG)] for k in range(KC)]
    NS = 4
    wb = [K.sbuf(f"wb{i}", [128, 8192], BF16) for i in range(NS)]
    t_wb = [K.tile(f"wb{i}") for i in range(NS)]
    cst = K.sbuf("cst_sb", [128, NCST], F32); t_cst = K.tile("cst")
    msk = K.sbuf("msk_sb", [128, NMSK], F32); t_msk = K.tile("msk")
    onesb = K.sbuf("onesb", [128, 128], BF16); t_onesb = K.tile("onesb")
    identb = K.sbuf("identb", [128, 128], BF16); t_identb = K.tile("identb")
    ARENA = 62976
    arena = K.sbuf("arena", [128, ARENA // 4], F32)
    P = [K.psum(f"P{i}", [128, 512], F32) for i in range(8)]
    tP = [K.tile(f"P{i}") for i in range(8)]

    ident = msk[:, M_ID:M_ID + 128]
    onesf = msk[:, M_ONES:M_ONES + 128]

    def xs(k, g):
        return xT[:, k, g * TG:(g + 1) * TG]


    def run_interleaved(gens, width=2):
        gens = list(gens)
        active = []
        while gens or active:
            while len(active) < width and gens:
                active.append(gens.pop(0))
            for g_ in list(active):
                try:
                    next(g_)
                except StopIteration:
                    active.remove(g_)

    class Carver:
        def __init__(self):
            self.off = 0

        def take(self, nbytes):
            o = self.off
            self.off += (nbytes + 3) // 4 * 4
            assert self.off <= ARENA, self.off
            return o

        def f32(self, n):
            o = self.take(n * 4)
            return arena[:, o // 4:o // 4 + n]

        def bf16(self, n):
            o = self.take(n * 2)
            return arena[:, o // 4:o // 4 + (n + 1) // 2].bitcast(BF16)[:, 0:n]

    K.dma("sp", cst[:], cst_d, writes=[t_cst])
    K.dma("sp", msk[:], msk_d, writes=[t_msk])
    K.op("dve", lambda e: e.tensor_copy(onesb[:], onesf), reads=[t_msk], writes=[t_onesb])
    K.op("dve", lambda e: e.tensor_copy(identb[:], ident), reads=[t_msk], writes=[t_identb])

    NCACHE = 20
    wcache_d = nc.dram_tensor("wcache", [NCACHE, 128, 8192], BF16, kind="Internal").ap()
    t_wcache = [K.tile(f"wcache{i}") for i in range(NCACHE)]

    class WS:
        def __init__(self):
            self.plan = []
            self.issued = 0
            self.cache_idx = {}
            self.cache_ready = set()

        def add_col(self, w, c0, key=None):
            self.plan.append(("col", w, c0, key)); return len(self.plan) - 1

        def add_row(self, w, r0, key=None):
            self.plan.append(("row", w, r0, key)); return len(self.plan) - 1

        def add_cached(self, key):
            self.plan.append(("flat", None, None, key)); return len(self.plan) - 1

        def _issue(self, i):
            kind, w, o, key = self.plan[i]
            b = i % NS
            assert kind != "flat" or key in self.cache_ready
            if key is not None and key in self.cache_ready:
                ci = self.cache_idx[key]
                for q in range(4):
                    K.dma("pool", wb[b][:, q * 2048:(q + 1) * 2048], wcache_d[ci][:, q * 2048:(q + 1) * 2048],
                          reads=[t_wcache[ci]], writes=[t_wb[b]], cont=(q > 0))
                return
            if kind == "col":
                src = w.rearrange("(kc p) c -> p kc c", p=128)
                dst = wb[b][:].rearrange("p (kc c) -> p kc c", c=512)
                for q in range(4):
                    K.dma("pool", dst[:, q * 4:(q + 1) * 4, :], src[:, q * 4:(q + 1) * 4, o:o + 512], writes=[t_wb[b]], cont=(q > 0))
            else:
                src = w[o:o + 512, :].rearrange("(j p) c -> p j c", p=128)
                dst = wb[b][:].rearrange("p (j c) -> p j c", c=2048)
                for q in range(4):
                    K.dma("pool", dst[:, q:q + 1, :], src[:, q:q + 1, :], writes=[t_wb[b]], cont=(q > 0))
            if key is not None:
                ci = self.cache_idx.setdefault(key, len(self.cache_idx))
                assert ci < NCACHE
                K.dma("sp", wcache_d[ci], wb[b][:], reads=[t_wb[b]], writes=[t_wcache[ci]])
                self.cache_ready.add(key)

        def get(self, i, upto=None):
            upto = i + 3 if upto is None else upto
            while self.issued < min(len(self.plan), upto + 1):
                self._issue(self.issued); self.issued += 1
            b = i % NS
            kind = self.plan[i][0]
            if kind == "col":
                return wb[b][:].rearrange("p (kc c) -> p kc c", c=512), t_wb[b]
            if kind == "flat":
                return wb[b][:], t_wb[b]
            return wb[b][:].rearrange("p (j c) -> p j c", c=2048), t_wb[b]

    W = WS()
    groups = [(0, 1), (2, 3), (4, 5), (6, 7), (8, 9), (10,)]

    def plan_ffn(g_d, u_d, d_d):
        pl = []
        for grp in groups:
            gu = [(W.add_col(g_d, p * 512), W.add_col(u_d, p * 512)) for p in grp]
            dn = [W.add_row(d_d, p * 512) for p in grp]
            pl.append((gu, dn))
        return pl

    plan_f1_pre = plan_ffn(f1g, f1u, f1d)
    plan_mix_pre = []
    for pp_ in range(4):
        extra = [W.add_col(win, p * 512, key=("win", p)) for p in (8, 10, 9, 11)] if pp_ == 3 else []
        plan_mix_pre.append((extra, [W.add_col(win, p * 512, key=("win", p)) for p in (2, 3, 4, 5)]))
    plan_f1 = plan_ffn(f1g, f1u, f1d)
    plan_mix = []
    PASSES = [(0, 2, False), (256, 2, False), (512, 2, False), (768, 2, False), (1024, 1, True)]
    WIN_ORDER = [8, 10, 9, 11, 0, 1, 6, 7, 2, 3, 4, 5]
    for (tok0_, ntile_, sample_) in PASSES:
        seq_ = [8, 10] + ([] if sample_ else ["dg0", "dg1"]) + [9, 11] + ([] if sample_ else ["dg2", "dg3"]) + [0, 1, 6, 7, 2, 3, 4, 5]
        ids_ = {}
        for p in seq_:
            ids_[p] = W.add_cached(("dg", int(p[2]))) if isinstance(p, str) else W.add_col(win, p * 512, key=("win", p))
        plan_mix.append(([ids_[p] for p in WIN_ORDER], [W.add_col(wout, p * 512, key=("wout", p)) for p in range(4)],
                         [ids_.get(f"dg{q}") for q in range(4)]))
    plan_xa = (W.add_col(wk, 0), W.add_col(wv, 0), W.add_col(wq, 0), W.add_row(wo, 0))
    plan_f2 = plan_ffn(f2g, f2u, f2d)

    dbg_off = [0]

    def dump(ap, reads, n):
        if dbg_d is None:
            return
        o = dbg_off[0]
        K.dma("sp", dbg_d[:, o:o + n], ap, reads=reads)
        dbg_off[0] += n

    lbt = K.sbuf("lbt", [128, 1024], F32); t_lbt = K.tile("lbt")

    def setup_lb():
        cv0 = Carver()
        tmp = cv0.f32(2048); t_tmp = K.tile("lbtmp")
        K.dma("sp", tmp, lb_d.rearrange("a b -> (a b)").partition_broadcast(128), writes=[t_tmp])
        K.op("dve", lambda e: e.tensor_tensor(tmp[:, 0:1024], tmp[:, 0:1024], tmp[:, 1024:2048], ALU.subtract), reads=[t_tmp], writes=[t_tmp])
        K.op("act", lambda e: e.activation(lbt[:], tmp[:, 0:1024], AF.Sigmoid), reads=[t_tmp], writes=[t_lbt])

    setup_lb()

    def setup_dg():
        K.barrier()
        cv0 = Carver()
        stg_ = [cv0.bf16(8192), cv0.bf16(8192)]
        t_stg_ = [K.tile("dgstg0"), K.tile("dgstg1")]
        for q in range(4):
            b = q % 2
            for cc in range(2):
                c = 2 * q + cc
                for k_ in range(31):
                    off = (cc * 31 + k_) * 128
                    K.op("dve", lambda e: e.tensor_scalar(stg_[b][:, off:off + 128], identb[:], cst[:, C_CW + c * 31 + k_:C_CW + c * 31 + k_ + 1], None, ALU.mult),
                         reads=[t_identb, t_cst], writes=[t_stg_[b]])
            ci = W.cache_idx.setdefault(("dg", q), len(W.cache_idx))
            K.dma("sp", wcache_d[ci], stg_[b], reads=[t_stg_[b]], writes=[t_wcache[ci]])
            W.cache_ready.add(("dg", q))

    setup_dg()

    def load_x(src_d, tag):
        K.barrier()
        cv = Carver()
        xst = [cv.f32(2048) for _ in range(4)]
        t_xst = [K.tile(f"xst{tag}{i}") for i in range(4)]
        for t in range(9):
            s = t % 4
            K.dma("sp", xst[s], src_d[t * 128:(t + 1) * 128, :], writes=[t_xst[s]])
            g, off = divmod(t * 128, TG)
            for q in range(4):
                bank = (t * 4 + q) % 4
                for j in range(4):
                    kc = q * 4 + j
                    K.op("pe", lambda e: e.transpose(P[bank][:, j * 128:(j + 1) * 128], xst[s][:, kc * 128:(kc + 1) * 128], ident),
                         reads=[t_xst[s], t_msk], writes=[tP[bank]], inc=(j == 3))
                eng = "act" if q % 2 == 0 else "dve"
                dst = xT[:, q * 4:(q + 1) * 4, t * 128:(t + 1) * 128]
                src = P[bank][:].rearrange("p (a b) -> p a b", b=128)
                if eng == "act":
                    K.op("act", lambda e: e.copy(dst, src), reads=[tP[bank]], writes=[t_xT[q * 4 + j][g] for j in range(4)])
                else:
                    K.op("dve", lambda e: e.tensor_copy(dst, src), reads=[tP[bank]], writes=[t_xT[q * 4 + j][g] for j in range(4)])

    def rmsnorm(gcol0, cvn, ntg=NTG, src=None, dst=None, t_src=None, t_dst=None, width=TG, tag=''):
        src = src or xs
        t_src = t_src or t_xT
        sq = [cvn.bf16(width) for _ in range(2)]
        t_sq = [K.tile(f"sq{i}_{gcol0}{tag}") for i in range(2)]
        rst = cvn.f32(width); t_rst = K.tile(f"rst_{gcol0}{tag}")
        for g in range(ntg):
            for kc in range(KC):
                b = kc % 2
                K.op("act", lambda e: e.activation(sq[b], src(kc, g), AF.Square), reads=[t_src[kc][g]], writes=[t_sq[b]])
                K.op("pe", lambda e: e.matmul(P[6][:, 0:width], onesb[:], sq[b], start=(kc == 0), stop=(kc == KC - 1)),
                     reads=[t_onesb, t_sq[b]], writes=[tP[6]], inc=True)
            K.op("dve", lambda e: e.tensor_scalar(rst, P[6][:, 0:width], 1.0 / D, EPS, ALU.mult, ALU.add), reads=[tP[6]], writes=[t_rst])
            K.op("act", lambda e: e.activation(rst, rst, AF.Sqrt), reads=[t_rst], writes=[t_rst])
            K.op("dve", lambda e: e.reciprocal(rst, rst), reads=[t_rst], writes=[t_rst])
            for kc in range(KC):
                eng = "dve"
                K.op(eng, lambda e: e.scalar_tensor_tensor(dst(kc, g), src(kc, g), cst[:, gcol0 + kc:gcol0 + kc + 1], rst, ALU.mult, ALU.mult),
                     reads=[t_src[kc][g], t_cst, t_rst], writes=[t_dst[kc][g]])

    def xt_tiles(kc, tok0, n):
        return [t_xT[kc][g] for g in range(tok0 // TG, (tok0 + n - 1) // TG + 1)]

    def ffn(plan, gcol0, tag, ntg=NTG, tgw=TG):
        K.barrier()
        cvf = Carver()
        hT = cvf.bf16(KC * NT).rearrange("p (k t) -> p k t", t=NT)
        t_hT = [[K.tile(f"hT{tag}{k}_{g}") for g in range(ntg)] for k in range(KC)]
        hs = lambda k, g: hT[:, k, g * tgw:(g + 1) * tgw]
        xsl = lambda m, g: xT[:, m, g * tgw:(g + 1) * tgw]
        xtl = lambda m, g: xt_tiles(m, g * tgw, tgw)
        if tgw == TG:
            rmsnorm(gcol0, cvf, dst=hs, t_dst=t_hT, tag=tag)
        else:
            t_dummy = [[K.tile(f"xsrc{tag}{k}_{g}") for g in range(ntg)] for k in range(KC)]
            rmsnorm(gcol0, cvf, ntg=ntg, src=xsl, dst=hs, t_src=t_dummy, t_dst=t_hT, width=tgw, tag=tag)
        sg = [cvf.f32(tgw) for g in range(ntg)]
        t_sg = [K.tile(f"sg{tag}_{g}") for g in range(ntg)]
        act = [[cvf.bf16(tgw) for g in range(ntg)] for _ in range(8)]
        t_act = [[K.tile(f"act{tag}{i}_{g}") for g in range(ntg)] for i in range(8)]
        dset = 0
        for gu, dn in plan:
            for pi, (ig, iu) in enumerate(gu):
                wg, twg = W.get(ig, upto=ig + 2)
                wu, twu = W.get(iu, upto=iu + 2)
                for j in range(4):
                    ch = pi * 4 + j
                    for kc in range(KC):
                        for g in range(ntg):
                            K.op("pe", lambda e: e.matmul(P[g][:, 0:tgw], wg[:, kc, j * 128:(j + 1) * 128], hs(kc, g), start=(kc == 0), stop=(kc == KC - 1)),
                                 reads=[twg, t_hT[kc][g]], writes=[tP[g]], inc=(kc == KC - 1))
                    for g in range(ntg):
                        K.op("act", lambda e: e.activation(sg[g], P[g][:, 0:tgw], AF.Silu), reads=[tP[g]], writes=[t_sg[g]])
                    for kc in range(KC):
                        for g in range(ntg):
                            K.op("pe", lambda e: e.matmul(P[3 + g][:, 0:tgw], wu[:, kc, j * 128:(j + 1) * 128], hs(kc, g), start=(kc == 0), stop=(kc == KC - 1)),
                                 reads=[twu, t_hT[kc][g]], writes=[tP[3 + g]], inc=(kc == KC - 1))
                    for g in range(ntg):
                        K.op("dve", lambda e: e.tensor_tensor(act[ch][g], sg[g], P[3 + g][:, 0:tgw], ALU.mult),
                             reads=[t_sg[g], tP[3 + g]], writes=[t_act[ch][g]])
            nk = 4 * len(dn)
            wds = [W.get(i, upto=i + 1 + (0 if (len(dn) == 2 and i == dn[0]) else 1)) for i in dn]
            for m in range(KC):
                base = 3 * (dset % 2); dset += 1
                for k in range(nk):
                    wd, twd = wds[k // 4]
                    for g in range(ntg):
                        K.op("pe", lambda e: e.matmul(P[base + g][:, 0:tgw], wd[:, k % 4, m * 128:(m + 1) * 128], act[k][g], start=(k == 0), stop=(k == nk - 1)),
                             reads=[twd, t_act[k][g]], writes=[tP[base + g]], inc=(k == nk - 1))
                for g in range(ntg):
                    K.op("dve", lambda e: e.scalar_tensor_tensor(xsl(m, g), P[base + g][:, 0:tgw], 0.5, xsl(m, g), ALU.mult, ALU.add),
                         reads=[tP[base + g]] + xtl(m, g), writes=xtl(m, g))

    t_pre = K.tile("pre_scratch")

    def mixer_pre():
        cvP = Carver()
        S0 = cvP.f32(1024); tS = [K.tile(f"Spre{h}") for h in range(8)]
        halo = cvP.f32(8 * 30).rearrange("p (c k) -> p c k", k=30); t_halo = K.tile("halopre")
        base_off = cvP.off
        K.barrier()
        K.op("dve", lambda e: e.memset(S0, 0.0), writes=tS)
        B, NB, G, NTK, ntile = 16, 8, 4, 256, 2
        TRm = msk[:, M_TR16:M_TR16 + 128]
        BIm = msk[:, M_BI16:M_BI16 + NB]
        for pi_ in range(4):
            tok0 = pi_ * 256
            extra_p, st_p = plan_mix_pre[pi_]
            K.barrier()
            cv = Carver(); cv.off = base_off
            tg_ = f"pre{pi_}"
            hTp = cv.bf16(KC * NTK).rearrange("p (k t) -> p k t", t=NTK)
            t_hTp = [[K.tile(f"hTp{tg_}_{kc}")] for kc in range(KC)]
            rmsnorm(C_GMIX, cv, ntg=1, src=lambda kc, g: xT[:, kc, tok0:tok0 + NTK], dst=lambda kc, g: hTp[:, kc, :],
                    t_src=[[xt_tiles(kc, tok0, NTK)[0]] for kc in range(KC)], t_dst=t_hTp, width=NTK, tag=tg_)

            def proj_fm(w, tw, j, bank):
                for kc in range(KC):
                    K.op("pe", lambda e: e.matmul(P[bank][:, 0:NTK], w[:, kc, j * 128:(j + 1) * 128], hTp[:, kc, :], start=(kc == 0), stop=(kc == KC - 1)),
                         reads=[tw, t_hTp[kc][0]], writes=[tP[bank]], inc=(kc == KC - 1))

            def proj_tm(w, tw, tl, bank):
                for kc in range(KC):
                    K.op("pe", lambda e: e.matmul(P[bank][:, 0:512], hTp[:, kc, tl * 128:(tl + 1) * 128], w[:, kc, :], start=(kc == 0), stop=(kc == KC - 1)),
                         reads=[tw, t_hTp[kc][0]], writes=[tP[bank]], inc=(kc == KC - 1))

            if extra_p:
                zaT = cv.f32(4 * NTK).rearrange("p (c t) -> p c t", t=NTK); t_zaT = K.tile("zaT" + tg_)
                sgt = cv.f32(NTK); t_sgt = K.tile("sgt" + tg_)
                for half in range(2):
                    wza, twza = W.get(extra_p[2 * half])
                    for j in range(4):
                        proj_fm(wza, twza, j, j % 2)
                        K.op("act", lambda e: e.copy(zaT[:, j, :], P[j % 2][:, 0:NTK]), reads=[tP[j % 2]], writes=[t_zaT])
                    wzb, twzb = W.get(extra_p[2 * half + 1])
                    for j in range(4):
                        c = half * 4 + j
                        proj_fm(wzb, twzb, j, j % 2)
                        K.op("act", lambda e: e.activation(sgt, P[j % 2][:, 0:NTK], AF.Sigmoid), reads=[tP[j % 2]], writes=[t_sgt])
                        K.op("dve", lambda e: e.tensor_tensor(halo[:, c, :], zaT[:, j, NTK - 30:NTK], sgt[:, NTK - 30:NTK], ALU.mult),
                             reads=[t_zaT, t_sgt], writes=[t_halo])
            Kh = cv.bf16(ntile * 1024).rearrange("p (l c) -> p l c", c=1024); t_Kh = K.tile("Kh" + tg_)
            Vv = cv.bf16(ntile * 1024).rearrange("p (l c) -> p l c", c=1024); t_V = K.tile("V" + tg_)
            Aa = cv.f32(512); t_A = K.tile("A" + tg_)
            Bf = cv.f32(512); t_B = K.tile("B" + tg_)
            Cc = cv.f32(512); t_C = K.tile("C" + tg_)
            dec = cv.f32(ntile * 8 * NB).rearrange("p (l c) -> p l c", c=8 * NB); t_dec = K.tile("dec" + tg_)
            khm_raw = cv.f32(G * 512); t_Khm = K.tile("Khm" + tg_)
            Khm = khm_raw.bitcast(BF16).rearrange("p (r c) -> p r c", c=1024)
            bufsets = [(Aa, Bf, Cc, [t_A], [t_B], [t_C]),
                       (khm_raw[:, 0:512], khm_raw[:, 512:1024], khm_raw[:, 1024:1536], [t_Khm], [t_Khm], [t_Khm])]

            def zf_unit(ui, half, tl):
                par = ui % 2
                bA, bC = (0, 2) if par == 0 else (1, 3)
                A_, B_, C_, tA_, tB_, tC_ = bufsets[par]
                w, tw = W.get(st_p[half], upto=st_p[half] + 2)
                cols = slice(half * 512, half * 512 + 512)
                proj_tm(w, tw, tl, bA)
                yield
                K.op("act", lambda e: e.activation(A_, P[bA][:, 0:512], AF.Sigmoid), reads=[tP[bA]], writes=tA_)
                yield
                K.op("dve", lambda e: e.tensor_tensor(C_, A_, lbt[:, cols], ALU.mult), reads=tA_ + [t_lbt], writes=tC_)
                K.op("dve", lambda e: e.tensor_tensor(A_, A_, C_, ALU.subtract), reads=tA_ + tC_, writes=tA_)
                K.op("dve", lambda e: e.tensor_tensor(A_, A_, lbt[:, cols], ALU.add), reads=tA_ + [t_lbt], writes=tA_)
                yield
                K.op("act", lambda e: e.activation(B_, A_, AF.Ln), reads=tA_, writes=tB_)
                yield
                K.op("dve", lambda e: e.tensor_scalar(A_, A_, -1.0, 1.0, ALU.mult, ALU.add), reads=tA_, writes=tA_)
                K.op("pe", lambda e: e.matmul(P[bC][:, 0:512], TRm, B_, start=True, stop=True), reads=[t_msk] + tB_, writes=[tP[bC]])
                for j in range(4):
                    h = half * 4 + j
                    K.op("pe", lambda e: e.matmul(P[7][:, tl * 8 * NB + h * NB:tl * 8 * NB + (h + 1) * NB], B_[:, j * 128:(j + 1) * 128], BIm, start=True, stop=True),
                         reads=tB_ + [t_msk], writes=[tP[7]], inc=(j == 3))
                yield
                K.op("act", lambda e: e.activation(C_, P[bC][:, 0:512], AF.Exp), reads=[tP[bC]], writes=tC_)
                yield
                K.op("dve", lambda e: e.tensor_tensor(Kh[:, tl, cols], A_, C_, ALU.mult), reads=tA_ + tC_, writes=[t_Kh])
                yield

            run_interleaved([zf_unit(half * ntile + tl, half, tl) for half in range(2) for tl in range(ntile)])
            K.op("act", lambda e: e.activation(dec.rearrange("p l c -> p (l c)"), P[7][:, 0:ntile * 8 * NB], AF.Exp), reads=[tP[7]], writes=[t_dec])
            for half in range(2):
                w, tw = W.get(st_p[2 + half])
                for tl in range(ntile):
                    bank = (half * ntile + tl) % 2
                    proj_tm(w, tw, tl, bank)
                    K.op("act", lambda e: e.copy(Vv[:, tl, half * 512:half * 512 + 512], P[bank][:, 0:512]), reads=[tP[bank]], writes=[t_V])
            for tl in range(ntile):
                for r in range(G):
                    K.op("dve", lambda e: e.tensor_scalar(Khm[:, r, :], Kh[:, tl, :], msk[:, M_RM16 + r:M_RM16 + r + 1], None, ALU.mult),
                         reads=[t_Kh, t_msk], writes=[t_Khm])
                for blk in range(NB):
                    a, r = divmod(blk, G)
                    ub = 4 + 2 * (blk % 2)
                    for h in range(8):
                        K.op("pe", lambda e: e.matmul(P[ub + h // 4][:, (h % 4) * 128:(h % 4 + 1) * 128], Khm[64 * a:64 * a + 64, r, h * 128:(h + 1) * 128],
                                                      Vv[64 * a:64 * a + 64, tl, h * 128:(h + 1) * 128], start=True, stop=True),
                             reads=[t_Khm, t_V], writes=[tP[ub + h // 4]], inc=(h % 4 == 3))
                    for h in range(8):
                        dcol = dec[:, tl, h * NB + blk:h * NB + blk + 1]
                        K.op("dve", lambda e: e.scalar_tensor_tensor(S0[:, h * 128:(h + 1) * 128], S0[:, h * 128:(h + 1) * 128], dcol,
                                                                     P[ub + h // 4][:, (h % 4) * 128:(h % 4 + 1) * 128], ALU.mult, ALU.add),
                             reads=[tS[h], t_dec, tP[ub + h // 4]], writes=[tS[h]])
        K.op("dve", lambda e: e.tensor_scalar(S0, S0, cst[:, C_FLAG:C_FLAG + 1], None, ALU.mult), reads=tS + [t_cst], writes=tS)
        K.op("dve", lambda e: e.tensor_scalar(halo.rearrange("p c k -> p (c k)"), halo.rearrange("p c k -> p (c k)"), cst[:, C_FLAG:C_FLAG + 1], None, ALU.mult),
             reads=[t_halo, t_cst], writes=[t_halo])
        K.dma("sp", pre_d[:, 0:1024], S0, reads=tS, writes=[t_pre])
        K.dma("sp", pre_d[:, 1024:1264], halo.rearrange("p c k -> p (c k)"), reads=[t_halo], writes=[t_pre], cont=True)

    def mixer():
        cvP = Carver()
        S_ = [cvP.f32(1024), None, None, None]
        t_S = [[K.tile(f"S{i}_{h}") for h in range(8)] for i in range(4)]
        NSB = 4
        halo = cvP.f32(8 * 30).rearrange("p (c k) -> p c k", k=30); t_halo = K.tile("halo")
        base_off = cvP.off
        K.barrier()
        K.dma("sp", S_[0], pre_d[:, 0:1024], reads=[t_pre], writes=t_S[0])
        K.dma("sp", halo.rearrange("p c k -> p (c k)"), pre_d[:, 1024:1264], reads=[t_pre], writes=[t_halo])
        for pi_, (tok0, ntile, sample) in enumerate(PASSES):
            NTK = ntile * 128
            B = 8 if sample else 16
            NB = 128 // B
            G = 64 // B
            TPm = msk[:, (M_TP8 if sample else M_TP16):(M_TP8 if sample else M_TP16) + 128]
            TRm = msk[:, (M_TR8 if sample else M_TR16):(M_TR8 if sample else M_TR16) + 128]
            TNm = msk[:, (M_TN8 if sample else M_TN16):(M_TN8 if sample else M_TN16) + 128]
            BIm = msk[:, (M_BI8 if sample else M_BI16):(M_BI8 if sample else M_BI16) + NB]
            RM0 = M_RM8 if sample else M_RM16
            win_p, wout_p, dg_p = plan_mix[pi_]
            K.barrier()
            cv = Carver(); cv.off = base_off
            tg_ = f"m{pi_}"
            hTp = cv.bf16(KC * NTK).rearrange("p (k t) -> p k t", t=NTK)
            t_hTp = [[K.tile(f"hTp{tg_}_{kc}")] for kc in range(KC)]
            rmsnorm(C_GMIX, cv, ntg=1, src=lambda kc, g: xT[:, kc, tok0:tok0 + NTK], dst=lambda kc, g: hTp[:, kc, :],
                    t_src=[[xt_tiles(kc, tok0, NTK)[0]] for kc in range(KC)], t_dst=t_hTp, width=NTK, tag=tg_)
            hT_all = [t_hTp[kc][0] for kc in range(KC)]
            o_bT = cv.bf16(8 * NTK).rearrange("p (c t) -> p c t", t=NTK); t_obT = K.tile("obT" + tg_)
            if sample:
                S_[1] = cv.f32(1024)
            qs = cv.bf16(8 * NTK).rearrange("p (h t) -> p h t", t=NTK); t_qs = K.tile("qs" + tg_)
            o_aT = qs; t_oaT = t_qs
            sgz = cv.bf16(8 * NTK).rearrange("p (h t) -> p h t", t=NTK); t_sgz = K.tile("sgz" + tg_)
            mark = cv.off

            def proj_fm(w, tw, j, bank):
                for kc in range(KC):
                    K.op("pe", lambda e: e.matmul(P[bank][:, 0:NTK], w[:, kc, j * 128:(j + 1) * 128], hTp[:, kc, :], start=(kc == 0), stop=(kc == KC - 1)),
                         reads=[tw, t_hTp[kc][0]], writes=[tP[bank]], inc=(kc == KC - 1))

            def proj_tm(w, tw, tl, bank):
                for kc in range(KC):
                    K.op("pe", lambda e: e.matmul(P[bank][:, 0:512], hTp[:, kc, tl * 128:(tl + 1) * 128], w[:, kc, :], start=(kc == 0), stop=(kc == KC - 1)),
                         reads=[tw, t_hTp[kc][0]], writes=[tP[bank]], inc=(kc == KC - 1))

            zaT = cv.f32(4 * NTK).rearrange("p (c t) -> p c t", t=NTK); t_zaT = K.tile("zaT" + tg_)
            sgt = cv.f32(NTK); t_sgt = K.tile("sgt" + tg_)
            dw = cv.f32(8 * NTK).rearrange("p (c t) -> p c t", t=NTK); t_dw = [K.tile(f"dw{tg_}_{c}") for c in range(8)]
            t_u = [K.tile(f"u{tg_}_{c}") for c in range(8)]
            if not sample:
                ub = cv.bf16(8 * (32 + NTK)).rearrange("p (c t) -> p c t", t=32 + NTK)[:, :, 0:30 + NTK]
                utmp = [cv.f32(NTK), cv.f32(NTK)]; t_utmp = [K.tile("utmp0" + tg_), K.tile("utmp1" + tg_)]
                K.op("dve", lambda e: e.tensor_copy(ub[:, :, 0:30], halo), reads=[t_halo], writes=t_u)
            else:
                uT = cv.f32(8 * 16 * 38).rearrange("p (c j t) -> p c j t", j=16, t=38)
                mark_s = cv.off
                stg2_ = [cv.f32(1024), cv.f32(1024)]; t_stg2_ = [K.tile("cstg0"), K.tile("cstg1")]
                scv = sc_d.rearrange("j r c -> (j r) c")
                for q in range(4):
                    stg = stg2_[q % 2]; t_stg = t_stg2_[q % 2]
                    K.dma("sp", stg[0:120, :], scv[q * 120:(q + 1) * 120, :], writes=[t_stg])
                    for c in range(8):
                        bank = c % 2
                        K.op("pe", lambda e: e.transpose(P[bank][:, 0:120], stg[0:120, c * 128:(c + 1) * 128], ident[0:120, 0:120]),
                             reads=[t_stg, t_msk], writes=[tP[bank]])
                        K.op("act", lambda e: e.copy(uT[:, c, q * 4:(q + 1) * 4, 0:30], P[bank][:, 0:120].rearrange("p (j r) -> p j r", r=30)),
                             reads=[tP[bank]], writes=[t_u[c]])
                K.barrier()
                cv.off = mark_s
            dgcur = {}

            def conv_chunk(c):
                if not sample:
                    if c % 2 == 0:
                        dgcur["v"], dgcur["t"] = W.get(dg_p[c // 2])
                    dgv, tdg = dgcur["v"], dgcur["t"]
                    bank = 2 + c % 2
                    for k_ in range(31):
                        off = ((c % 2) * 31 + k_) * 128
                        K.op("pe", lambda e: e.matmul(P[bank][:, 0:NTK], dgv[:, off:off + 128], ub[:, c, k_:k_ + NTK], start=(k_ == 0), stop=(k_ == 30)),
                             reads=[tdg, t_u[c]], writes=[tP[bank]], inc=(k_ == 30))
                    K.op("act", lambda e: e.activation(dw[:, c, :], P[bank][:, 0:NTK], AF.Identity, bias=cst[:, C_CB + c:C_CB + c + 1]),
                         reads=[tP[bank], t_cst], writes=[t_dw[c]])
                    return

                def uwin(k):
                    return uT[:, c, k:k + NTK] if not sample else uT[:, c, :, k:k + 8]
                dwc = dw[:, c, :] if not sample else dw[:, c, :].rearrange("p (j t) -> p j t", t=8)
                wc = lambda k: cst[:, C_CW + c * 31 + k:C_CW + c * 31 + k + 1]
                K.op("dve", lambda e: e.tensor_scalar(dwc, uwin(0), wc(0), cst[:, C_CB + c:C_CB + c + 1], ALU.mult, ALU.add),
                     reads=[t_u[c], t_cst], writes=[t_dw[c]])
                for k in range(1, 31):
                    K.op("dve", lambda e: e.scalar_tensor_tensor(dwc, uwin(k), wc(k), dwc, ALU.mult, ALU.add),
                         reads=[t_u[c], t_cst, t_dw[c]], writes=[t_dw[c]])
            for half in range(2):
                wza, twza = W.get(win_p[2 * half])
                for j in range(4):
                    proj_fm(wza, twza, j, j % 2)
                    K.op("act", lambda e: e.copy(zaT[:, j, :], P[j % 2][:, 0:NTK]), reads=[tP[j % 2]], writes=[t_zaT])
                wzb, twzb = W.get(win_p[2 * half + 1])
                for j in range(4):
                    c = half * 4 + j
                    proj_fm(wzb, twzb, j, j % 2)
                    K.op("act", lambda e: e.activation(sgt, P[j % 2][:, 0:NTK], AF.Sigmoid), reads=[tP[j % 2]], writes=[t_sgt])
                    if not sample:
                        ut_ = utmp[c % 2]; tut_ = t_utmp[c % 2]
                        K.op("dve", lambda e: e.tensor_tensor(ut_, zaT[:, j, :], sgt, ALU.mult), reads=[t_zaT, t_sgt], writes=[tut_])
                        K.op("act", lambda e: e.copy(ub[:, c, 30:30 + NTK], ut_), reads=[tut_], writes=[t_u[c]])
                        K.op("dve", lambda e: e.tensor_copy(halo[:, c, :], ut_[:, NTK - 30:NTK]), reads=[tut_], writes=[t_halo])
                    else:
                        K.op("dve", lambda e: e.tensor_tensor(uT[:, c, :, 30:38], zaT[:, j, :].rearrange("p (j t) -> p j t", t=8),
                                                              sgt.rearrange("p (j t) -> p j t", t=8), ALU.mult),
                             reads=[t_zaT, t_sgt], writes=[t_u[c]])
                for c_ in range(half * 4, half * 4 + 4):
                    conv_chunk(c_)
            for half in range(2):
                w, tw = W.get(win_p[4 + half])
                for j in range(4):
                    proj_fm(w, tw, j, j % 2)
                    K.op("act", lambda e: e.activation(qs[:, half * 4 + j, :], P[j % 2][:, 0:NTK], AF.Silu), reads=[tP[j % 2]], writes=[t_qs])
            for half in range(2):
                w, tw = W.get(win_p[6 + half])
                for j in range(4):
                    proj_fm(w, tw, j, j % 2)
                    K.op("act", lambda e: e.activation(sgz[:, half * 4 + j, :], P[j % 2][:, 0:NTK], AF.Silu), reads=[tP[j % 2]], writes=[t_sgz])
            if not sample:
                if tok0 + NTK == 1024:
                    cso = cv.f32(1024); t_cso = K.tile("cso")
                    for c in range(8):
                        bank = c // 4
                        K.op("pe", lambda e: e.transpose(P[bank][0:30, (c % 4) * 128:(c % 4 + 1) * 128], halo[:, c, :], ident),
                             reads=[t_halo, t_msk], writes=[tP[bank]])
                    for bank in range(2):
                        K.op("act", lambda e: e.copy(cso[0:30, bank * 512:(bank + 1) * 512], P[bank][0:30, :]), reads=[tP[bank]], writes=[t_cso])
                    K.dma("sp", scp_d, cso[0:30, :], reads=[t_cso])
            else:
                cso = cv.f32(1024); t_cso = K.tile("csos")
                unew = cv.f32(1024).rearrange("p (c t) -> p c t", t=128); t_unew = K.tile("unew")
                K.op("dve", lambda e: e.tensor_copy(unew.rearrange("p c (j t) -> p c j t", t=8), uT[:, :, :, 30:38]), reads=t_u, writes=[t_unew])
                for c in range(8):
                    bank = c // 4
                    K.op("pe", lambda e: e.transpose(P[bank][:, (c % 4) * 128:(c % 4 + 1) * 128], unew[:, c, :], ident),
                         reads=[t_unew, t_msk], writes=[tP[bank]])
                for bank in range(2):
                    K.op("act", lambda e: e.copy(cso[:, bank * 512:(bank + 1) * 512], P[bank][:, :]), reads=[tP[bank]], writes=[t_cso])
                for j in range(16):
                    K.dma("sp", scs_d[j, 22:30, :], cso[8 * j:8 * j + 8, :], reads=[t_cso])
                t_cpy = K.tile("sccopy")
                K.dma("sp", scs_d[:, 0:22, :], sc_d[:, 8:30, :], writes=[t_cpy])
            sqd = cv.f32(NTK); t_sqd = K.tile("sqd" + tg_)
            mu = cv.f32(NTK); t_mu = K.tile("mu" + tg_)
            rs2 = cv.f32(NTK); t_rs2 = K.tile("rs2" + tg_)
            tt = cv.f32(NTK); t_tt = K.tile("tt" + tg_)
            for c in range(8):
                K.op("pe", lambda e: e.matmul(P[6][:, 0:NTK], onesf, dw[:, c, :], start=(c == 0), stop=(c == 7)),
                     reads=[t_msk, t_dw[c]], writes=[tP[6]])
                K.op("act", lambda e: e.activation(sqd, dw[:, c, :], AF.Square), reads=[t_dw[c]], writes=[t_sqd])
                K.op("pe", lambda e: e.matmul(P[7][:, 0:NTK], onesf, sqd, start=(c == 0), stop=(c == 7)),
                     reads=[t_msk, t_sqd], writes=[tP[7]])
            K.op("dve", lambda e: e.tensor_scalar(mu, P[6][:, 0:NTK], 1.0 / 1024, None, ALU.mult), reads=[tP[6]], writes=[t_mu])
            K.op("dve", lambda e: e.tensor_tensor(rs2, mu, mu, ALU.mult), reads=[t_mu], writes=[t_rs2])
            K.op("dve", lambda e: e.scalar_tensor_tensor(rs2, P[7][:, 0:NTK], 1.0 / 1024, rs2, ALU.mult, ALU.subtract), reads=[tP[7], t_rs2], writes=[t_rs2])
            K.op("act", lambda e: e.activation(rs2, rs2, AF.Sqrt, bias=EPS), reads=[t_rs2], writes=[t_rs2])
            K.op("dve", lambda e: e.reciprocal(rs2, rs2), reads=[t_rs2], writes=[t_rs2])
            for c in range(8):
                K.op("dve", lambda e: e.tensor_tensor(tt, dw[:, c, :], mu, ALU.subtract), reads=[t_dw[c], t_mu], writes=[t_tt])
                K.op("dve", lambda e: e.tensor_tensor(tt, tt, rs2, ALU.mult), reads=[t_tt, t_rs2], writes=[t_tt])
                K.op("act", lambda e: e.activation(o_bT[:, c, :], tt, AF.Silu, bias=cst[:, C_LB + c:C_LB + c + 1], scale=cst[:, C_LG + c:C_LG + c + 1]),
                     reads=[t_tt, t_cst], writes=[t_obT])

            K.barrier()
            cv.off = mark
            if sample:
                S_[2] = cv.f32(1024); S_[3] = cv.f32(1024)
            KtT = cv.bf16(8 * NTK).rearrange("p (h t) -> p h t", t=NTK); t_KtT = K.tile("KtT" + tg_)
            QtT = cv.bf16(8 * NTK).rearrange("p (h t) -> p h t", t=NTK); t_QtT = K.tile("QtT" + tg_)
            Kh = cv.bf16(ntile * 1024).rearrange("p (l c) -> p l c", c=1024); t_Kh = K.tile("Kh" + tg_)
            Vv = cv.bf16(ntile * 1024).rearrange("p (l c) -> p l c", c=1024); t_V = K.tile("V" + tg_)
            Aa = cv.f32(512); t_A = K.tile("A" + tg_)
            Bf = cv.f32(512); t_B = K.tile("B" + tg_)
            Cc = cv.f32(512); t_C = K.tile("C" + tg_)
            C2 = Cc; t_C2 = t_C

            dec = cv.f32(ntile * 8 * NB).rearrange("p (l c) -> p l c", c=8 * NB); t_dec = K.tile("dec" + tg_)
            kh_raw = cv.f32(1024); t_Khm = K.tile("Khm" + tg_)
            Khm = kh_raw.bitcast(BF16).rearrange("p (r c) -> p r c", c=1024)
            ATm_flat = cv.bf16(1024); ATm = ATm_flat.rearrange("p (h t) -> p h t", t=128); t_ATm = K.tile("ATm" + tg_); Ktm = ATm_flat[:, 0:512]; t_Ktm = t_ATm
            sbf_raw = cv.f32(512); Sbf = sbf_raw.bitcast(BF16); t_Sbf = [K.tile(f"Sbf{tg_}_{h}") for h in range(8)]
            osq = Sbf
            rsn = kh_raw; t_rsn = t_Khm
            P4b = P[4][:].bitcast(BF16)
            bufsets = [(Aa, Bf, Cc, Ktm, [t_A], [t_B], [t_C], [t_Ktm]),
                       (kh_raw[:, 0:512], kh_raw[:, 512:1024], sbf_raw, ATm_flat[:, 512:1024], [t_Khm], [t_Khm], t_Sbf, [t_ATm])]

            def zf_unit(ui, half, tl):
                par = ui % 2
                bA, bB, bC = (0, 2, 4) if par == 0 else (1, 3, 5)
                A_, B_, C_, Kt_, tA_, tB_, tC_, tK_ = bufsets[par]
                PAb = P[bA][:].bitcast(BF16)
                w, tw = W.get(win_p[8 + half], upto=win_p[8 + half] + 2)
                cols = slice(half * 512, half * 512 + 512)
                proj_tm(w, tw, tl, bA)
                yield
                K.op("act", lambda e: e.activation(A_, P[bA][:, 0:512], AF.Sigmoid), reads=[tP[bA]], writes=tA_)
                yield
                K.op("dve", lambda e: e.tensor_tensor(C_, A_, lbt[:, cols], ALU.mult), reads=tA_ + [t_lbt], writes=tC_)
                K.op("dve", lambda e: e.tensor_tensor(A_, A_, C_, ALU.subtract), reads=tA_ + tC_, writes=tA_)
                K.op("dve", lambda e: e.tensor_tensor(A_, A_, lbt[:, cols], ALU.add), reads=tA_ + [t_lbt], writes=tA_)
                yield
                K.op("act", lambda e: e.activation(B_, A_, AF.Ln), reads=tA_, writes=tB_)
                yield
                K.op("dve", lambda e: e.tensor_scalar(A_, A_, -1.0, 1.0, ALU.mult, ALU.add), reads=tA_, writes=tA_)
                K.op("pe", lambda e: e.matmul(P[bB][:, 0:512], TNm, B_, start=True, stop=True), reads=[t_msk] + tB_, writes=[tP[bB]])
                K.op("pe", lambda e: e.matmul(P[bC][:, 0:512], TRm, B_, start=True, stop=True), reads=[t_msk] + tB_, writes=[tP[bC]])
                yield
                K.op("act", lambda e: e.activation(C_, P[bB][:, 0:512], AF.Exp), reads=[tP[bB]], writes=tC_)
                yield
                K.op("dve", lambda e: e.tensor_tensor(Kt_, A_, C_, ALU.mult), reads=tA_ + tC_, writes=tK_)
                yield
                K.op("act", lambda e: e.activation(C_, P[bC][:, 0:512], AF.Exp), reads=[tP[bC]], writes=tC_)
                for j in range(4):
                    K.op("pe", lambda e: e.transpose(PAb[:, j * 128:(j + 1) * 128], Kt_[:, j * 128:(j + 1) * 128], identb[:]),
                         reads=tK_ + [t_identb], writes=[tP[bA]], inc=(j == 3))
                yield
                K.op("dve", lambda e: e.tensor_tensor(Kh[:, tl, cols], A_, C_, ALU.mult), reads=tA_ + tC_, writes=[t_Kh])
                K.op("act", lambda e: e.copy(KtT[:, half * 4:half * 4 + 4, tl * 128:(tl + 1) * 128], PAb[:, 0:512].rearrange("p (h t) -> p h t", t=128)),
                     reads=[tP[bA]], writes=[t_KtT])
                for j in range(4):
                    K.op("pe", lambda e: e.matmul(P[bB][:, j * 128:(j + 1) * 128], B_[:, j * 128:(j + 1) * 128], TPm, start=True, stop=True),
                         reads=tB_ + [t_msk], writes=[tP[bB]], inc=(j == 3))
                yield
                K.op("act", lambda e: e.activation(C_, P[bB][:, 0:512], AF.Exp), reads=[tP[bB]], writes=tC_)
                for j in range(4):
                    h = half * 4 + j
                    K.op("pe", lambda e: e.matmul(P[7][:, tl * 8 * NB + h * NB: tl * 8 * NB + (h + 1) * NB], B_[:, j * 128:(j + 1) * 128], BIm, start=True, stop=True),
                         reads=tB_ + [t_msk], writes=[tP[7]], inc=(j == 3))
                yield
                K.op("dve", lambda e: e.tensor_tensor(QtT[:, half * 4:half * 4 + 4, tl * 128:(tl + 1) * 128], qs[:, half * 4:half * 4 + 4, tl * 128:(tl + 1) * 128],
                                                      C_.rearrange("p (h t) -> p h t", t=128), ALU.mult), reads=[t_qs] + tC_, writes=[t_QtT])
                yield

            run_interleaved([zf_unit(half * ntile + tl, half, tl) for half in range(2) for tl in range(ntile)])
            K.op("act", lambda e: e.activation(dec.rearrange("p l c -> p (l c)"), P[7][:, 0:ntile * 8 * NB], AF.Exp), reads=[tP[7]], writes=[t_dec])
            for half in range(2):
                w, tw = W.get(win_p[10 + half])
                for tl in range(ntile):
                    bank = (half * ntile + tl) % 2
                    proj_tm(w, tw, tl, bank)
                    K.op("act", lambda e: e.copy(Vv[:, tl, half * 512:half * 512 + 512], P[bank][:, 0:512]), reads=[tP[bank]], writes=[t_V])
            for tl in range(ntile):
                tsl = slice(tl * 128, (tl + 1) * 128)
                for h in range(8):
                    K.op("pe", lambda e: e.matmul(P[h // 4][:, (h % 4) * 128:(h % 4 + 1) * 128], KtT[:, h, tsl], QtT[:, h, tsl], start=True, stop=True),
                         reads=[t_KtT, t_QtT], writes=[tP[h // 4]], inc=(h % 4 == 3))
                for h in range(8):
                    K.op("dve", lambda e: e.tensor_tensor(ATm[:, h, :], P[h // 4][:, (h % 4) * 128:(h % 4 + 1) * 128], TPm, ALU.mult),
                         reads=[tP[h // 4], t_msk], writes=[t_ATm])
                for h in range(8):
                    K.op("pe", lambda e: e.matmul(P[2 + h // 4][:, (h % 4) * 128:(h % 4 + 1) * 128], Vv[:, tl, h * 128:(h + 1) * 128], ATm[:, h, :], start=(h % 4 == 0), stop=True),
                         reads=[t_V, t_ATm], writes=[tP[2 + h // 4]], inc=(h % 4 == 3))
                def s_load(b_):
                    K.dma("sp", S_[b_ % NSB].rearrange("p (h d) -> p h d", d=128), sh_d[b_].rearrange("h e d -> e h d"), writes=t_S[b_ % NSB])
                if sample:
                    for b_ in range(NSB):
                        s_load(b_)
                for blk in range(NB):
                    kb_ = blk % 2
                    K.op("dve", lambda e: e.tensor_scalar(Khm[:, kb_, :], Kh[:, tl, :], msk[:, RM0 + blk % G:RM0 + blk % G + 1], None, ALU.mult),
                         reads=[t_Kh, t_msk], writes=[t_Khm])
                    ub = 4 + 2 * (blk % 2)
                    si = blk % NSB if sample else 0
                    Sc = S_[si]; tS = t_S[si]
                    a, r = divmod(blk, G)
                    for h in range(8):
                        K.op("act", lambda e: e.copy(Sbf[:, h * 128:(h + 1) * 128], Sc[:, h * 128:(h + 1) * 128]), reads=[tS[h]], writes=[t_Sbf[h]])
                    for h in range(8):
                        c0 = (h % 4) * 128 + blk * B
                        K.op("pe", lambda e: e.matmul(P[2 + h // 4][:, c0:c0 + B], Sbf[:, h * 128:(h + 1) * 128], QtT[:, h, tl * 128 + blk * B: tl * 128 + (blk + 1) * B],
                                                      start=False, stop=True), reads=[t_Sbf[h], t_QtT], writes=[tP[2 + h // 4]], inc=(h % 4 == 3))
                    for h in range(8):
                        K.op("pe", lambda e: e.matmul(P[ub + h // 4][:, (h % 4) * 128:(h % 4 + 1) * 128], Khm[64 * a:64 * a + 64, kb_, h * 128:(h + 1) * 128],
                                                      Vv[64 * a:64 * a + 64, tl, h * 128:(h + 1) * 128], start=True, stop=True),
                             reads=[t_Khm, t_V], writes=[tP[ub + h // 4]], inc=(h % 4 == 3))
                    for h in range(8):
                        dcol = dec[:, tl, h * NB + blk:h * NB + blk + 1]
                        K.op("dve", lambda e: e.scalar_tensor_tensor(Sc[:, h * 128:(h + 1) * 128], Sc[:, h * 128:(h + 1) * 128], dcol,
                                                                     P[ub + h // 4][:, (h % 4) * 128:(h % 4 + 1) * 128], ALU.mult, ALU.add),
                             reads=[tS[h], t_dec, tP[ub + h // 4]], writes=[tS[h]])
                    if sample:
                        K.dma("sp", shs_d[blk].rearrange("h e d -> e h d"), Sc.rearrange("p (h d) -> p h d", d=128), reads=tS)
                        if blk + NSB < NB:
                            s_load(blk + NSB)
                for hb in range(2):
                    K.op("act", lambda e: e.activation(osq[:, hb * 512:(hb + 1) * 512], P[2 + hb][:, :], AF.Square), reads=[tP[2 + hb]], writes=t_Sbf[hb * 4:hb * 4 + 4])
                for h in range(8):
                    K.op("pe", lambda e: e.matmul(P[6 + h // 4][:, (h % 4) * 128:(h % 4 + 1) * 128], onesb[:], osq[:, h * 128:(h + 1) * 128], start=True, stop=True),
                         reads=[t_onesb, t_Sbf[h]], writes=[tP[6 + h // 4]], inc=(h % 4 == 3))
                for hb in range(2):
                    K.op("dve", lambda e: e.tensor_scalar(rsn[:, hb * 512:(hb + 1) * 512], P[6 + hb][:, :], 1.0 / 128, EPS, ALU.mult, ALU.add), reads=[tP[6 + hb]], writes=[t_rsn])
                K.op("act", lambda e: e.activation(rsn, rsn, AF.Sqrt), reads=[t_rsn], writes=[t_rsn])
                K.op("dve", lambda e: e.reciprocal(rsn, rsn), reads=[t_rsn], writes=[t_rsn])
                for hb in range(2):
                    K.op("dve", lambda e: e.scalar_tensor_tensor(rsn[:, hb * 512:(hb + 1) * 512], P[2 + hb][:, :], cst[:, C_GN:C_GN + 1], rsn[:, hb * 512:(hb + 1) * 512], ALU.mult, ALU.mult),
                         reads=[tP[2 + hb], t_cst, t_rsn], writes=[t_rsn])
                    K.op("dve", lambda e: e.tensor_tensor(o_aT[:, hb * 4:hb * 4 + 4, tsl], rsn[:, hb * 512:(hb + 1) * 512].rearrange("p (h t) -> p h t", t=128),
                                                          sgz[:, hb * 4:hb * 4 + 4, tsl], ALU.mult), reads=[t_rsn, t_sgz], writes=[t_oaT])
            if (not sample) and tok0 + NTK == 1024:
                K.dma("sp", shp_d.rearrange("h e d -> e h d"), S_[0].rearrange("p (h d) -> p h d", d=128), reads=t_S[0])
            if dbg_d is not None and stop_after == "mixer" and pi_ == 0:
                dtmp = cv.f32(1024); t_dtmp = K.tile("dtmpm")
                K.op("dve", lambda e: e.tensor_copy(dtmp, o_aT.rearrange("p c t -> p (c t)")), reads=[t_oaT], writes=[t_dtmp])
                dump(dtmp, [t_dtmp], 1024)
                K.op("dve", lambda e: e.tensor_copy(dtmp, o_bT.rearrange("p c t -> p (c t)")), reads=[t_obT], writes=[t_dtmp])
                dump(dtmp, [t_dtmp], 1024)
            for pp in range(4):
                w, tw = W.get(wout_p[pp])
                for j in range(4):
                    m = pp * 4 + j
                    bank = m % 2
                    for k in range(KC):
                        rhs = o_aT[:, k, :] if k < 8 else o_bT[:, k - 8, :]
                        K.op("pe", lambda e: e.matmul(P[bank][:, 0:NTK], w[:, k, j * 128:(j + 1) * 128], rhs, start=(k == 0), stop=(k == KC - 1)),
                             reads=[tw, t_oaT if k < 8 else t_obT], writes=[tP[bank]], inc=(k == KC - 1))
                    xv = xT[:, m, tok0:tok0 + NTK]
                    K.op("dve", lambda e: e.tensor_tensor(xv, xv, P[bank][:, 0:NTK], ALU.add), reads=[tP[bank]] + xt_tiles(m, tok0, NTK), writes=xt_tiles(m, tok0, NTK))

    if not os.environ.get("SKIP_PRE"):
        load_x(xpre_d, "p")
        ffn(plan_f1_pre, C_GF1, "p", ntg=2, tgw=512)
        mixer_pre()
    load_x(x_d, "m")
    if enabled("ffn1") and not os.environ.get("SKIP_FFN1"):
        ffn(plan_f1, C_GF1, "a")

    if dbg_d is not None and stop_after in ("load", "ffn1"):
        for g in range(NTG):
            dump(xT[:, 0, g * TG:(g + 1) * TG], [t_xT[0][g]], TG)
            dump(xT[:, 15, g * TG:(g + 1) * TG], [t_xT[15][g]], TG)

    if enabled("mixer") and not os.environ.get("SKIP_MIX"):
        mixer()
    if dbg_d is not None and stop_after == "mixer":
        for g in range(NTG):
            dump(xT[:, 0, g * TG:(g + 1) * TG], [t_xT[0][g]], TG)
            dump(xT[:, 15, g * TG:(g + 1) * TG], [t_xT[15][g]], TG)

    def xattn():
        K.barrier()
        cvx = Carver()
        mkT = cvx.bf16(4 * 256).rearrange("p (h m) -> p h m", m=256); t_mkT = K.tile("mkT")
        mvb = cvx.bf16(2 * 512).rearrange("p (c d) -> p c d", d=512); t_mvb = K.tile("mvb")
        qT = cvx.bf16(4 * NT).rearrange("p (h t) -> p h t", t=NT); t_qT = [K.tile(f"qT{g}") for g in range(NTG)]
        oxT = cvx.bf16(4 * NT).rearrange("p (h t) -> p h t", t=NT); t_oxT = [K.tile(f"oxT{g}") for g in range(NTG)]
        base = cvx.off
        wk_i, wv_i, wq_i, wo_i = plan_xa
        mst0 = cvx.f32(2048); mst = [mst0, mst0]; t_mst0 = K.tile("mst0"); t_mst = [t_mst0, t_mst0]
        memT = cvx.f32(KC * 256).rearrange("p (k t) -> p k t", t=256); t_memT = [[K.tile(f"memT{kc}")] for kc in range(KC)]
        mnT = cvx.bf16(KC * 256).rearrange("p (k t) -> p k t", t=256); t_mnT = [[K.tile(f"mnT{kc}")] for kc in range(KC)]
        ostg = [cvx.f32(512) for _ in range(2)]; t_ostg = [K.tile(f"ostg{i}") for i in range(2)]
        for t in range(2):
            K.dma("sp", mst[t], mem_d[t * 128:(t + 1) * 128, :], writes=[t_mst[t]])
            for q in range(4):
                bank = q
                for j in range(4):
                    kc = q * 4 + j
                    K.op("pe", lambda e: e.transpose(P[bank][:, j * 128:(j + 1) * 128], mst[t][:, kc * 128:(kc + 1) * 128], ident),
                         reads=[t_mst[t], t_msk], writes=[tP[bank]], inc=(j == 3))
                K.op("act", lambda e: e.copy(memT[:, q * 4:(q + 1) * 4, t * 128:(t + 1) * 128], P[bank][:].rearrange("p (a b) -> p a b", b=128)),
                     reads=[tP[bank]], writes=[t_memT[q * 4 + j][0] for j in range(4)])
        if os.environ.get('XA_STOP') == 'a1':
            return
        rmsnorm(C_GMEM, cvx, ntg=1, src=lambda kc, g: memT[:, kc, :], dst=lambda kc, g: mnT[:, kc, :], t_src=t_memT, t_dst=t_mnT, width=256, tag="mem")
        if os.environ.get('XA_STOP') == 'a2':
            return
        wkp, twk = W.get(wk_i)
        for h in range(4):
            for kc in range(KC):
                K.op("pe", lambda e: e.matmul(P[h % 2][:, 0:256], wkp[:, kc, h * 128:(h + 1) * 128], mnT[:, kc, :], start=(kc == 0), stop=(kc == KC - 1)),
                     reads=[twk, t_mnT[kc][0]], writes=[tP[h % 2]], inc=(kc == KC - 1))
            K.op("act", lambda e: e.copy(mkT[:, h, :], P[h % 2][:, 0:256]), reads=[tP[h % 2]], writes=[t_mkT])
        if os.environ.get('XA_STOP') == 'a3':
            return
        for t in range(2):
            for kc in range(KC):
                K.op("pe", lambda e: e.matmul(P[2 + t][:, 0:512], mnT[:, kc, t * 128:(t + 1) * 128], wkp[:, kc, :], start=(kc == 0), stop=(kc == KC - 1)),
                     reads=[twk, t_mnT[kc][0]], writes=[tP[2 + t]], inc=(kc == KC - 1))
            K.op("act", lambda e: e.copy(ostg[t], P[2 + t][:, 0:512]), reads=[tP[2 + t]], writes=[t_ostg[t]])
            K.dma("sp", mk_d[t * 128:(t + 1) * 128, :], ostg[t], reads=[t_ostg[t]])
        if os.environ.get('XA_STOP') == 'a4':
            return
        wvp, twv = W.get(wv_i)
        for t in range(2):
            for kc in range(KC):
                K.op("pe", lambda e: e.matmul(P[4 + t][:, 0:512], mnT[:, kc, t * 128:(t + 1) * 128], wvp[:, kc, :], start=(kc == 0), stop=(kc == KC - 1)),
                     reads=[twv, t_mnT[kc][0]], writes=[tP[4 + t]], inc=(kc == KC - 1))
            K.op("act", lambda e: e.copy(ostg[t], P[4 + t][:, 0:512]), reads=[tP[4 + t]], writes=[t_ostg[t]])
            K.op("dve", lambda e: e.tensor_copy(mvb[:, t, :], ostg[t]), reads=[t_ostg[t]], writes=[t_mvb])
            K.dma("sp", mv_d[t * 128:(t + 1) * 128, :], ostg[t], reads=[t_ostg[t]])
        if os.environ.get('XA_STOP') == 'a':
            return
        K.barrier()
        cvx.off = base
        hT = cvx.bf16(KC * NT).rearrange("p (k t) -> p k t", t=NT)
        t_hT = [[K.tile(f"hTx{k}_{g}") for g in range(NTG)] for k in range(KC)]
        hs = lambda k, g: hT[:, k, g * TG:(g + 1) * TG]
        rmsnorm(C_GXA, cvx, dst=hs, t_dst=t_hT, tag="xa")
        wqp, twq = W.get(wq_i)
        for h in range(4):
            bs = 3 * (h % 2)
            for kc in range(KC):
                for g in range(NTG):
                    K.op("pe", lambda e: e.matmul(P[bs + g][:, 0:TG], wqp[:, kc, h * 128:(h + 1) * 128], hs(kc, g), start=(kc == 0), stop=(kc == KC - 1)),
                         reads=[twq, t_hT[kc][g]], writes=[tP[bs + g]], inc=(kc == KC - 1))
            for g in range(NTG):
                K.op("act", lambda e: e.mul(qT[:, h, g * TG:(g + 1) * TG], P[bs + g][:, 0:TG], 128.0 ** -0.5), reads=[tP[bs + g]], writes=[t_qT[g]])
        if os.environ.get('XA_STOP') == 'b':
            return
        K.barrier()
        cvx.off = base
        xsets = []
        for i_ in range(2):
            xsets.append((cvx.f32(1024).rearrange("p (h m) -> p h m", m=256), cvx.bf16(1024), cvx.bf16(1024).rearrange("p (a t) -> p a t", t=128),
                          cvx.f32(4), cvx.f32(4), K.tile(f"pf{i_}"), K.tile(f"pn{i_}"), K.tile(f"pT{i_}"), K.tile(f"mx{i_}"), K.tile(f"rsum{i_}")))
        kst2 = [cvx.f32(1024).rearrange("p (c d) -> p c d", d=512) for _ in range(4)]; t_kst2 = [K.tile(f"kst{i}") for i in range(4)]
        KjT = cvx.bf16(1024).rearrange("p (h m) -> p h m", m=256); t_KjT = K.tile("KjT")
        qm = [cvx.bf16(512).rearrange("p (h t) -> p h t", t=128) for _ in range(2)]; t_qm = [K.tile(f"qm{i}") for i in range(2)]
        Vjb = cvx.bf16(1024).rearrange("p (c d) -> p c d", d=512); t_Vjb = K.tile("Vjb")
        def xa_tile(t):
            sample = (t == 8) and not os.environ.get('XA_NOSAMPLE')
            par = 0 if t == 8 else t % 2
            pf, pn, pT, mx, rsum, t_pf, t_pn, t_pT, t_mx, t_rsum = xsets[par]
            bs0, btr, bpv = (0, 4, 5) if par == 0 else (2, 6, 7)
            P4b = P[btr][:].bitcast(BF16)
            g = (t * 128) // TG
            tsl = slice(t * 128, (t + 1) * 128)
            if not sample:
                for h in range(4):
                    K.op("pe", lambda e: e.matmul(P[bs0 + h // 2][:, (h % 2) * 256:(h % 2 + 1) * 256], qT[:, h, tsl], mkT[:, h, :], start=True, stop=True),
                         reads=[t_qT[g], t_mkT], writes=[tP[bs0 + h // 2]], inc=(h % 2 == 1))
            else:
                for j in range(16):
                    kst = kst2[j % 4]; t_kst = t_kst2[j % 4]
                    K.dma("sp", kst, ck_d[j].rearrange("(c p) d -> p c d", p=128), writes=[t_kst])
                    for h in range(4):
                        for mc in range(2):
                            K.op("pe", lambda e: e.transpose(P[2 + h // 2][:, ((h % 2) * 2 + mc) * 128:((h % 2) * 2 + mc + 1) * 128], kst[:, mc, h * 128:(h + 1) * 128], ident),
                                 reads=[t_kst, t_msk], writes=[tP[2 + h // 2]])
                    for hb in range(2):
                        K.op("act", lambda e: e.copy(KjT[:, hb * 2:hb * 2 + 2, :], P[2 + hb][:].rearrange("p (h m) -> p h m", m=256)), reads=[tP[2 + hb]], writes=[t_KjT])
                    qb = j % 2
                    K.op("dve", lambda e: e.memset(qm[qb], 0.0), writes=[t_qm[qb]])
                    K.op("dve", lambda e: e.tensor_copy(qm[qb][:, :, 8 * j:8 * j + 8], qT[:, :, 1024 + 8 * j:1024 + 8 * j + 8]), reads=[t_qT[2]], writes=[t_qm[qb]])
                    for h in range(4):
                        K.op("pe", lambda e: e.matmul(P[h // 2][:, (h % 2) * 256:(h % 2 + 1) * 256], qm[qb][:, h, :], KjT[:, h, :], start=(j == 0 and h % 2 == 0), stop=(j == 15)),
                             reads=[t_qm[qb], t_KjT], writes=[tP[h // 2]])
            yield
            for hb in range(2):
                K.op("dve", lambda e: e.reduce_max(mx[:, hb * 2:hb * 2 + 2], P[bs0 + hb][:].rearrange("p (h m) -> p h m", m=256), AX.X), reads=[tP[bs0 + hb]], writes=[t_mx])
            K.op("dve", lambda e: e.tensor_scalar(mx, mx, -1.0, None, ALU.mult), reads=[t_mx], writes=[t_mx])
            yield
            for h in range(4):
                K.op("act", lambda e: e.activation(pf[:, h, :], P[bs0 + h // 2][:, (h % 2) * 256:(h % 2 + 1) * 256], AF.Exp, bias=mx[:, h:h + 1], scale=1.0, accum_out=rsum[:, h:h + 1]),
                     reads=[tP[bs0 + h // 2], t_mx], writes=[t_pf, t_rsum])
            yield
            K.op("dve", lambda e: e.reciprocal(rsum, rsum), reads=[t_rsum], writes=[t_rsum])
            for h in range(4):
                K.op("dve", lambda e: e.tensor_scalar(pn[:, h * 256:(h + 1) * 256], pf[:, h, :], rsum[:, h:h + 1], None, ALU.mult), reads=[t_pf, t_rsum], writes=[t_pn])
            yield
            for a in range(8):
                K.op("pe", lambda e: e.transpose(P4b[:, a * 128:(a + 1) * 128], pn[:, a * 128:(a + 1) * 128], identb[:]), reads=[t_pn, t_identb], writes=[tP[btr]], inc=(a == 7))
            K.op("act", lambda e: e.copy(pT, P4b[:, 0:1024].rearrange("p (a t) -> p a t", t=128)), reads=[tP[btr]], writes=[t_pT])
            yield
            if not sample:
                for h in range(4):
                    for mc in range(2):
                        K.op("pe", lambda e: e.matmul(P[bpv][:, h * 128:(h + 1) * 128], mvb[:, mc, h * 128:(h + 1) * 128], pT[:, h * 2 + mc, :], start=(h == 0 and mc == 0), stop=True),
                             reads=[t_mvb, t_pT], writes=[tP[bpv]], inc=(h == 3 and mc == 1))
            else:
                for j in range(16):
                    kst = kst2[j % 4]; t_kst = t_kst2[j % 4]
                    K.dma("sp", kst, cv_d[j].rearrange("(c p) d -> p c d", p=128), writes=[t_kst])
                    K.op("dve", lambda e: e.tensor_copy(Vjb, kst), reads=[t_kst], writes=[t_Vjb])
                    for h in range(4):
                        for mc in range(2):
                            K.op("pe", lambda e: e.matmul(P[5][:, h * 128 + 8 * j:h * 128 + 8 * j + 8], Vjb[:, mc, h * 128:(h + 1) * 128], pT[:, h * 2 + mc, 8 * j:8 * j + 8],
                                                          start=(j == 0 and h == 0 and mc == 0), stop=True), reads=[t_Vjb, t_pT], writes=[tP[5]])
            K.op("act", lambda e: e.copy(oxT[:, :, tsl], P[bpv][:].rearrange("p (h t) -> p h t", t=128)), reads=[tP[bpv]], writes=[t_oxT[g]])
            yield

        run_interleaved([xa_tile(t) for t in range(8)])
        for _ in xa_tile(8):
            pass
        if os.environ.get('XA_STOP') == 'c':
            return
        wop, two = W.get(wo_i)
        for m in range(KC):
            bs = 3 * (m % 2)
            for k in range(4):
                for g in range(NTG):
                    K.op("pe", lambda e: e.matmul(P[bs + g][:, 0:TG], wop[:, k, m * 128:(m + 1) * 128], oxT[:, k, g * TG:(g + 1) * TG], start=(k == 0), stop=(k == 3)),
                         reads=[two, t_oxT[g]], writes=[tP[bs + g]], inc=(k == 3))
            for g in range(NTG):
                K.op("dve", lambda e: e.tensor_tensor(xs(m, g), xs(m, g), P[bs + g][:, 0:TG], ALU.add), reads=[tP[bs + g], t_xT[m][g]], writes=[t_xT[m][g]])

    if enabled("xattn"):
        xattn()
    if dbg_d is not None and stop_after == "xattn":
        for g in range(NTG):
            dump(xT[:, 0, g * TG:(g + 1) * TG], [t_xT[0][g]], TG)
            dump(xT[:, 15, g * TG:(g + 1) * TG], [t_xT[15][g]], TG)

    if enabled("ffn2"):
        ffn(plan_f2, C_GF2, "b")
    if enabled("final"):
        K.barrier()
        cvz = Carver()
        rmsnorm(C_GFIN, cvz, dst=xs, t_dst=t_xT, tag="fin")
        ystg = [cvz.f32(2048) for _ in range(2)]
        t_ystg = [K.tile(f"ystg{i}") for i in range(2)]
        for t in range(9):
            s = t % 2
            g = (t * 128) // TG
            for q in range(4):
                bank = q
                for j in range(4):
                    kc = q * 4 + j
                    K.op("pe", lambda e: e.transpose(P[bank][:, j * 128:(j + 1) * 128], xT[:, kc, t * 128:(t + 1) * 128], ident),
                         reads=[t_xT[kc][g], t_msk], writes=[tP[bank]], inc=(j == 3))
                if q % 2 == 0:
                    K.op("act", lambda e: e.copy(ystg[s][:, q * 512:(q + 1) * 512], P[bank][:]), reads=[tP[bank]], writes=[t_ystg[s]])
                else:
                    K.op("dve", lambda e: e.tensor_copy(ystg[s][:, q * 512:(q + 1) * 512], P[bank][:]), reads=[tP[bank]], writes=[t_ystg[s]])
            K.dma("sp", y_d[t * 128:(t + 1) * 128, :], ystg[s], reads=[t_ystg[s]])

    K.finish()
    lp.__exit__(None, None, None)
    K.close()
    print("instructions", K.n_ins, "waits", K.n_wait, "panels", len(W.plan))
    return nc


def make_masks():
    m = np.zeros((128, NMSK), np.float32)
    idx = np.arange(128)
    m[:, M_ID:M_ID + 128] = np.eye(128, dtype=np.float32)
    for B, otp, otr, otn in ((16, M_TP16, M_TR16, M_TN16), (8, M_TP8, M_TR8, M_TN8)):
        same = (idx[:, None] // B) == (idx[None, :] // B)
        tp = (same & (idx[:, None] <= idx[None, :])).astype(np.float32)
        tr = (same & (idx[:, None] > idx[None, :])).astype(np.float32)
        m[:, otp:otp + 128] = tp
        m[:, otr:otr + 128] = tr
        m[:, otn:otn + 128] = -tp
    m[:, M_BI16:M_BI16 + 8] = (idx[:, None] // 16 == np.arange(8)[None, :])
    m[:, M_BI8:M_BI8 + 16] = (idx[:, None] // 8 == np.arange(16)[None, :])
    m[:, M_ONES:M_ONES + 128] = 1.0
    m[:, M_RM16:M_RM16 + 4] = ((idx[:, None] % 64) // 16 == np.arange(4)[None, :])
    m[:, M_RM8:M_RM8 + 8] = ((idx[:, None] % 64) // 8 == np.arange(8)[None, :])
    return m


def fm(v):
    return np.ascontiguousarray(v.reshape(16, 128).T)


def prep_core(inp, c, masks):
    seq, half = c // 2, c % 2
    d = {}
    xp = inp["x_prompt"][seq, half * 1024:(half + 1) * 1024]
    xsm = inp["x_sample"][c * 16:(c + 1) * 16].reshape(128, D)
    d["x"] = np.ascontiguousarray(np.concatenate([xp, xsm], 0))
    d["mem"] = np.ascontiguousarray(inp["mem_prompt"][seq])
    d["xpre"] = np.ascontiguousarray(np.concatenate([inp["x_prompt"][seq, 0:1024], np.zeros((128, D), np.float32)], 0))
    d["sh"] = np.ascontiguousarray(inp["state_hgrn"][0, c * 16:(c + 1) * 16])
    d["sc"] = np.ascontiguousarray(inp["state_conv"][0, c * 16:(c + 1) * 16])
    d["ck"] = np.ascontiguousarray(inp["cache_mem_k"][0, c * 16:(c + 1) * 16].reshape(16, 256, 512))
    d["cv"] = np.ascontiguousarray(inp["cache_mem_v"][0, c * 16:(c + 1) * 16].reshape(16, 256, 512))
    for k, n in (("f1g", "ffn1_w_gate"), ("f1u", "ffn1_w_up"), ("f1d", "ffn1_w_down"), ("win", "w_in"), ("wout", "w_out"),
                 ("wq", "xattn_wq"), ("wk", "xattn_wk"), ("wv", "xattn_wv"), ("wo", "xattn_wo"),
                 ("f2g", "ffn2_w_gate"), ("f2u", "ffn2_w_up"), ("f2d", "ffn2_w_down")):
        d[k] = inp[n][0]
    cst = np.zeros((128, NCST), np.float32)
    cst[:, C_GF1:C_GF1 + 16] = fm(inp["norm_ffn1"][0])
    cst[:, C_GMIX:C_GMIX + 16] = fm(inp["norm_mix"][0])
    cst[:, C_GXA:C_GXA + 16] = fm(inp["norm_xattn"][0])
    cst[:, C_GMEM:C_GMEM + 16] = fm(inp["norm_mem"][0])
    cst[:, C_GF2:C_GF2 + 16] = fm(inp["norm_ffn2"][0])
    cst[:, C_GFIN:C_GFIN + 16] = fm(inp["norm_final"])
    cst[:, C_GN] = inp["hgrn_gnorm"][0]
    cw = inp["conv_w"][0]
    cst[:, C_CW:C_CW + 248] = cw.reshape(31, 8, 128).transpose(2, 1, 0).reshape(128, 248)
    cst[:, C_CB:C_CB + 8] = inp["conv_b"][0].reshape(8, 128).T
    cst[:, C_LG:C_LG + 8] = inp["conv_ln_g"][0].reshape(8, 128).T
    cst[:, C_LB:C_LB + 8] = inp["conv_ln_b"][0].reshape(8, 128).T
    cst[:, C_FLAG] = float(half)
    d["cst"] = cst
    d["msk"] = masks
    d["lb"] = np.ascontiguousarray(inp["hgrn_lb"])
    return d


from concourse.bass_utils import run_bass_kernel_spmd

_NC = None


def kernel(**inp):
    global _NC
    inp = {k: np.asarray(v) for k, v in inp.items()}
    masks = make_masks()
    in_maps = [prep_core(inp, c, masks) for c in range(8)]
    nc = build(stop_after="all")
    res = run_bass_kernel_spmd(nc, in_maps, core_ids=list(range(8)))
    R_ = res.results
    y_prompt = np.zeros((4, 2048, 2048), np.float32)
    y_sample = np.zeros((128, 8, 2048), np.float32)
    shp = np.zeros((1, 4, 8, 128, 128), np.float32)
    scp = np.zeros((1, 4, 30, 1024), np.float32)
    mk = np.zeros((1, 4, 256, 4, 128), np.float32)
    mv = np.zeros((1, 4, 256, 4, 128), np.float32)
    shs = np.zeros((1, 128, 8, 128, 128), np.float32)
    scs = np.zeros((1, 128, 30, 1024), np.float32)
    for c in range(8):
        r = R_[c]
        seq, half = c // 2, c % 2
        y_prompt[seq, half * 1024:(half + 1) * 1024] = r["y"][:1024]
        y_sample[c * 16:(c + 1) * 16] = r["y"][1024:].reshape(16, 8, 2048)
        shs[0, c * 16:(c + 1) * 16] = r["shs"]
        scs[0, c * 16:(c + 1) * 16] = r["scs"]
        if half == 1:
            shp[0, seq] = r["shp"]
            scp[0, seq] = r["scp"]
        else:
            mk[0, seq] = r["mko"].reshape(256, 4, 128)
            mv[0, seq] = r["mvo"].reshape(256, 4, 128)
    return (y_prompt, y_sample, shp, scp, mk, mv, shs, scs)
```
